# Optimizing a Trainium2 kernel written in Bass

```python
import math
import jax
import jax.numpy as jnp
from jax import lax
import numpy as np


D_MODEL = 1024
BATCH = 8
SEQ = 2048
DEPTH = 4

GRID_W = 64
CTX_LEN = 256
N_MIXERS = 4
EPS = 1e-6
NEG_INF = -1e30
ROPE_BASE = 10000.0
HEAD_DIM = 64

LRU_WIDTH = 1408
LRU_BLOCKS = 16
LRU_BW = LRU_WIDTH // LRU_BLOCKS
CONV_W = 4
LRU_C = 8.0

SWA_HEADS = 16
SWA_KV_HEADS = 4
WINDOW = 128
BLOCK_Q = 128

NA_HEADS = 16
NA_ROWS = 8
NA_COLS = 16

S5_WIDTH = 1024
S5_GROUP = 16
S5_GROUPS = S5_WIDTH // S5_GROUP
S5_STATE = 64

kernel_name = 'hybrid_interleaved_dit_trunk'


def rmsnorm(x, g):
    xf = x.astype(jnp.float32)
    y = xf * lax.rsqrt(jnp.mean(xf * xf, axis=-1, keepdims=True) + EPS)
    return (y * g.astype(jnp.float32)).astype(x.dtype)


def modulation(vec, w, b):
    m = jax.nn.silu(vec) @ w + b
    shift, scale, gate = jnp.split(m, 3, axis=-1)
    return shift[:, None, :], scale[:, None, :], gate[:, None, :]


def axial_rope(t_len):
    pos = jnp.arange(t_len)
    row = (pos // GRID_W).astype(jnp.float32)
    col = (pos % GRID_W).astype(jnp.float32)
    n_ax = HEAD_DIM // 4
    freqs = ROPE_BASE ** (-jnp.arange(n_ax, dtype=jnp.float32) / n_ax)
    ang = jnp.concatenate([row[:, None] * freqs, col[:, None] * freqs], axis=-1)
    return jnp.cos(ang), jnp.sin(ang)


def apply_rope(x, cos, sin):
    shape = (1, x.shape[1]) + (1,) * (x.ndim - 3) + (cos.shape[-1],)
    cos = cos.reshape(shape)
    sin = sin.reshape(shape)
    x1, x2 = jnp.split(x.astype(jnp.float32), 2, axis=-1)
    return jnp.concatenate([x1 * cos - x2 * sin, x2 * cos + x1 * sin], axis=-1).astype(x.dtype)


def sink_softmax(s, sink):
    s = s.astype(jnp.float32)
    sk = jnp.broadcast_to(sink.astype(jnp.float32), s.shape[:-1] + (1,))
    return jax.nn.softmax(jnp.concatenate([sk, s], axis=-1), axis=-1)[..., 1:]


def linear_scan(a, b, h0=None):
    def combine(l, r):
        return l[0] * r[0], r[0] * l[1] + r[1]
    a_cum, h = lax.associative_scan(combine, (a, b), axis=1)
    if h0 is not None:
        h = h + a_cum * h0[:, None]
    return h


def scan_with_prefix(a_c, b_c, a_l, b_l, reverse):
    if reverse:
        a_c, b_c, a_l, b_l = [jnp.flip(t, axis=1) for t in (a_c, b_c, a_l, b_l)]
    h_c = linear_scan(a_c, b_c)
    h_l = linear_scan(a_l, b_l, h_c[:, -1])
    if reverse:
        h_c, h_l = jnp.flip(h_c, axis=1), jnp.flip(h_l, axis=1)
    return h_c, h_l


def centred_dwconv(u, w, b):
    pad_l = CONV_W // 2
    y = lax.conv_general_dilated(u, w[:, None, :], window_strides=(1,), padding=[(pad_l, CONV_W - 1 - pad_l)],
                                 dimension_numbers=('NWC', 'WIO', 'NWC'), feature_group_count=u.shape[-1])
    return y + b


def lru_coeffs(u, wa, ba, wx, bx, lam):
    bsz, t_len, _ = u.shape
    uf = u.astype(jnp.float32)
    ub = uf.reshape(bsz, t_len, LRU_BLOCKS, LRU_BW)
    r = jax.nn.sigmoid(jnp.einsum('btki,kij->btkj', ub, wa.astype(jnp.float32)).reshape(bsz, t_len, LRU_WIDTH)
                       + ba.astype(jnp.float32))
    i = jax.nn.sigmoid(jnp.einsum('btki,kij->btkj', ub, wx.astype(jnp.float32)).reshape(bsz, t_len, LRU_WIDTH)
                       + bx.astype(jnp.float32))
    log_a = -LRU_C * r * jax.nn.softplus(-lam.astype(jnp.float32))
    a = jnp.exp(log_a)
    b = jnp.sqrt(-jnp.expm1(2.0 * log_a)) * (i * uf)
    return a, b


def rglru_mixer(n_c, n_l, w_in, conv_w, conv_b, wa, ba, wx, bx, lam, w_out, ctx_out):
    if ctx_out:
        u_c, g_c = jnp.split(n_c @ w_in, 2, axis=-1)
    else:
        u_c = n_c @ w_in[:, :LRU_WIDTH]
    u_l, g_l = jnp.split(n_l @ w_in, 2, axis=-1)
    u_c = centred_dwconv(u_c, conv_w, conv_b)
    u_l = centred_dwconv(u_l, conv_w, conv_b)
    hc_dirs, hl_dirs = [], []
    for d in range(2):
        a_c, b_c = lru_coeffs(u_c, wa[d], ba[d], wx[d], bx[d], lam[d])
        a_l, b_l = lru_coeffs(u_l, wa[d], ba[d], wx[d], bx[d], lam[d])
        h_c, h_l = scan_with_prefix(a_c, b_c, a_l, b_l, reverse=(d == 1))
        hc_dirs.append(h_c)
        hl_dirs.append(h_l)
    h_l = hl_dirs[0] + hl_dirs[1]
    y_l = (h_l * jax.nn.silu(g_l.astype(jnp.float32))).astype(n_l.dtype) @ w_out
    y_c = None
    if ctx_out:
        h_c = hc_dirs[0] + hc_dirs[1]
        y_c = (h_c * jax.nn.silu(g_c.astype(jnp.float32))).astype(n_c.dtype) @ w_out
    return y_c, y_l


def swa_mixer(n_c, n_l, w_in, sink, w_out, ctx_out):
    bsz, t_len, _ = n_l.shape
    c_len = n_c.shape[1]
    qd = SWA_HEADS * HEAD_DIM
    kvd = SWA_KV_HEADS * HEAD_DIM
    grp = SWA_HEADS // SWA_KV_HEADS
    scale = HEAD_DIM ** -0.5
    sink_kg = sink.reshape(SWA_KV_HEADS, grp)[:, :, None, None]
    q, k, v, g = jnp.split(n_l @ w_in, [qd, qd + kvd, qd + 2 * kvd], axis=-1)
    q = (q * scale).reshape(bsz, t_len, SWA_KV_HEADS, grp, HEAD_DIM)
    k = k.reshape(bsz, t_len, SWA_KV_HEADS, HEAD_DIM)
    v = v.reshape(bsz, t_len, SWA_KV_HEADS, HEAD_DIM)
    if ctx_out:
        q_c, k_c, v_c, g_c = jnp.split(n_c @ w_in, [qd, qd + kvd, qd + 2 * kvd], axis=-1)
    else:
        k_c, v_c = jnp.split(n_c @ w_in[:, qd:qd + 2 * kvd], 2, axis=-1)
    k_c = k_c.reshape(bsz, c_len, SWA_KV_HEADS, HEAD_DIM)
    v_c = v_c.reshape(bsz, c_len, SWA_KV_HEADS, HEAD_DIM)
    cos, sin = axial_rope(t_len)
    q_rot = apply_rope(q, cos, sin)
    k_rot = apply_rope(k, cos, sin)
    nb = t_len // BLOCK_Q
    qb_rot = q_rot.reshape(bsz, nb, BLOCK_Q, SWA_KV_HEADS, grp, HEAD_DIM)
    qb = q.reshape(bsz, nb, BLOCK_Q, SWA_KV_HEADS, grp, HEAD_DIM)
    pad = ((0, 0), (BLOCK_Q, BLOCK_Q), (0, 0), (0, 0))
    kp = jnp.pad(k_rot, pad).reshape(bsz, nb + 2, BLOCK_Q, SWA_KV_HEADS, HEAD_DIM)
    vp = jnp.pad(v, pad).reshape(bsz, nb + 2, BLOCK_Q, SWA_KV_HEADS, HEAD_DIM)
    kb = jnp.concatenate([kp[:, :-2], kp[:, 1:-1], kp[:, 2:]], axis=2)
    vb = jnp.concatenate([vp[:, :-2], vp[:, 1:-1], vp[:, 2:]], axis=2)
    s_band = jnp.einsum('bnqkgd,bnskd->bnkgqs', qb_rot, kb).astype(jnp.float32)
    blk = jnp.arange(nb)[:, None]
    qpos = blk * BLOCK_Q + jnp.arange(BLOCK_Q)[None, :]
    kpos = (blk - 1) * BLOCK_Q + jnp.arange(3 * BLOCK_Q)[None, :]
    valid = ((jnp.abs(qpos[:, :, None] - kpos[:, None, :]) <= WINDOW)
             & (kpos[:, None, :] >= 0) & (kpos[:, None, :] < t_len))
    s_band = jnp.where(valid[None, :, None, None], s_band, NEG_INF)
    s_ctx = jnp.einsum('bnqkgd,bckd->bnkgqc', qb, k_c).astype(jnp.float32)
    p = sink_softmax(jnp.concatenate([s_ctx, s_band], axis=-1), sink_kg).astype(v.dtype)
    o = (jnp.einsum('bnkgqc,bckd->bnqkgd', p[..., :c_len], v_c)
         + jnp.einsum('bnkgqs,bnskd->bnqkgd', p[..., c_len:], vb)).reshape(bsz, t_len, qd)
    y_l = (o * jax.nn.silu(g)) @ w_out
    y_c = None
    if ctx_out:
        q_c = (q_c * scale).reshape(bsz, c_len, SWA_KV_HEADS, grp, HEAD_DIM)
        pc = sink_softmax(jnp.einsum('bckgd,bskd->bkgcs', q_c, k_c), sink_kg).astype(v_c.dtype)
        o_c = jnp.einsum('bkgcs,bskd->bckgd', pc, v_c).reshape(bsz, c_len, qd)
        y_c = (o_c * jax.nn.silu(g_c)) @ w_out
    return y_c, y_l


def na_mixer(n_c, n_l, w_in, rpb, w_out, ctx_out):
    bsz, t_len, _ = n_l.shape
    c_len = n_c.shape[1]
    rows = t_len // GRID_W
    kr = min(NA_ROWS, rows)
    wd = NA_HEADS * HEAD_DIM
    scale = HEAD_DIM ** -0.5
    q, k, v, g = jnp.split(n_l @ w_in, 4, axis=-1)
    if ctx_out:
        q_c, k_c, v_c, g_c = jnp.split(n_c @ w_in, 4, axis=-1)
    else:
        k_c, v_c = jnp.split(n_c @ w_in[:, wd:3 * wd], 2, axis=-1)
    k_c = k_c.reshape(bsz, c_len, NA_HEADS, HEAD_DIM)
    v_c = v_c.reshape(bsz, c_len, NA_HEADS, HEAD_DIM)
    qg = (q * scale).reshape(bsz, rows, GRID_W, NA_HEADS, HEAD_DIM)
    kg = k.reshape(bsz, rows, GRID_W, NA_HEADS, HEAD_DIM)
    vg = v.reshape(bsz, rows, GRID_W, NA_HEADS, HEAD_DIM)
    r = jnp.arange(rows)
    ridx = jnp.clip(r - kr // 2, 0, rows - kr)[:, None] + jnp.arange(kr)[None, :]
    kb = kg[:, ridx].reshape(bsz, rows, kr * GRID_W, NA_HEADS, HEAD_DIM)
    vb = vg[:, ridx].reshape(bsz, rows, kr * GRID_W, NA_HEADS, HEAD_DIM)
    s_nb = jnp.einsum('brqhd,brkhd->brhqk', qg, kb).astype(jnp.float32)
    cq = jnp.arange(GRID_W)
    cstart = jnp.clip(cq - NA_COLS // 2, 0, GRID_W - NA_COLS)
    col_ok = (cq[None, :] >= cstart[:, None]) & (cq[None, :] < cstart[:, None] + NA_COLS)
    dy = ridx - r[:, None]
    dxi = jnp.clip(cq[None, :] - cq[:, None] + NA_COLS - 1, 0, 2 * NA_COLS - 2)
    bias = rpb[:, dy[:, None, :, None] + NA_ROWS - 1, dxi[None, :, None, :]]
    bias = jnp.moveaxis(bias, 0, 1).reshape(rows, NA_HEADS, GRID_W, kr * GRID_W).astype(jnp.float32)
    mask = jnp.broadcast_to(col_ok[:, None, :], (GRID_W, kr, GRID_W)).reshape(GRID_W, kr * GRID_W)
    s_nb = jnp.where(mask, s_nb + bias, NEG_INF)
    s_ctx = jnp.einsum('brqhd,bchd->brhqc', qg, k_c).astype(jnp.float32)
    p = jax.nn.softmax(jnp.concatenate([s_ctx, s_nb], axis=-1), axis=-1).astype(v.dtype)
    o = (jnp.einsum('brhqc,bchd->brqhd', p[..., :c_len], v_c)
         + jnp.einsum('brhqk,brkhd->brqhd', p[..., c_len:], vb)).reshape(bsz, t_len, wd)
    y_l = (o * jax.nn.silu(g)) @ w_out
    y_c = None
    if ctx_out:
        q_c = (q_c * scale).reshape(bsz, c_len, NA_HEADS, HEAD_DIM)
        pc = jax.nn.softmax(jnp.einsum('bchd,bshd->bhcs', q_c, k_c).astype(jnp.float32), axis=-1).astype(v_c.dtype)
        o_c = jnp.einsum('bhcs,bshd->bchd', pc, v_c).reshape(bsz, c_len, wd)
        y_c = (o_c * jax.nn.silu(g_c)) @ w_out
    return y_c, y_l


def s5_readout(y, g, glu_w, glu_b, w_out, dtype):
    y = jax.nn.gelu(y)
    y = y * jax.nn.sigmoid(y @ glu_w.astype(jnp.float32) + glu_b.astype(jnp.float32))
    return (y * jax.nn.silu(g.astype(jnp.float32))).astype(dtype) @ w_out


def s5_mixer(n_c, n_l, w_in, a_re, a_im, log_dt, b_re, b_im, c_re, c_im, d_skip, glu_w, glu_b, w_out, ctx_out):
    bsz, t_len, _ = n_l.shape
    c_len = n_c.shape[1]
    f32 = jnp.float32
    u_l, g_l = jnp.split(n_l @ w_in, 2, axis=-1)
    if ctx_out:
        u_c, g_c = jnp.split(n_c @ w_in, 2, axis=-1)
    else:
        u_c = n_c @ w_in[:, :S5_WIDTH]

    def groups(u):
        return u.astype(f32).reshape(u.shape[0], u.shape[1], S5_GROUPS, S5_GROUP).astype(jnp.complex64)

    uc_g, ul_g = groups(u_c), groups(u_l)
    y_l = d_skip.astype(f32) * u_l.astype(f32)
    y_c = d_skip.astype(f32) * u_c.astype(f32)
    for d in range(2):
        lam = lax.complex(a_re[d].astype(f32), a_im[d].astype(f32))
        dt = jnp.exp(log_dt[d].astype(f32))[:, None]
        lam_bar = jnp.exp(lam * dt)
        b_bar = ((lam_bar - 1.0) / lam)[..., None] * lax.complex(b_re[d].astype(f32), b_im[d].astype(f32))
        c_mat = lax.complex(c_re[d].astype(f32), c_im[d].astype(f32))
        bu_c = jnp.einsum('btgi,gpi->btgp', uc_g, b_bar)
        bu_l = jnp.einsum('btgi,gpi->btgp', ul_g, b_bar)
        a_c = jnp.broadcast_to(lam_bar, (1, c_len, S5_GROUPS, S5_STATE))
        a_l = jnp.broadcast_to(lam_bar, (1, t_len, S5_GROUPS, S5_STATE))
        h_c, h_l = scan_with_prefix(a_c, bu_c, a_l, bu_l, reverse=(d == 1))
        y_l = y_l + jnp.einsum('btgp,gip->btgi', h_l, c_mat).real.reshape(bsz, t_len, S5_WIDTH)
        if ctx_out:
            y_c = y_c + jnp.einsum('btgp,gip->btgi', h_c, c_mat).real.reshape(bsz, c_len, S5_WIDTH)
    out_l = s5_readout(y_l, g_l, glu_w, glu_b, w_out, n_l.dtype)
    out_c = s5_readout(y_c, g_c, glu_w, glu_b, w_out, n_c.dtype) if ctx_out else None
    return out_c, out_l


def setup_inputs(seed: int = 0) -> dict:
    key = jax.random.key(seed)
    keys = iter(jax.random.split(key, 64))
    f32 = jnp.float32
    D = D_MODEL

    def nrm(shape, scale):
        return scale * jax.random.normal(next(keys), shape, f32)

    def gain(n):
        return 1.0 + nrm((n,), 0.05)

    qd = SWA_HEADS * HEAD_DIM
    kvd = SWA_KV_HEADS * HEAD_DIM
    wd = NA_HEADS * HEAD_DIM
    inp = {}
    inp['x'] = nrm((BATCH, SEQ, D), 1.0)
    inp['c'] = nrm((BATCH, D), 1.0)
    inp['ctx'] = nrm((BATCH, CTX_LEN, D), 1.0)
    inp['c_ctx'] = nrm((D,), 1.0)
    inp['ada_w0'] = nrm((D, 3 * D), 0.5 * D ** -0.5)
    inp['ada_b0'] = nrm((3 * D,), 0.02)
    inp['norm0'] = gain(D)
    inp['w_in0'] = nrm((D, 2 * LRU_WIDTH), D ** -0.5)
    inp['conv_w0'] = nrm((CONV_W, LRU_WIDTH), CONV_W ** -0.5)
    inp['conv_b0'] = nrm((LRU_WIDTH,), 0.02)
    inp['lru_wa0'] = nrm((2, LRU_BLOCKS, LRU_BW, LRU_BW), LRU_BW ** -0.5)
    inp['lru_ba0'] = nrm((2, LRU_WIDTH), 0.02)
    inp['lru_wx0'] = nrm((2, LRU_BLOCKS, LRU_BW, LRU_BW), LRU_BW ** -0.5)
    inp['lru_bx0'] = nrm((2, LRU_WIDTH), 0.02)
    a0 = jax.random.uniform(next(keys), (2, LRU_WIDTH), f32, 0.9 ** (1.0 / LRU_C), 0.999 ** (1.0 / LRU_C))
    inp['lru_lam0'] = jnp.log(a0) - jnp.log1p(-a0)
    inp['w_out0'] = nrm((LRU_WIDTH, D), LRU_WIDTH ** -0.5)
    inp['ada_w1'] = nrm((D, 3 * D), 0.5 * D ** -0.5)
    inp['ada_b1'] = nrm((3 * D,), 0.02)
    inp['norm1'] = gain(D)
    inp['w_in1'] = nrm((D, 2 * qd + 2 * kvd), D ** -0.5)
    inp['sink1'] = nrm((SWA_HEADS,), 1.0)
    inp['w_out1'] = nrm((qd, D), qd ** -0.5)
    inp['ada_w2'] = nrm((D, 3 * D), 0.5 * D ** -0.5)
    inp['ada_b2'] = nrm((3 * D,), 0.02)
    inp['norm2'] = gain(D)
    inp['w_in2'] = nrm((D, 4 * wd), D ** -0.5)
    inp['rpb2'] = nrm((NA_HEADS, 2 * NA_ROWS - 1, 2 * NA_COLS - 1), 0.1)
    inp['w_out2'] = nrm((wd, D), wd ** -0.5)
    inp['ada_w3'] = nrm((D, 3 * D), 0.5 * D ** -0.5)
    inp['ada_b3'] = nrm((3 * D,), 0.02)
    inp['norm3'] = gain(D)
    inp['w_in3'] = nrm((D, 2 * S5_WIDTH), D ** -0.5)
    inp['s5_a_re3'] = -0.5 + nrm((2, S5_GROUPS, S5_STATE), 0.01)
    inp['s5_a_im3'] = math.pi * jnp.arange(S5_STATE, dtype=f32) + nrm((2, S5_GROUPS, S5_STATE), 0.01)
    inp['s5_log_dt3'] = jax.random.uniform(next(keys), (2, S5_GROUPS), f32, math.log(1e-3), math.log(1e-1))
    inp['s5_b_re3'] = nrm((2, S5_GROUPS, S5_STATE, S5_GROUP), (2 * S5_GROUP) ** -0.5)
    inp['s5_b_im3'] = nrm((2, S5_GROUPS, S5_STATE, S5_GROUP), (2 * S5_GROUP) ** -0.5)
    inp['s5_c_re3'] = nrm((2, S5_GROUPS, S5_GROUP, S5_STATE), (2 * S5_STATE) ** -0.5)
    inp['s5_c_im3'] = nrm((2, S5_GROUPS, S5_GROUP, S5_STATE), (2 * S5_STATE) ** -0.5)
    inp['s5_d3'] = nrm((S5_WIDTH,), 1.0)
    inp['glu_w3'] = nrm((S5_WIDTH, S5_WIDTH), S5_WIDTH ** -0.5)
    inp['glu_b3'] = nrm((S5_WIDTH,), 0.02)
    inp['w_out3'] = nrm((S5_WIDTH, D), S5_WIDTH ** -0.5)
    inp['norm_f'] = gain(D)
    return inp


def reference(x, c, ctx, c_ctx,
              ada_w0, ada_b0, norm0, w_in0, conv_w0, conv_b0, lru_wa0, lru_ba0, lru_wx0, lru_bx0, lru_lam0, w_out0,
              ada_w1, ada_b1, norm1, w_in1, sink1, w_out1,
              ada_w2, ada_b2, norm2, w_in2, rpb2, w_out2,
              ada_w3, ada_b3, norm3, w_in3, s5_a_re3, s5_a_im3, s5_log_dt3, s5_b_re3, s5_b_im3, s5_c_re3, s5_c_im3,
              s5_d3, glu_w3, glu_b3, w_out3,
              norm_f):
    mixers = (rglru_mixer, swa_mixer, na_mixer, s5_mixer)
    layers = (
        ((ada_w0, ada_b0, norm0), (w_in0, conv_w0, conv_b0, lru_wa0, lru_ba0, lru_wx0, lru_bx0, lru_lam0, w_out0)),
        ((ada_w1, ada_b1, norm1), (w_in1, sink1, w_out1)),
        ((ada_w2, ada_b2, norm2), (w_in2, rpb2, w_out2)),
        ((ada_w3, ada_b3, norm3), (w_in3, s5_a_re3, s5_a_im3, s5_log_dt3, s5_b_re3, s5_b_im3, s5_c_re3, s5_c_im3,
                                   s5_d3, glu_w3, glu_b3, w_out3)),
    )
    h_lat, h_ctx = x, ctx
    for i in range(DEPTH):
        (ada_w, ada_b, g_norm), margs = layers[i]
        ctx_out = i < DEPTH - 1
        sh_l, sc_l, gt_l = modulation(c, ada_w, ada_b)
        sh_c, sc_c, gt_c = modulation(c_ctx[None, :], ada_w, ada_b)
        n_l = rmsnorm(h_lat, g_norm) * (1.0 + sc_l) + sh_l
        n_c = rmsnorm(h_ctx, g_norm) * (1.0 + sc_c) + sh_c
        y_c, y_l = mixers[i % N_MIXERS](n_c, n_l, *margs, ctx_out=ctx_out)
        h_lat = h_lat + gt_l * y_l
        if ctx_out:
            h_ctx = h_ctx + gt_c * y_c
    return rmsnorm(h_lat, norm_f)
```

```python
import math
from contextlib import ExitStack
import numpy as np
import ml_dtypes
import concourse.bass as bass
import concourse.mybir as mybir
from concourse.ap import AP
from concourse.bass_utils import run_bass_kernel_spmd

F32 = mybir.dt.float32
BF16 = mybir.dt.bfloat16
I32 = mybir.dt.int32
AF = mybir.ActivationFunctionType
ALU = mybir.AluOpType
DSZ = {F32: 4, BF16: 2, I32: 4}

D = 1024
T = 2048
C = 256
TT = T + C
NCH = 8
EPS = 1e-6
EPOCH = 2000
ENG = ["pe", "act", "dve", "pool", "sp"]
import os as _os
SAME_SYNC = {"dve": _os.environ.get("KN_DVESYNC", "1") == "1", "act": False, "pool": False, "pe": False, "sp": False}
DVE_MIN = int(_os.environ.get("KN_DVEMIN", "0"))
NDMA = 12


def ap_region(ap):
    t = ap.tensor
    dims = [list(x) for x in ap.ap]
    off = int(ap.offset)
    sz = DSZ.get(ap.dtype, None)
    if sz is None:
        sz = mybir.dt.size(ap.dtype)
    if str(ap.space) == "DRAM":
        lo = hi = off
        for st, cn in dims:
            e = st * (cn - 1)
            lo += min(0, e)
            hi += max(0, e)
        return (t.name, 0, 1, lo * sz, (hi + 1) * sz)
    pst, pcn = dims[0]
    row = 1
    for s in t.shape[1:]:
        row *= s
    p0 = off // row
    f0 = off % row
    if pcn > 1:
        assert pst == row, (pst, row)
    lo = hi = f0
    for st, cn in dims[1:]:
        e = st * (cn - 1)
        lo += min(0, e)
        hi += max(0, e)
    assert lo >= 0 and hi < row, (lo, hi, row, ap.ap, off)
    return (t.name, p0, p0 + pcn, lo * sz, (hi + 1) * sz)


def _fsz(ap):
    n = 1
    for x in ap.shape[1:]:
        n *= x
    return n


def _vcost(eng, out):
    n = _fsz(out)
    if eng == "pool":
        return 500.0 + 5.0 * n
    return 60.0 + 1.05 * n


def _apsig(ap):
    if not DVE_MIN:
        return None
    dims = ap.ap
    if len(dims) != 2 or dims[1][0] != 1 or dims[1][1] < DVE_MIN:
        return None
    return (ap.tensor.name, int(ap.offset), tuple(tuple(x) for x in dims), str(ap.dtype))


class Prog:
    BUCKET = 1024

    def __init__(self, nc):
        self.nc = nc
        self.ops = []
        self.strict = False
        self.last_on = {}
        self.since = {e: [] for e in ENG}
        self.barrier = {}
        self.pending_start = set()
        self.bk = {}
        import os
        self.sched = os.environ.get('KN_SCHED', '1') == '1'
        self.K = int(os.environ.get('KN_K', '64'))

    def _buckets(self, name, b0, b1):
        d = self.bk.setdefault(name, {})
        B = self.BUCKET if name in ("arena", "psum") else 65536
        for i in range(b0 // B, (b1 - 1) // B + 1):
            e = d.get(i)
            if e is None:
                e = d[i] = ([], [])
            yield e

    def add(self, eng, fn, reads=(), writes=(), dma=False, cost=100.0, nbytes=0, tbl=None, order_after=()):
        oid = len(self.ops)
        deps = set()
        regs_r = [ap_region(a) for a in reads]
        regs_w = [ap_region(a) for a in writes]
        hard = set()
        for a_, (name, p0, p1, b0, b1) in zip(reads, regs_r):
            sig = _apsig(a_)
            for (wl, rl) in self._buckets(name, b0, b1):
                for e in wl:
                    if e[0] < p1 and p0 < e[1] and e[2] < b1 and b0 < e[3]:
                        deps.add(e[4])
                        if not (sig is not None and e[5] == sig):
                            hard.add(e[4])
        for (name, p0, p1, b0, b1) in regs_w:
            for (wl, rl) in self._buckets(name, b0, b1):
                for lst in (wl, rl):
                    for e in lst:
                        if e[0] < p1 and p0 < e[1] and e[2] < b1 and b0 < e[3]:
                            deps.add(e[4])
        deps.discard(oid)
        wsig = {}
        for a_, r_ in zip(writes, regs_w):
            wsig[r_] = _apsig(a_)
        odeps = set()
        if eng in self.barrier:
            odeps.add(self.barrier[eng])
        if eng in self.pending_start:
            odeps.update(self.since[eng])
            self.pending_start.discard(eng)
            self.since[eng] = []
            self.barrier[eng] = oid
        elif self.strict and eng in self.last_on:
            odeps.add(self.last_on[eng])
        odeps.update(order_after)
        odeps -= deps
        odeps.discard(oid)
        self.last_on[eng] = oid
        self.since[eng].append(oid)
        for (name, p0, p1, b0, b1) in regs_w:
            for (wl, rl) in self._buckets(name, b0, b1):
                wl[:] = [e for e in wl if not (p0 <= e[0] and e[1] <= p1 and b0 <= e[2] and e[3] <= b1)]
                rl[:] = [e for e in rl if not (p0 <= e[0] and e[1] <= p1 and b0 <= e[2] and e[3] <= b1)]
                wl.append((p0, p1, b0, b1, oid, wsig[(name, p0, p1, b0, b1)]))
        for (name, p0, p1, b0, b1) in regs_r:
            for (wl, rl) in self._buckets(name, b0, b1):
                rl.append((p0, p1, b0, b1, oid))
        self.ops.append(dict(id=oid, eng=eng, fn=fn, dma=dma, deps=deps, odeps=odeps, hard=hard, cost=float(cost), nbytes=nbytes, tbl=tbl))
        return oid

    def begin_strict(self):
        self.strict = True
        self.pending_start = set(ENG)

    def full_barrier(self, engines=("pe", "act", "dve", "pool")):
        return
        for e in engines:
            if e in self.last_on:
                self.barrier[e] = self.last_on[e]
            self.pending_start.add(e)

    def end_strict(self):
        self.strict = False
        for e in ENG:
            if e in self.last_on:
                self.barrier[e] = self.last_on[e]
            self.since[e] = []

    def _schedule(self):
        import bisect
        ops = self.ops
        n = len(ops)
        if not self.sched:
            order = {e: [] for e in ENG}
            for o in ops:
                order[o["eng"]].append(o["id"])
            return order
        users = [[] for _ in range(n)]
        ndep = [0] * n
        for o in ops:
            alld = o["deps"] | o["odeps"]
            ndep[o["id"]] = len(alld)
            for d in alld:
                users[d].append(o["id"])
        finish = [0.0] * n
        ready_t = [0.0] * n
        cand = {e: [] for e in ENG}
        for o in ops:
            if ndep[o["id"]] == 0:
                cand[o["eng"]].append(o["id"])
        free = {e: 0.0 for e in ENG}
        order = {e: [] for e in ENG}
        K = self.K
        cur_tbl = [None]
        last_dve = [-1]
        DVE_STALL = float(_os.environ.get("KN_DVESTALL", "250"))
        import os
        KE = {e: int(os.environ.get('KN_K_' + e, str(K))) for e in ENG}
        done = 0
        while done < n:
            best = None
            for e in ENG:
                c = cand[e]
                if not c:
                    continue
                bi, bt = None, None
                for oid in c[:KE[e]]:
                    t = max(ready_t[oid], free[e])
                    if e == "dve" and last_dve[0] in ops[oid]["deps"]:
                        t += DVE_STALL
                    if e == "act":
                        tb_ = ops[oid]["tbl"]
                        if tb_ is not None and cur_tbl[0] is not None and tb_ != cur_tbl[0]:
                            t += 1300.0
                    if bt is None or t < bt - 1e-9:
                        bi, bt = oid, t
                if best is None or bt < best[0] - 1e-9 or (abs(bt - best[0]) <= 1e-9 and bi < best[1]):
                    best = (bt, bi, e)
            bt, oid, e = best
            o = ops[oid]
            cand[e].remove(oid)
            order[e].append(oid)
            if e == "act" and o["tbl"] is not None:
                cur_tbl[0] = o["tbl"]
            if e == "dve":
                last_dve[0] = oid
            if o["dma"]:
                free[e] = bt + 60.0
                finish[oid] = bt + 2000.0 + o["nbytes"] / 120.0
            else:
                free[e] = bt + o["cost"]
                finish[oid] = free[e]
            o["start"] = bt
            done += 1
            for u in users[oid]:
                ndep[u] -= 1
                if finish[oid] > ready_t[u]:
                    ready_t[u] = finish[oid]
                if ndep[u] == 0:
                    bisect.insort(cand[ops[u]["eng"]], u)
        self.model_span = max(finish) if n else 0.0
        return order

    def _semkey(self, ek, idx):
        if isinstance(ek, tuple):
            return (ek, (idx - 1) // 100), 16 * ((idx - 1) % 100 + 1), 16
        return (ek, (idx - 1) // EPOCH), (idx - 1) % EPOCH + 1, 1

    def emit(self):
        nc = self.nc
        ops = self.ops
        order = self._schedule()
        ident = {}
        cnt = {}
        for e in ENG:
            for oid in order[e]:
                if not ops[oid]["dma"]:
                    cnt[e] = cnt.get(e, 0) + 1
                    ident[oid] = (e, cnt[e])
        dmas = [o for o in ops if o["dma"]]
        dmas.sort(key=lambda o: (o.get("start", o["id"]), o["id"]))
        rr = 0
        prev_on_vq = {}
        chain = {}
        for o in dmas:
            ek = ("dma", rr)
            rr = (rr + 1) % NDMA
            cnt[ek] = cnt.get(ek, 0) + 1
            ident[o["id"]] = (ek, cnt[ek])
            if ek in prev_on_vq:
                chain[o["id"]] = prev_on_vq[ek]
            prev_on_vq[ek] = o["id"]
        q = {e: [] for e in ENG}
        keys = set()
        for e in ENG:
            seen = {}
            for oid in order[e]:
                o = ops[oid]
                deps = set(o["deps"])
                if oid in chain:
                    deps.add(chain[oid])
                need = {}
                for d in deps:
                    dk, di = ident[d]
                    if dk == e and not o["dma"] and (not SAME_SYNC[e] or (DVE_MIN and d not in o["hard"])):
                        continue
                    if need.get(dk, 0) < di:
                        need[dk] = di
                waits = []
                for dk, di in need.items():
                    if seen.get(dk, 0) >= di:
                        continue
                    seen[dk] = di
                    waits.append((dk, di))
                    keys.add(self._semkey(dk, di)[0])
                ek, idx = ident[oid]
                keys.add(self._semkey(ek, idx)[0])
                q[e].append((waits, o["fn"], ek, idx))
        final_waits = [ident[oid] for oid in prev_on_vq.values()]
        with ExitStack() as st:
            sems = {}
            for i, k in enumerate(sorted(keys, key=str)):
                sems[k] = st.enter_context(nc.semaphore("s%d" % i))
            block = st.enter_context(nc.Block())

            def replay(ename):
                def body(e):
                    for waits, fn, ek, idx in q[ename]:
                        for dk, di in waits:
                            k, v, _ = self._semkey(dk, di)
                            e.wait_ge(sems[k], v)
                        ins = fn(e)
                        k, v, inc = self._semkey(ek, idx)
                        ins.then_inc(sems[k], inc)
                    if ename == "sp":
                        for dk, di in final_waits:
                            k, v, _ = self._semkey(dk, di)
                            e.wait_ge(sems[k], v)
                return body

            block.tensor(replay("pe"))
            block.scalar(replay("act"))
            block.vector(replay("dve"))
            block.gpsimd(replay("pool"))
            block.sync(replay("sp"))

    def mm(self, out, lhsT, rhs, start=True, stop=True, **kw):
        rd = [lhsT, rhs] + ([] if start else [out])
        reg = ap_region(out)
        banks = range(reg[3] // 2048, (reg[4] - 1) // 2048 + 1)
        if not hasattr(self, "last_mm_bank"):
            self.last_mm_bank = {}
        after = [self.last_mm_bank[b_] for b_ in banks if b_ in self.last_mm_bank]
        oid = self.add("pe", lambda e: e.matmul(out, lhsT, rhs, start=start, stop=stop, **kw), rd, [out],
                       cost=70.0 + 0.36 * _fsz(out), order_after=after)
        for b_ in banks:
            self.last_mm_bank[b_] = oid
        return oid

    def tr(self, out, in_, ident):
        reg = ap_region(out)
        banks = range(reg[3] // 2048, (reg[4] - 1) // 2048 + 1)
        if not hasattr(self, "last_mm_bank"):
            self.last_mm_bank = {}
        after = [self.last_mm_bank[b_] for b_ in banks if b_ in self.last_mm_bank]
        oid = self.add("pe", lambda e: e.transpose(out, in_, ident), [in_, ident], [out], cost=150.0, order_after=after)
        for b_ in banks:
            self.last_mm_bank[b_] = oid
        return oid

    def act(self, out, in_, func, bias=None, scale=None, eng="act"):
        rd = [in_]
        kw = {}
        if bias is not None:
            kw["bias"] = bias
            if not isinstance(bias, (int, float)):
                rd.append(bias)
        if scale is not None:
            kw["scale"] = scale
            if not isinstance(scale, (int, float)):
                rd.append(scale)
        tbl = None if func in (AF.Copy, AF.Identity) else str(func)
        return self.add(eng, lambda e: e.activation(out, in_, func, **kw), rd, [out], cost=220.0 + 0.72 * _fsz(out), tbl=tbl)

    def copy(self, eng, out, in_):
        if eng == "act":
            return self.add("act", lambda e: e.activation(out, in_, AF.Copy), [in_], [out], cost=220.0 + 0.72 * _fsz(out))
        return self.add(eng, lambda e: e.tensor_copy(out, in_), [in_], [out], cost=_vcost(eng, out))

    def tt(self, eng, out, in0, in1, op):
        return self.add(eng, lambda e: e.tensor_tensor(out, in0, in1, op), [in0, in1], [out], cost=_vcost(eng, out))

    def ts(self, eng, out, in0, s1, s2, op0, op1=None):
        rd = [in0] + [s for s in (s1, s2) if s is not None and not isinstance(s, (int, float))]
        if op1 is None:
            return self.add(eng, lambda e: e.tensor_scalar(out, in0, s1, None, op0), rd, [out], cost=_vcost(eng, out))
        return self.add(eng, lambda e: e.tensor_scalar(out, in0, s1, s2, op0, op1), rd, [out], cost=_vcost(eng, out))

    def stt(self, eng, out, in0, scalar, in1, op0, op1):
        rd = [in0, in1] + ([] if isinstance(scalar, (int, float)) else [scalar])
        return self.add(eng, lambda e: e.scalar_tensor_tensor(out, in0, scalar, in1, op0, op1), rd, [out], cost=_vcost(eng, out))

    def scan(self, eng, out, d0, d1, init, op0=ALU.mult, op1=ALU.add):
        rd = [d0, d1] + ([] if isinstance(init, (int, float)) else [init])
        return self.add(eng, lambda e: e.tensor_tensor_scan(out, d0, d1, init, op0, op1), rd, [out], cost=100.0 + 2.1 * _fsz(out))

    def memset(self, eng, out, val):
        return self.add(eng, lambda e: e.memset(out, val), [], [out], cost=_vcost(eng, out))

    def recip(self, out, in_):
        return self.add("dve", lambda e: e.reciprocal(out, in_), [in_], [out], cost=100.0 + 4.0 * _fsz(out))

    def dma(self, out, in_, eng="sp", slow=False):
        kw = {"allow_slow_non_contiguous": True} if slow else {}
        nb = _fsz(out) * out.shape[0] * DSZ.get(out.dtype, 4)
        return self.add(eng, lambda e: e.dma_start(out=out, in_=in_, **kw), [in_], [out], dma=True, nbytes=nb * (6 if slow else 1))


class Arena:
    def __init__(self, nc, name, words):
        self.t = nc.alloc_sbuf_tensor(name, [128, words], F32)
        self.words = words

    def view(self, off_w, shape, dt):
        n = 1
        for s in shape[1:]:
            n *= s
        nw = (n * DSZ[dt] + 3) // 4
        assert off_w + nw <= self.words, (off_w, nw, self.words)
        v = self.t[0:shape[0], off_w:off_w + nw]
        if dt != F32:
            v = v.bitcast(dt)
            v = v[:, 0:n]
        if len(shape) > 2:
            names = " ".join("d%d" % i for i in range(1, len(shape)))
            kw = {"d%d" % i: shape[i] for i in range(1, len(shape))}
            v = v.rearrange("p (%s) -> p %s" % (names, names), **kw)
        return v


class Bump:
    def __init__(self, arena, lo_w, hi_w):
        self.a = arena
        self.lo = lo_w
        self.hi = hi_w
        self.p = lo_w

    def alloc(self, shape, dt):
        n = 1
        for s in shape[1:]:
            n *= s
        nw = (n * DSZ[dt] + 3) // 4
        nw = (nw + 7) // 8 * 8
        assert self.p + nw <= self.hi, ("bump overflow", self.p, nw, self.hi)
        v = self.a.view(self.p, shape, dt)
        self.p += nw
        return v

    def reset(self):
        self.p = self.lo


def rev(ap):
    dims = [list(x) for x in ap.ap]
    assert len(dims) == 2
    st, cn = dims[1]
    return AP(ap.tensor, ap.offset + st * (cn - 1), [dims[0], [-st, cn]])


def bc_free(ap, n, axis_pos):
    dims = [list(x) for x in ap.ap]
    dims.insert(axis_pos, [0, n])
    return AP(ap.tensor, ap.offset, dims)


COLCH = [(0, 256)] + [(256 + 512 * i, 512) for i in range(4)]


class K:
    pass


def build_program(n_layers=4, final_norm=True):
    nc = bass.Bass("TRN2", target_bir_lowering=False)
    P = Prog(nc)
    k = K()
    k.nc, k.P = nc, P

    def din(name, shape, dt=F32):
        return nc.dram_tensor(name, list(shape), dt, kind="ExternalInput").ap()

    W = {}
    W["x"] = din("x", [T, D])
    W["ctx"] = din("ctx", [C, D])
    W["c"] = din("c", [D])
    W["c_ctx"] = din("c_ctx", [D])
    shapes = dict(
        ada_w0=[D, 3 * D], ada_b0=[3 * D], norm0=[D], w_in0=[D, 2816], conv_w0=[4, 1408], conv_b0=[1408],
        lru_wa0=[2, 16, 88, 88], lru_ba0=[2, 1408], lru_wx0=[2, 16, 88, 88], lru_bx0=[2, 1408],
        lru_lam0=[2, 1408], w_out0=[1408, D],
        ada_w1=[D, 3 * D], ada_b1=[3 * D], norm1=[D], w_in1=[D, 2560], sink1=[16], w_out1=[D, D],
        ada_w2=[D, 3 * D], ada_b2=[3 * D], norm2=[D], w_in2=[D, 4096], w_out2=[D, D],
        ada_w3=[D, 3 * D], ada_b3=[3 * D], norm3=[D], w_in3=[D, 2048],
        s5_a_re3=[2, 64, 64], s5_a_im3=[2, 64, 64], s5_log_dt3=[2, 64],
        s5_b_re3=[2, 64, 64, 16], s5_b_im3=[2, 64, 64, 16], s5_c_re3=[2, 64, 16, 64], s5_c_im3=[2, 64, 16, 64],
        s5_d3=[D], glu_w3=[D, D], glu_b3=[D], w_out3=[D, D], norm_f=[D],
    )
    for nm, sh in shapes.items():
        W[nm] = din(nm, sh)
    W["ident_f"] = din("ident_f", [128, 128])
    W["ident_b"] = din("ident_b", [128, 128], BF16)
    W["pswap"] = din("pswap", [128, 128], BF16)
    W["m_prev"] = din("m_prev", [128, 128], BF16)
    W["m_next"] = din("m_next", [128, 128], BF16)
    W["cosT"] = din("cosT", [128, T])
    W["sinT"] = din("sinT", [128, T])
    W["natab"] = din("natab", [16, len(na_plan()[0]), 128, 128])
    W["s5_kf"] = din("s5_kf", [128, 4, 8])
    W["s5_sgn"] = din("s5_sgn", [128, 1])
    W["s5_msk"] = din("s5_msk", [128, 2, 128])
    W["s5_iota"] = din("s5_iota", [128, 320])
    W["s5_wd"] = din("s5_wd", [128, 8, 240], BF16)
    W["s5_v4"] = din("s5_v4", [128, 4, 240], BF16)
    if final_norm:
        out = nc.dram_tensor("out", [T, D], F32, kind="ExternalOutput").ap()
    else:
        out = nc.dram_tensor("out", [TT, D], F32, kind="ExternalOutput").ap()
    k.W = W
    k.hspill = nc.dram_tensor("hspill", [128, NCH, TT], F32, kind="Internal").ap()

    AW = 52000
    ar = Arena(nc, "arena", AW)
    k.ar = ar
    k.ps = nc.alloc_psum_tensor("psum", [128, 4096], F32)
    k.bank_rr = 0

    def bank(n=1):
        if k.bank_rr + n > 8:
            k.bank_rr = 0
        b = k.bank_rr
        k.bank_rr = (k.bank_rr + n) % 8
        return k.ps[:, 512 * b:512 * (b + n)]
    k.bank = bank

    CW = 5900
    RW = TT * NCH
    k.cb = Bump(ar, 0, CW)
    k.R1 = (CW, CW + RW)
    k.R2 = (CW + RW, CW + 2 * RW)
    k.R3 = (CW + 2 * RW, AW)
    assert k.R3[1] - k.R3[0] >= TT * NCH // 2

    k.ident_f = k.cb.alloc([128, 128], F32)
    k.ident_b = k.cb.alloc([128, 128], BF16)
    k.ones_b = k.cb.alloc([128, 128], BF16)
    P.dma(k.ident_f, W["ident_f"])
    P.dma(k.ident_b, W["ident_b"])
    P.memset("pool", k.ones_b, 1.0)
    k.oneb = k.cb.alloc([128, 1], F32)
    k.epsb = k.cb.alloc([128, 1], F32)
    P.memset("pool", k.oneb, 1.0)
    P.memset("pool", k.epsb, EPS)
    k.halfpi = k.cb.alloc([128, 1], F32)
    P.memset("pool", k.halfpi, math.pi / 2)
    k.mod = [k.cb.alloc([128, 24, 2], F32) for _ in range(4)]
    k.gmul = [k.cb.alloc([128, NCH, 2], F32) for _ in range(4)]
    k.gnorm = k.cb.alloc([128, 5, NCH], F32)

    modulation_setup(k)
    k.lb = Bump(ar, k.cb.p, CW)
    phase0(k)
    hreg = k.R1
    import os
    lsel = os.environ.get("KN_LAYERS")
    llist = [int(x) for x in lsel.split(",")] if lsel else list(range(n_layers))
    k.llist = llist
    if llist:
        modulation_layer(k, llist[0])
    for li, l in enumerate(llist):
        P.full_barrier()
        k.next_layer = llist[li + 1] if li + 1 < len(llist) else None
        hreg = layer(k, l, hreg)
    P.full_barrier()
    finalize(k, hreg, out, final_norm)
    P.emit()
    return nc


def phase0(k):
    P, W = k.P, k.W
    k.hT = k.ar.view(k.R1[0], [128, NCH, TT], F32)
    wb = Bump(k.ar, k.R2[0], k.R2[1])
    tiles = [wb.alloc([128, D], F32) for _ in range(4)]
    for j in range(TT // 128):
        xt = tiles[j % 4]
        src = W["ctx"][128 * j:128 * (j + 1), :] if j < 2 else W["x"][128 * (j - 2):128 * (j - 1), :]
        P.dma(xt, src)
        for half in range(2):
            ps = k.bank()
            for q in range(4):
                kk = half * 4 + q
                P.tr(ps[:, 128 * q:128 * (q + 1)], xt[:, 128 * kk:128 * (kk + 1)], k.ident_f)
            dst = k.hT[:, half * 4:half * 4 + 4, 128 * j:128 * (j + 1)]
            P.copy("act" if half == 0 else "dve", dst, ps.rearrange("p (q c) -> p q c", q=4))


def modulation_setup(k):
    P, W = k.P, k.W
    k.vec = k.cb.alloc([128, NCH, 2], F32)
    k.vecb = k.cb.alloc([128, NCH, 2], BF16)
    P.dma(k.vec[:, :, 0], W["c"].rearrange("(k p) -> p k", p=128), slow=True)
    P.dma(k.vec[:, :, 1], W["c_ctx"].rearrange("(k p) -> p k", p=128), slow=True)
    P.act(k.vecb, k.vec, AF.Silu)
    names = ["norm0", "norm1", "norm2", "norm3", "norm_f"]
    for i, nm in enumerate(names):
        P.dma(k.gnorm[:, i, :], W[nm].rearrange("(k p) -> p k", p=128), slow=True)
    k.adaw = [k.cb.alloc([128, NCH, 128], BF16) for _ in range(2)]
    k.adab = k.cb.alloc([128, 4, 24], F32)
    k.ada_it = 0


def modulation_layer(k, l):
    P, W = k.P, k.W
    bias = k.adab[:, l, :]
    P.dma(bias, W["ada_b%d" % l].rearrange("(k p) -> p k", p=128), slow=True)
    for part in range(3):
        ps = k.bank()
        for m in range(8):
            wt = k.adaw[k.ada_it % 2]
            k.ada_it += 1
            c0 = 1024 * part + 128 * m
            P.dma(wt, W["ada_w%d" % l][:, c0:c0 + 128].rearrange("(k p) n -> p k n", p=128), eng="pool")
            for kk in range(8):
                P.mm(ps[:, 2 * m:2 * m + 2], wt[:, kk, :], k.vecb[:, kk, :], start=(kk == 0), stop=(kk == 7))
        dst = k.mod[l][:, 8 * part:8 * part + 8, :]
        P.tt("dve", dst, ps[:, 0:16].rearrange("p (m v) -> p m v", v=2),
             bc_free(bias[:, 8 * part:8 * part + 8], 2, 2), ALU.add)
    sc = k.mod[l][:, 8:16, :]
    P.ts("dve", k.gmul[l], sc, 1.0, None, ALU.add)
    P.tt("dve", k.gmul[l], k.gmul[l], bc_free(k.gnorm[:, l, :], 2, 2), ALU.mult)


def rms_rstd(k, hT, wb, ncols, col0=0):
    P = k.P
    rstd = wb.alloc([128, ncols], F32)
    sq = wb.alloc([128, NCH, 512], BF16)
    c = 0
    while c < ncols:
        n = min(512, ncols - c)
        P.act(sq[:, :, 0:n], hT[:, :, col0 + c:col0 + c + n], AF.Square)
        ps = k.bank()
        for kk in range(NCH):
            P.mm(ps[:, 0:n], k.ones_b, sq[:, kk, 0:n], start=(kk == 0), stop=(kk == NCH - 1))
        P.act(rstd[:, c:c + n], ps[:, 0:n], AF.Sqrt, bias=k.epsb, scale=1.0 / D)
        P.recip(rstd[:, c:c + n], rstd[:, c:c + n])
        c += n
    return rstd


def layer(k, l, hreg):
    P, W = k.P, k.W
    other = k.R2 if hreg == k.R1 else k.R1
    hT = k.ar.view(hreg[0], [128, NCH, TT], F32)
    nT = k.ar.view(k.R3[0], [128, NCH, TT], BF16)
    wb = Bump(k.ar, other[0], other[1])
    rstd = rms_rstd(k, hT, wb, TT)
    tmp = [wb.alloc([128, TT], F32) for _ in range(2)]
    for kk in range(NCH):
        t = tmp[kk % 2]
        for (v, c0, n) in ((1, 0, C), (0, C, T)):
            P.stt("dve", t[:, c0:c0 + n], hT[:, kk, c0:c0 + n], k.gmul[l][:, kk, v:v + 1], rstd[:, c0:c0 + n],
                  ALU.mult, ALU.mult)
            P.act(nT[:, kk, c0:c0 + n], t[:, c0:c0 + n], AF.Identity, bias=k.mod[l][:, kk, v:v + 1], scale=1.0)
    for kk in range(NCH):
        P.dma(k.hspill[:, kk, :], hT[:, kk, :])
    if k.next_layer is not None:
        modulation_layer(k, k.next_layer)
    free = [(k.R1[0], k.R2[1])]
    abump = Bump(k.ar, k.R1[0], k.R1[1])
    wbump = Bump(k.ar, k.R2[0], k.R2[1])
    lat_only = (l == 3)
    if l == 0:
        chunks = mixer_rglru(k, nT, abump, wbump)
    elif l == 1:
        chunks = mixer_swa(k, nT, abump, wbump)
    elif l == 2:
        chunks = mixer_na(k, nT, abump, wbump)
    else:
        chunks = mixer_s5(k, nT, abump, wbump)
    newh = k.ar.view(k.R2[0], [128, NCH, TT], F32)
    sb = Bump(k.ar, k.R3[0], k.R3[1])
    hold = [sb.alloc([128, TT], F32) for _ in range(2)]
    nk = len(chunks)
    kp = chunks[0][0].shape[0]
    wo = [sb.alloc([kp, nk, 128], BF16) for _ in range(2)]
    for m in range(NCH):
        ho = hold[m % 2]
        P.dma(ho, k.hspill[:, m, :])
        wt = wo[m % 2]
        for ci, (a_ap, r0) in enumerate(chunks):
            pass
        r0s = [r0 for (_, r0) in chunks]
        assert all(r0s[i] == r0s[0] + i * kp for i in range(nk))
        P.dma(wt, W["w_out%d" % l][r0s[0]:r0s[0] + nk * kp, 128 * m:128 * (m + 1)].rearrange("(c p) n -> p c n", p=kp),
              eng="pool")
        for (c0, n) in COLCH:
            if lat_only and c0 < C:
                P.copy("pool", newh[:, m, c0:c0 + n], ho[:, c0:c0 + n])
                continue
            v = 1 if c0 < C else 0
            ps = k.bank()
            for ci, (a_ap, r0) in enumerate(chunks):
                P.mm(ps[:, 0:n], wt[:, ci, :], a_ap[:, c0:c0 + n], start=(ci == 0), stop=(ci == nk - 1))
            P.stt("dve", newh[:, m, c0:c0 + n], ps[:, 0:n], k.mod[l][:, 16 + m, v:v + 1], ho[:, c0:c0 + n],
                  ALU.mult, ALU.add)
    return k.R2


def finalize(k, hreg, out, final_norm):
    P, W = k.P, k.W
    hT = k.ar.view(hreg[0], [128, NCH, TT], F32)
    other = k.R2 if hreg == k.R1 else k.R1
    wb = Bump(k.ar, other[0], other[1])
    if final_norm:
        rstd = rms_rstd(k, hT, wb, T, col0=C)
        for kk in range(NCH):
            P.stt("dve", hT[:, kk, C:TT], hT[:, kk, C:TT], k.gnorm[:, 4, kk:kk + 1], rstd, ALU.mult, ALU.mult)
        j0 = 2
    else:
        j0 = 0
    ot = [wb.alloc([128, D], F32) for _ in range(3)]
    for j in range(j0, TT // 128):
        o = ot[j % 3]
        for half in range(2):
            ps = k.bank()
            for q in range(4):
                kk = half * 4 + q
                P.tr(ps[:, 128 * q:128 * (q + 1)], hT[:, kk, 128 * j:128 * (j + 1)], k.ident_f)
            P.copy("act" if half == 0 else "dve", o[:, 512 * half:512 * (half + 1)], ps)
        r = 128 * (j - j0)
        P.dma(out[r:r + 128, :], o)


def uraw_alias(k, uraw, n):
    return uraw[:, 0:n]


def mixer_rglru(k, nT, abump, wbump):
    P, W = k.P, k.W
    NB, BW = 16, 88
    aT = abump.alloc([BW, NB, TT], BF16)
    wb = wbump
    k.lb.reset()
    lb = k.lb
    cw = lb.alloc([BW, NB, 4], F32)
    cbias = lb.alloc([BW, NB], F32)
    gb = lb.alloc([BW, 2, 2, NB], F32)
    lam = lb.alloc([BW, 2, NB], F32)
    cl = lb.alloc([BW, 2, NB], F32)
    for j in range(4):
        P.dma(cw[:, :, j], W["conv_w0"][j].rearrange("(k p) -> p k", p=BW), slow=True)
    P.dma(cbias, W["conv_b0"].rearrange("(k p) -> p k", p=BW), slow=True)
    for d in range(2):
        P.dma(gb[:, 0, d], W["lru_ba0"][d].rearrange("(k p) -> p k", p=BW), slow=True)
        P.dma(gb[:, 1, d], W["lru_bx0"][d].rearrange("(k p) -> p k", p=BW), slow=True)
        P.dma(lam[:, d], W["lru_lam0"][d].rearrange("(k p) -> p k", p=BW), slow=True)
    P.act(cl, lam, AF.Exp, scale=-1.0)
    P.act(cl, cl, AF.Ln, bias=k.oneb[0:BW], scale=1.0)
    P.ts("dve", cl, cl, -8.0, None, ALU.mult)
    wa = lb.alloc([BW, 32, BW], BF16)
    wx = lb.alloc([BW, 32, BW], BF16)
    P.dma(wa, W["lru_wa0"].rearrange("d k i j -> i (d k) j"), eng="pool")
    P.dma(wx, W["lru_wx0"].rearrange("d k i j -> i (d k) j"), eng="pool")
    win = [wb.alloc([128, NCH, 2, BW], BF16) for _ in range(2)]
    PADL = 2
    UW = TT + 8
    OC, OL = 2, 2 + C + 3
    uraw = wb.alloc([BW, UW], F32)
    P.memset("pool", uraw, 0.0)
    u = wb.alloc([BW, TT], F32)
    ub = wb.alloc([BW, TT], BF16)
    sg = wb.alloc([BW, TT], BF16)
    ta = wb.alloc([BW, TT], F32)
    tb = wb.alloc([BW, TT], F32)
    tc = wb.alloc([BW, TT], F32)
    h0 = wb.alloc([BW, TT], F32)
    h1 = tc
    tcb = uraw_alias(k, uraw, TT)
    def upos(c0):
        return OC + c0 if c0 < C else OL + (c0 - C)

    for b in range(NB):
        wt = win[b % 2]
        P.dma(wt[:, :, 0, :], W["w_in0"][:, BW * b:BW * (b + 1)].rearrange("(k p) n -> p k n", p=128), eng="pool")
        P.dma(wt[:, :, 1, :], W["w_in0"][:, 1408 + BW * b:1408 + BW * (b + 1)].rearrange("(k p) n -> p k n", p=128), eng="pool")

        if b > 0:
            P.memset("pool", uraw[:, 0:OC], 0.0)
            P.memset("pool", uraw[:, OC + C:OL], 0.0)

        def conv_chunk(c0, n):
            o = upos(c0)
            P.ts("dve", u[:, c0:c0 + n], uraw[:, o - 2:o - 2 + n], cw[:, b, 0:1], cbias[:, b:b + 1], ALU.mult, ALU.add)
            for j in range(1, 4):
                P.stt("dve", u[:, c0:c0 + n], uraw[:, o - 2 + j:o - 2 + j + n], cw[:, b, j:j + 1], u[:, c0:c0 + n],
                      ALU.mult, ALU.add)
            P.copy("act", ub[:, c0:c0 + n], u[:, c0:c0 + n])

        for ci, (c0, n) in enumerate(COLCH):
            ps = k.bank()
            for kk in range(NCH):
                P.mm(ps[0:BW, 0:n], wt[:, kk, 0, :], nT[:, kk, c0:c0 + n], start=(kk == 0), stop=(kk == NCH - 1))
            o = upos(c0)
            P.copy("act", uraw[:, o:o + n], ps[0:BW, 0:n])
            ps2 = k.bank()
            for kk in range(NCH):
                P.mm(ps2[0:BW, 0:n], wt[:, kk, 1, :], nT[:, kk, c0:c0 + n], start=(kk == 0), stop=(kk == NCH - 1))
            P.act(sg[:, c0:c0 + n], ps2[0:BW, 0:n], AF.Silu)
            if ci == 0:
                conv_chunk(c0, n)
            elif ci >= 2:
                conv_chunk(*COLCH[ci - 1])
        conv_chunk(*COLCH[-1])
        for d in range(2):
            hd = h0 if d == 0 else h1
            order = list(range(5)) if d == 0 else [0, 4, 3, 2, 1]
            prev_c = None
            for oi, ci in enumerate(order):
                c0, n = COLCH[ci]
                sl = slice(c0, c0 + n)
                ps = k.bank()
                P.mm(ps[0:BW, 0:n], wa[:, d * NB + b, :], ub[:, sl])
                P.act(ta[:, sl], ps[0:BW, 0:n], AF.Sigmoid, bias=gb[:, 0, d, b:b + 1], scale=1.0)
                ps2 = k.bank()
                P.mm(ps2[0:BW, 0:n], wx[:, d * NB + b, :], ub[:, sl])
                P.act(tb[:, sl], ps2[0:BW, 0:n], AF.Sigmoid, bias=gb[:, 1, d, b:b + 1], scale=1.0)
                P.act(ta[:, sl], ta[:, sl], AF.Exp, scale=cl[:, d, b:b + 1])
                P.tt("dve", tc[:, sl], ta[:, sl], ta[:, sl], ALU.mult) if d == 0 else P.tt("dve", tcb[:, sl], ta[:, sl], ta[:, sl], ALU.mult)
                tcc = tc if d == 0 else tcb
                P.act(tcc[:, sl], tcc[:, sl], AF.Sqrt, bias=k.oneb[0:BW], scale=-1.0)
                P.tt("pool", tb[:, sl], tb[:, sl], u[:, sl], ALU.mult)
                P.tt("dve", tb[:, sl], tb[:, sl], tcc[:, sl], ALU.mult)
                if d == 0:
                    init = 0.0 if oi == 0 else hd[:, c0 - 1:c0]
                    P.scan("dve", hd[:, sl], ta[:, sl], tb[:, sl], init)
                else:
                    if oi == 0:
                        init = 0.0
                    elif oi == 1:
                        init = hd[:, 0:1]
                    else:
                        init = hd[:, c0 + n:c0 + n + 1]
                    P.scan("dve", rev(hd[:, sl]), rev(ta[:, sl]), rev(tb[:, sl]), init)
        for (c0, n) in COLCH:
            sl = slice(c0, c0 + n)
            P.tt("dve", h0[:, sl], h0[:, sl], h1[:, sl], ALU.add)
            P.tt("dve", aT[:, b, sl], h0[:, sl], sg[:, sl], ALU.mult)
    return [(aT[:, b, :], BW * b) for b in range(NB)]


NEG = -30000.0


def na_w0(r):
    return min(max(r - 4, 0), 24)


def na_plan():
    variants = {}
    plan = []
    for i in range(16):
        r = 2 * i
        lo = na_w0(r) // 2
        hi = (na_w0(r + 1) + 7) // 2
        lst = []
        for kt in range(lo, hi + 1):
            key = (2 * kt - r, na_w0(r) - r, na_w0(r + 1) - (r + 1))
            if key not in variants:
                variants[key] = len(variants)
            lst.append((kt, variants[key]))
        plan.append(lst)
    return variants, plan


def mixer_swa(k, nT, abump, wbump):
    import os
    st = os.environ.get("KN_SWA_STRICT", "0") == "1"
    if st:
        k.P.begin_strict()
    r = mixer_attn(k, nT, "swa")
    if st:
        k.P.end_strict()
    return r


def mixer_na(k, nT, abump, wbump):
    import os
    st = os.environ.get("KN_NA_STRICT", "0") == "1"
    if st:
        k.P.begin_strict()
    r = mixer_attn(k, nT, "na")
    if st:
        k.P.end_strict()
    return r


def mixer_attn(k, nT, kind):
    P, W = k.P, k.W
    swa = kind == "swa"
    l = 1 if swa else 2
    win = W["w_in%d" % l]
    aT = k.ar.view(k.R1[0], [128, NCH, TT], BF16)
    wb = Bump(k.ar, k.R1[0] + TT * NCH // 2, k.R2[1])
    k.lb.reset()
    lb = k.lb
    NT = TT // 128
    if swa:
        pswap = lb.alloc([128, 128], BF16)
        m_prev = lb.alloc([128, 128], BF16)
        m_next = lb.alloc([128, 128], BF16)
        P.dma(pswap, W["pswap"])
        P.dma(m_prev, W["m_prev"])
        P.dma(m_next, W["m_next"])
        cosT = wb.alloc([128, T], F32)
        sinT = wb.alloc([128, T], F32)
        P.dma(cosT, W["cosT"])
        P.dma(sinT, W["sinT"])
        sk = lb.alloc([128, 16], F32)
        P.dma(sk, W["sink1"].rearrange("(o h) -> o h", o=1).partition_broadcast(128), slow=True)
        P.act(sk, sk, AF.Exp)
        sinkcol = lb.alloc([128, 8], F32)
        skv = sk.rearrange("p (a b) -> p a b", b=2)
        P.copy("pool", sinkcol[0:64, :], skv[0:64, :, 0])
        P.copy("pool", sinkcol[64:128, :], skv[64:128, :, 1])
        qoff, koff, voff, goff = 0, 1024, 1280, 1536
    else:
        variants, plan = na_plan()
        NV = len(variants)
        qoff, koff, voff, goff = 0, 1024, 2048, 3072
    sets = []
    for _ in range(2):
        st = {}
        st["w"] = wb.alloc([128, NCH, 4, 128], BF16)
        st["qT"] = wb.alloc([128, TT], BF16)
        st["kT"] = wb.alloc([128, TT], BF16)
        st["V"] = wb.alloc([128, NT, 128], BF16)
        st["sg"] = wb.alloc([128, TT], BF16)
        if swa:
            st["qr"] = wb.alloc([128, T], BF16)
            st["kr"] = wb.alloc([128, T], BF16)
        else:
            st["tab0"] = wb.alloc([128, NV, 128], BF16)
            st["tab1"] = wb.alloc([128, NV, 128], BF16)
        sets.append(st)
    t1s = [wb.alloc([128, 512], F32) for _ in range(2)]
    t2s = [wb.alloc([128, 512], F32) for _ in range(2)]
    qfs = [wb.alloc([128, 512], F32) for _ in range(2)]
    PTs = [wb.alloc([128, 8, 128], BF16) for _ in range(3)]
    rdens = [wb.alloc([128, 128], F32) for _ in range(2)]
    oas = [wb.alloc([128, 128], F32) for _ in range(2)]
    cnt = {"t": 0, "pt": 0, "r": 0, "od": 0, "ip": 0}

    def bank_ip():
        b_ = 6 + cnt["ip"] % 2
        cnt["ip"] += 1
        return k.ps[:, 512 * b_:512 * (b_ + 1)]

    def bank_od():
        b_ = 4 + cnt["od"] % 2
        cnt["od"] += 1
        return k.ps[:, 512 * b_:512 * (b_ + 1)]

    def wcols(dst, c0, n):
        P.dma(dst, win[:, c0:c0 + n].rearrange("(kk p) n -> p kk n", p=128), eng="pool")

    import os
    SKIP = os.environ.get('KN_SKIP', '').split(',')
    def inproj_units(hp):
        st = sets[hp % 2]
        w = st["w"]
        units = []

        def u_weights():
            wcols(w[:, :, 0, :], qoff + 128 * hp, 128)
            if swa:
                kvh = hp // 2
                for e in range(2):
                    wcols(w[:, :, 1, 64 * e:64 * e + 64], koff + 64 * kvh, 64)
                    wcols(w[:, :, 2, 64 * e:64 * e + 64], voff + 64 * kvh, 64)
            else:
                wcols(w[:, :, 1, :], koff + 128 * hp, 128)
                wcols(w[:, :, 2, :], voff + 128 * hp, 128)
                for e in range(2):
                    for v0 in range(0, NV, 3):
                        v1 = min(NV, v0 + 3)
                        P.dma(st["tab%d" % e][:, v0:v1, :], W["natab"][2 * hp + e, v0:v1].rearrange("v p q -> p v q"), eng="pool")
            wcols(w[:, :, 3, :], goff + 128 * hp, 128)
        units.append(u_weights)

        def mk_q(c0, n):
            def f():
                ps = bank_ip()
                for kk in range(NCH):
                    P.mm(ps[:, 0:n], w[:, kk, 0, :], nT[:, kk, c0:c0 + n], start=(kk == 0), stop=(kk == NCH - 1))
                if swa and c0 >= C:
                    qf = qfs[cnt["t"] % 2]
                    t1 = t1s[cnt["t"] % 2]
                    t2 = t2s[cnt["t"] % 2]
                    cnt["t"] += 1
                    lc = c0 - C
                    P.act(qf[:, 0:n], ps[:, 0:n], AF.Copy, scale=0.125)
                    P.act(st["qT"][:, c0:c0 + n], ps[:, 0:n], AF.Copy, scale=0.125)
                    P.tt("dve", t1[:, 0:n], qf[:, 0:n], cosT[:, lc:lc + n], ALU.mult)
                    ps2 = bank_ip()
                    P.mm(ps2[:, 0:n], pswap, st["qT"][:, c0:c0 + n])
                    P.tt("dve", t2[:, 0:n], ps2[:, 0:n], sinT[:, lc:lc + n], ALU.mult)
                    P.tt("pool", st["qr"][:, lc:lc + n], t1[:, 0:n], t2[:, 0:n], ALU.add)
                else:
                    P.act(st["qT"][:, c0:c0 + n], ps[:, 0:n], AF.Copy, scale=0.125)
            return f

        def mk_k(c0, n):
            def f():
                ps = bank_ip()
                for kk in range(NCH):
                    P.mm(ps[:, 0:n], w[:, kk, 1, :], nT[:, kk, c0:c0 + n], start=(kk == 0), stop=(kk == NCH - 1))
                if swa and c0 >= C:
                    qf = qfs[cnt["t"] % 2]
                    t1 = t1s[cnt["t"] % 2]
                    t2 = t2s[cnt["t"] % 2]
                    cnt["t"] += 1
                    lc = c0 - C
                    P.copy("act", qf[:, 0:n], ps[:, 0:n])
                    P.copy("act", st["kT"][:, c0:c0 + n], ps[:, 0:n])
                    P.tt("dve", t1[:, 0:n], qf[:, 0:n], cosT[:, lc:lc + n], ALU.mult)
                    ps2 = bank_ip()
                    P.mm(ps2[:, 0:n], pswap, st["kT"][:, c0:c0 + n])
                    P.tt("dve", t2[:, 0:n], ps2[:, 0:n], sinT[:, lc:lc + n], ALU.mult)
                    P.tt("pool", st["kr"][:, lc:lc + n], t1[:, 0:n], t2[:, 0:n], ALU.add)
                else:
                    P.copy("act", st["kT"][:, c0:c0 + n], ps[:, 0:n])
            return f

        def mk_g(c0, n):
            def f():
                ps = bank_ip()
                for kk in range(NCH):
                    P.mm(ps[:, 0:n], w[:, kk, 3, :], nT[:, kk, c0:c0 + n], start=(kk == 0), stop=(kk == NCH - 1))
                P.act(st["sg"][:, c0:c0 + n], ps[:, 0:n], AF.Silu)
            return f

        def mk_v(j4):
            def f():
                nj = min(4, NT - j4)
                ps = bank_ip()
                for jj in range(nj):
                    j = j4 + jj
                    for kk in range(NCH):
                        P.mm(ps[:, 128 * jj:128 * (jj + 1)], nT[:, kk, 128 * j:128 * (j + 1)], w[:, kk, 2, :],
                             start=(kk == 0), stop=(kk == NCH - 1))
                P.copy("act", st["V"][:, j4:j4 + nj, :], ps[:, 0:128 * nj].rearrange("p (j c) -> p j c", c=128))
            return f
        for (c0, n) in COLCH:
            units.append(mk_q(c0, n))
            units.append(mk_k(c0, n))
            units.append(mk_g(c0, n))
        for j4 in range(0, NT, 4):
            units.append(mk_v(j4))
        return units

    def stage1(hp, qt, e):
        st = sets[hp % 2]
        is_ctx = qt < 2
        pr = slice(64 * e, 64 * e + 64)
        tiles = []
        qraw = st["qT"][pr, 128 * qt:128 * (qt + 1)]
        for cj in range(2):
            tiles.append((st["kT"][pr, 128 * cj:128 * (cj + 1)], None, st["V"][:, cj, 64 * e:64 * e + 64], qraw))
        if not is_ctx:
            i = qt - 2
            if swa:
                qrot = st["qr"][pr, 128 * i:128 * (i + 1)]
                for j, tab in ((i - 1, m_prev), (i, None), (i + 1, m_next)):
                    if 0 <= j < 16:
                        tiles.append((st["kr"][pr, 128 * j:128 * (j + 1)], tab,
                                      st["V"][:, 2 + j, 64 * e:64 * e + 64], qrot))
            else:
                for (kt, vi) in plan[i]:
                    tiles.append((st["kT"][pr, C + 128 * kt:C + 128 * (kt + 1)], st["tab%d" % e][:, vi, :],
                                  st["V"][:, 2 + kt, 64 * e:64 * e + 64], qraw))
        nt = len(tiles)
        ps2 = k.ps[:, 1024 * e:1024 * (e + 1)]
        for t, (kT_, tab, V_, q_) in enumerate(tiles):
            o = ps2[:, 128 * t:128 * (t + 1)]
            P.mm(o, kT_, q_, start=True, stop=(tab is None))
            if tab is not None:
                P.mm(o, k.ident_b, tab, start=False, stop=True)
        PT = PTs[cnt["pt"] % 3]
        cnt["pt"] += 1
        P.act(PT[:, 0:nt, :], ps2[:, 0:128 * nt].rearrange("p (t c) -> p t c", c=128), AF.Exp)
        return (tiles, PT)

    def stage2(hp, qt, e, s1):
        st = sets[hp % 2]
        tiles, PT = s1
        nt = len(tiles)
        pr = slice(64 * e, 64 * e + 64)
        od = k.ps[:, 2048 + 512 * e:2048 + 512 * (e + 1)]
        for t, (kT_, tab, V_, q_) in enumerate(tiles):
            P.mm(od[pr, 0:128], V_, PT[:, t, :], start=(t == 0), stop=(t == nt - 1), tile_position=(0, 64 * e))
        for t in range(nt):
            P.mm(od[pr, 128:256], k.ones_b[:, 0:64], PT[:, t, :], start=(t == 0), stop=(t == nt - 1),
                 tile_position=(0, 64 * e))
        rden = rdens[cnt["r"] % 2]
        oa = oas[cnt["r"] % 2]
        cnt["r"] += 1
        if swa:
            P.ts("dve", rden[pr, :], od[pr, 128:256], sinkcol[pr, hp:hp + 1], None, ALU.add)
            P.recip(rden[pr, :], rden[pr, :])
        else:
            P.recip(rden[pr, :], od[pr, 128:256])
        P.tt("dve", oa[pr, :], od[pr, 0:128], rden[pr, :], ALU.mult)
        P.tt("pool", aT[pr, hp, 128 * qt:128 * (qt + 1)], oa[pr, :], st["sg"][pr, 128 * qt:128 * (qt + 1)], ALU.mult)

    import os
    PIPE = os.environ.get("KN_PIPE", "1") == "1"
    for u in inproj_units(0):
        u()
    for hp in range(8):
        nxt = inproj_units(hp + 1) if hp + 1 < 8 else []
        items = [(qt, e) for qt in range(NT) for e in range(2)]
        prev = None
        ui = 0
        for idx, (qt, e) in enumerate(items):
            s1 = stage1(hp, qt, e)
            if PIPE:
                if prev is not None:
                    stage2(hp, prev[0], prev[1], prev[2])
                prev = (qt, e, s1)
            else:
                stage2(hp, qt, e, s1)
            want = (len(nxt) * (idx + 1)) // len(items)
            while ui < want:
                nxt[ui]()
                ui += 1
        if PIPE and prev is not None:
            stage2(hp, prev[0], prev[1], prev[2])
        while ui < len(nxt):
            nxt[ui]()
            ui += 1
    return [(aT[:, c, :], 128 * c) for c in range(NCH)]


TWO_PI = 2.0 * math.pi
INV2PI = 1.0 / TWO_PI
PI_LO = 3.1415925


def trig_tables(k, eng, x, k32, kf, out_sin, out_cos):
    P = k.P
    P.ts("dve", kf, x, INV2PI, None, ALU.mult)
    P.copy("dve", k32, kf)
    P.copy("dve", kf, k32)
    P.stt("dve", kf, kf, -TWO_PI, x, ALU.mult, ALU.add)
    P.ts("dve", kf, kf, PI_LO, -PI_LO, ALU.min, ALU.max)
    P.act(out_sin, kf, AF.Sin)
    P.act(x, kf, AF.Abs)
    P.act(out_cos, x, AF.Sin, bias=k.halfpi, scale=-1.0)


def cmul(k, eng, out_re, out_im, are, aim, bre, bim, t1, t2, neg_im=False):
    P = k.P
    P.tt(eng, t1, are, bre, ALU.mult)
    P.tt(eng, t2, aim, bim, ALU.mult)
    P.tt(eng, out_re, t1, t2, ALU.subtract)
    P.tt(eng, t1, are, bim, ALU.mult)
    P.tt(eng, t2, aim, bre, ALU.mult)
    if neg_im:
        P.ts(eng, t1, t1, -1.0, None, ALU.mult)
        P.tt(eng, out_im, t1, t2, ALU.subtract)
    else:
        P.tt(eng, out_im, t1, t2, ALU.add)


def mixer_s5(k, nT, abump, wbump):
    P, W = k.P, k.W
    win = W["w_in3"]
    base = k.R1[0]
    yg = k.ar.view(base, [128, NCH, T], BF16)
    aT = k.ar.view(base + 8192, [128, NCH, TT], BF16)
    wb = Bump(k.ar, base + 8192, k.R2[1])
    k.lb.reset()
    lb = k.lb
    NCOL = 320
    KF = lb.alloc([128, 4, 8], F32)
    SGN = lb.alloc([128, 1], F32)
    MSK = lb.alloc([128, 2, 128], F32)
    IOTA = lb.alloc([128, NCOL], F32)
    WD = lb.alloc([128, 8, 240], BF16)
    V4 = lb.alloc([128, 4, 240], BF16)
    dcol = lb.alloc([128, NCH], F32)
    gbias = lb.alloc([128, NCH], F32)
    for dst, nm in ((KF, "s5_kf"), (SGN, "s5_sgn"), (MSK, "s5_msk"), (IOTA, "s5_iota"), (WD, "s5_wd"), (V4, "s5_v4")):
        P.dma(dst, W[nm])
    P.dma(dcol, W["s5_d3"].rearrange("(k p) -> p k", p=128), slow=True)
    P.dma(gbias, W["glu_b3"].rearrange("(k p) -> p k", p=128), slow=True)
    uTf = wb.alloc([128, TT], F32)
    uTb = wb.alloc([128, 8 * NCOL], BF16)
    ytmp = wb.alloc([128, T], F32)
    wu = wb.alloc([128, NCH, 128], BF16)
    araw = wb.alloc([8, 2, 128], F32)
    A = wb.alloc([128, 2, 8], F32)
    ldt = wb.alloc([128, 8], F32)
    ar_ = wb.alloc([128, 8], F32)
    th = wb.alloc([128, 8], F32)
    rho8 = wb.alloc([128, 8], F32)
    phis = wb.alloc([128, 8], F32)
    fx = wb.alloc([128, 4, 8, 8], F32)
    fk32 = wb.alloc([128, 4, 8, 8], I32)
    fkf = wb.alloc([128, 4, 8, 8], F32)
    fmag = wb.alloc([128, 4, 8, 8], F32)
    fsin = wb.alloc([128, 4, 8, 8], F32)
    fcos = wb.alloc([128, 4, 8, 8], F32)
    Tre = wb.alloc([128, 4, 8, 8], F32)
    Tim = wb.alloc([128, 4, 8, 8], F32)
    sm = [wb.alloc([128, 8], F32) for _ in range(6)]
    kap = wb.alloc([128, 2, 8], F32)
    braw = wb.alloc([128, 2, 8, 16], F32)
    Bb = wb.alloc([128, 2, 8, 16], F32)
    craw = wb.alloc([128, 2, 128], F32)
    Cm = wb.alloc([128, 2, 8, 16], F32)
    bt1 = wb.alloc([128, 8, 8, 16], F32)
    bt2 = wb.alloc([128, 8, 8, 16], F32)
    W2p = wb.alloc([128, 2, 8, 128], BF16)
    Zp = wb.alloc([128, 2, 8, 128], BF16)
    W3 = wb.alloc([128, 2, 8, 128], BF16)
    W2 = wb.alloc([128, 8, 2, 128], BF16)
    W1 = wb.alloc([128, 8, 2, 128], BF16)
    NG = 4
    off_r = wb.p
    rx = wb.alloc([128, NG, NCOL], F32)
    rk32 = wb.alloc([128, NG, NCOL], I32)
    gtmp = k.ar.view(off_r, [128, T], F32)
    rkf = wb.alloc([128, NG, NCOL], F32)
    rsin = wb.alloc([128, NG, NCOL], F32)
    rcos = wb.alloc([128, NG, NCOL], F32)
    U8 = [wb.alloc([128, NCOL], BF16) for _ in range(2)]
    Ssb = [wb.alloc([128, 2, NCOL], F32) for _ in range(2)]
    Gin = wb.alloc([128, 2, NCOL], F32)
    Gs = wb.alloc([128, 2, NCOL], F32)
    pt = [wb.alloc([128, NCOL], F32) for _ in range(2)]
    Hb = [wb.alloc([128, 2, NCOL + 2], BF16) for _ in range(2)]
    Y8 = [wb.alloc([128, 256], BF16) for _ in range(2)]
    for h in Hb:
        P.memset("pool", h, 0.0)
    rho_b = wb.alloc([128, NCOL], F32)

    def bankS():
        b = k.s5_rr
        k.s5_rr = (k.s5_rr + 1) % 4
        return k.ps[:, 512 * b:512 * (b + 1)]
    k.s5_rr = 0
    UL = k.ps[:, 2048:4096].rearrange("p (t c) -> p t c", t=8)

    def bc(ap, n, pos):
        return bc_free(ap, n, pos)

    import os
    STG = int(os.environ.get('KN_S5', '99'))
    for ch in range(NCH if STG >= 99 else 1):
        g0 = 8 * ch
        P.dma(wu, win[:, 128 * ch:128 * (ch + 1)].rearrange("(kk p) n -> p kk n", p=128), eng="pool")
        for (c0, n) in COLCH:
            ps = bankS()
            for kk in range(NCH):
                P.mm(ps[:, 0:n], wu[:, kk, :], nT[:, kk, c0:c0 + n], start=(kk == 0), stop=(kk == NCH - 1))
            P.copy("act", uTf[:, c0:c0 + n], ps[:, 0:n])
        P.copy("act", uTb[:, 0:TT], uTf)
        P.copy("act", uTb[:, TT:TT + C], uTf[:, 0:C])
        if STG < 2:
            continue
        for d in range(2):
            P.dma(araw[:, 0, 64 * d:64 * d + 64], W["s5_a_re3"][d, g0:g0 + 8, :])
            P.dma(araw[:, 1, 64 * d:64 * d + 64], W["s5_a_im3"][d, g0:g0 + 8, :])
            P.dma(ldt[64 * d:64 * d + 64, :],
                  W["s5_log_dt3"][d:d + 1, g0:g0 + 8].partition_broadcast(64), slow=True)
            P.dma(braw[64 * d:64 * d + 64, 0], W["s5_b_re3"][d, g0:g0 + 8].rearrange("g p j -> p g j"), slow=True)
            P.dma(braw[64 * d:64 * d + 64, 1], W["s5_b_im3"][d, g0:g0 + 8].rearrange("g p j -> p g j"), slow=True)
            P.dma(craw[:, 0, 64 * d:64 * d + 64], W["s5_c_re3"][d, g0:g0 + 8].rearrange("g i p -> (g i) p"))
            P.dma(craw[:, 1, 64 * d:64 * d + 64], W["s5_c_im3"][d, g0:g0 + 8].rearrange("g i p -> (g i) p"))
        ps = bankS()
        for x in range(2):
            P.tr(ps[:, 8 * x:8 * x + 8], araw[:, x, :], k.ident_f[0:8, 0:8])
        P.copy("act", A, ps[:, 0:16].rearrange("p (x g) -> p x g", x=2))
        ps = bankS()
        for x in range(2):
            P.tr(ps[:, 128 * x:128 * x + 128], craw[:, x, :], k.ident_f)
        P.copy("act", Cm, ps[:, 0:256].rearrange("p (x g i) -> p x g i", x=2, g=8))
        if STG < 3:
            continue
        P.act(ldt, ldt, AF.Exp)
        P.tt("dve", ar_, A[:, 0, :], ldt, ALU.mult)
        P.tt("dve", th, A[:, 1, :], ldt, ALU.mult)
        P.act(rho8, ar_, AF.Exp, scale=8.0)
        P.ts("dve", phis, th, 8.0, SGN[:, 0:1], ALU.mult, ALU.mult)
        arb = bc(bc(ar_, 4, 1), 8, 3)
        thb = bc(bc(th, 4, 1), 8, 3)
        kfb = bc(KF, 8, 2)
        P.tt("dve", fmag, arb, kfb, ALU.mult)
        P.act(fmag, fmag, AF.Exp)
        P.tt("dve", fx, thb, kfb, ALU.mult)
        trig_tables(k, "dve", fx, fk32, fkf, fsin, fcos)
        P.tt("dve", Tre, fmag, fcos, ALU.mult)
        P.tt("dve", Tim, fmag, fsin, ALU.mult)
        lre, lim = Tre[:, 3, :, 0], Tim[:, 3, :, 0]
        Are, Aim = A[:, 0, :], A[:, 1, :]
        nre, den, t1_, t2_, rd = sm[0], sm[1], sm[2], sm[3], sm[4]
        P.ts("dve", nre, lre, -1.0, None, ALU.add)
        P.tt("dve", den, Are, Are, ALU.mult)
        P.tt("dve", t1_, Aim, Aim, ALU.mult)
        P.tt("dve", den, den, t1_, ALU.add)
        P.recip(rd, den)
        P.tt("dve", t1_, nre, Are, ALU.mult)
        P.tt("dve", t2_, lim, Aim, ALU.mult)
        P.tt("dve", t1_, t1_, t2_, ALU.add)
        P.tt("dve", kap[:, 0, :], t1_, rd, ALU.mult)
        P.tt("dve", t1_, lim, Are, ALU.mult)
        P.tt("dve", t2_, nre, Aim, ALU.mult)
        P.tt("dve", t1_, t1_, t2_, ALU.subtract)
        P.tt("dve", kap[:, 1, :], t1_, rd, ALU.mult)
        s1 = bt1[:, :, 0, :]
        s2 = bt2[:, :, 0, :]
        cmul(k, "dve", Bb[:, 0], Bb[:, 1], bc(kap[:, 0, :], 16, 2), bc(kap[:, 1, :], 16, 2),
             braw[:, 0], braw[:, 1], s1, s2)
        if STG < 4:
            continue
        def fam(f, x):
            t = Tre if x == 0 else Tim
            return bc(t[:, f], 16, 3)

        def vec(v, x):
            return bc(v[:, x], 8, 2)

        def o4(t, x):
            return t[:, x].rearrange("p g (s j) -> p g s j", s=8)
        cmul(k, "dve", o4(W2p, 0), o4(W2p, 1), fam(0, 0), fam(0, 1), vec(Bb, 0), vec(Bb, 1), bt1, bt2)
        cmul(k, "dve", o4(Zp, 0), o4(Zp, 1), fam(2, 0), fam(2, 1), vec(Bb, 0), vec(Bb, 1), bt1, bt2)
        cmul(k, "dve", o4(W3, 0), o4(W3, 1), fam(1, 0), fam(1, 1), vec(Cm, 0), vec(Cm, 1), bt1, bt2, neg_im=True)
        if STG < 5:
            continue
        for x in range(2 if os.environ.get('KN_5A', '1') == '1' else 0):
            ps = bankS()
            psb = ps.bitcast(BF16)
            for g8 in range(8):
                P.tr(psb[:, 128 * g8:128 * g8 + 128], W2p[:, x, g8, :], k.ident_b)
            P.copy("act", W2[:, :, x, :], psb[:, 0:1024].rearrange("p (g c) -> p g c", g=8))
        for gq in range(2):
            psd = [bankS(), bankS()]
            for gg in range(4):
                g8 = 4 * gq + gg
                for d in range(2):
                    o = psd[d][:, 128 * gg:128 * gg + 128]
                    pr = slice(64 * d, 64 * d + 64)
                    P.mm(o, Zp[pr, 0, g8, :], W3[pr, 0, g8, :], start=True, stop=False)
                    P.mm(o, Zp[pr, 1, g8, :], W3[pr, 1, g8, :], start=False, stop=True)
            for d in range(2):
                P.tt("dve", W1[:, 4 * gq:4 * gq + 4, d, :], psd[d].rearrange("p (g c) -> p g c", g=4),
                     bc(MSK[:, d, :], 4, 1), ALU.mult)
        if STG < 6:
            continue
        for g8 in range(8):
            if g8 % NG == 0:
                P.tt("dve", rx, bc(phis[:, g8:g8 + NG], NCOL, 2), bc(IOTA, NG, 1), ALU.mult)
                trig_tables(k, "dve", rx, rk32, rkf, rsin, rcos)
            gi = g8 % NG
            cs, sn = rcos[:, gi, :], rsin[:, gi, :]
            u8 = U8[g8 % 2]
            ps = bankS()
            for s_ in range(8):
                rhs = uTb.rearrange("p (c s) -> p s c", s=8)[:, s_, :]
                P.mm(ps[:, 0:NCOL], WD[:, g8, 112 - 16 * s_:112 - 16 * s_ + 128], rhs, start=(s_ == 0), stop=(s_ == 7))
            P.copy("act", u8, ps[:, 0:NCOL])
            ssb = Ssb[g8 % 2]
            for x in range(2):
                ps = bankS()
                P.mm(ps[:, 0:NCOL], W2[:, g8, x, :], u8)
                P.copy("act", ssb[:, x, :], ps[:, 0:NCOL])
            if STG < 7:
                continue
            P.tt("dve", pt[0], ssb[:, 0, :], cs, ALU.mult)
            P.tt("dve", pt[1], ssb[:, 1, :], sn, ALU.mult)
            P.tt("dve", Gin[:, 0, :], pt[0], pt[1], ALU.add)
            P.tt("dve", pt[0], ssb[:, 1, :], cs, ALU.mult)
            P.tt("dve", pt[1], ssb[:, 0, :], sn, ALU.mult)
            P.tt("dve", Gin[:, 1, :], pt[0], pt[1], ALU.subtract)
            if STG < 8:
                continue
            P.ts("dve", rho_b, IOTA, 0.0, rho8[:, g8:g8 + 1], ALU.mult, ALU.add)
            for x in range(2):
                P.scan("dve", Gs[0:64, x, 0:288], rho_b[0:64, 0:288], Gin[0:64, x, 0:288], 0.0)
                P.scan("dve", rev(Gs[64:128, x, 32:320]), rho_b[64:128, 0:288], rev(Gin[64:128, x, 32:320]), 0.0)
            if STG < 9:
                continue
            hb = Hb[g8 % 2]
            P.tt("dve", pt[0], Gs[:, 0, :], cs, ALU.mult)
            P.tt("dve", pt[1], Gs[:, 1, :], sn, ALU.mult)
            P.tt("dve", hb[0:64, 0, 1:289], pt[0][0:64, 0:288], pt[1][0:64, 0:288], ALU.subtract)
            P.tt("dve", hb[64:128, 0, 31:319], pt[0][64:128, 32:320], pt[1][64:128, 32:320], ALU.subtract)
            P.tt("dve", pt[0], Gs[:, 1, :], cs, ALU.mult)
            P.tt("dve", pt[1], Gs[:, 0, :], sn, ALU.mult)
            P.tt("dve", hb[0:64, 1, 1:289], pt[0][0:64, 0:288], pt[1][0:64, 0:288], ALU.add)
            P.tt("dve", hb[64:128, 1, 31:319], pt[0][64:128, 32:320], pt[1][64:128, 32:320], ALU.add)
            if STG < 10:
                continue
            ps = bankS()
            o = ps[:, 0:256]
            P.mm(o, W1[:, g8, 0, :], u8[:, 32:288], start=True, stop=False)
            P.mm(o, W1[:, g8, 1, :], u8[:, 32:288], start=False, stop=False)
            P.mm(o, W3[:, 0, g8, :], hb[:, 0, 32:288], start=False, stop=False)
            P.mm(o, W3[:, 1, g8, :], hb[:, 1, 32:288], start=False, stop=True)
            y8 = Y8[g8 % 2]
            P.copy("act", y8, o)
            for t in range(8):
                hh, tq = t // 4, t % 4
                P.mm(UL[:, t, :], V4[64 * hh:64 * hh + 64, tq, 112 - 16 * g8:112 - 16 * g8 + 128],
                     y8[64 * hh:64 * hh + 64, :], start=(g8 == 0 and t % 2 == 0), stop=(g8 == 7),
                     skip_group_check=True)
        if STG < 11:
            continue
        yv = ytmp.rearrange("p (c t) -> p t c", t=8)
        P.copy("dve", yv, UL)
        P.stt("dve", ytmp, uTf[:, C:TT], dcol[:, ch:ch + 1], ytmp, ALU.mult, ALU.add)
        P.tt("dve", gtmp, ytmp, ytmp, ALU.mult)
        P.ts("dve", gtmp, gtmp, 0.044715, 1.0, ALU.mult, ALU.add)
        P.tt("dve", gtmp, gtmp, ytmp, ALU.mult)
        P.act(gtmp, gtmp, AF.Sigmoid, scale=2.0 * math.sqrt(2.0 / math.pi))
        P.tt("dve", yg[:, ch, :], gtmp, ytmp, ALU.mult)
    if STG < 12:
        return [(aT[:, c, :], 128 * c) for c in range(NCH)]
    sb2 = Bump(k.ar, base + 8192 + TT * NCH // 2, k.R2[1])
    wg = [sb2.alloc([128, NCH, 128], BF16) for _ in range(2)]
    wi = [sb2.alloc([128, NCH, 128], BF16) for _ in range(2)]
    sgs = [sb2.alloc([128, 512], F32) for _ in range(2)]
    zs = [sb2.alloc([128, 512], F32) for _ in range(2)]
    sls = [sb2.alloc([128, 512], F32) for _ in range(2)]
    it = 0
    for m in range(NCH):
        P.dma(wg[m % 2], W["glu_w3"][:, 128 * m:128 * (m + 1)].rearrange("(kk p) n -> p kk n", p=128), eng="pool")
        P.dma(wi[m % 2], win[:, 1024 + 128 * m:1024 + 128 * (m + 1)].rearrange("(kk p) n -> p kk n", p=128), eng="pool")
        for q in range(4):
            c0 = 512 * q
            ps = k.bank()
            for kk in range(NCH):
                P.mm(ps, wg[m % 2][:, kk, :], yg[:, kk, c0:c0 + 512], start=(kk == 0), stop=(kk == NCH - 1))
            sg_, z_, sl_ = sgs[it % 2], zs[it % 2], sls[it % 2]
            it += 1
            P.act(sg_, ps, AF.Sigmoid, bias=gbias[:, m:m + 1], scale=1.0)
            P.tt("dve", z_, sg_, yg[:, m, c0:c0 + 512], ALU.mult)
            ps2 = k.bank()
            for kk in range(NCH):
                P.mm(ps2, wi[m % 2][:, kk, :], nT[:, kk, C + c0:C + c0 + 512], start=(kk == 0), stop=(kk == NCH - 1))
            P.act(sl_, ps2, AF.Silu)
            P.tt("dve", aT[:, m, C + c0:C + c0 + 512], z_, sl_, ALU.mult)
    return [(aT[:, c, :], 128 * c) for c in range(NCH)]


_CACHE = {}


def host_consts():
    c = {}
    c["ident_f"] = np.eye(128, dtype=np.float32)
    c["ident_b"] = np.eye(128, dtype=np.float32).astype(ml_dtypes.bfloat16)
    m = np.arange(128)
    sw = (m // 64) * 64 + ((m % 64) + 32) % 64
    ps = np.zeros((128, 128), np.float32)
    ps[sw, m] = 1.0
    c["pswap"] = ps.astype(ml_dtypes.bfloat16)
    kk = np.arange(128)[:, None]
    qq = np.arange(128)[None, :]
    c["m_prev"] = np.where(qq <= kk, 0.0, NEG).astype(np.float32).astype(ml_dtypes.bfloat16)
    c["m_next"] = np.where(kk <= qq, 0.0, NEG).astype(np.float32).astype(ml_dtypes.bfloat16)
    pos = np.arange(T)
    row = (pos // 64).astype(np.float32)
    col = (pos % 64).astype(np.float32)
    freqs = (10000.0 ** (-np.arange(16, dtype=np.float32) / 16)).astype(np.float32)
    ang = np.concatenate([row[:, None] * freqs, col[:, None] * freqs], axis=-1).astype(np.float32)
    d = (m % 64) % 32
    sign = np.where((m % 64) < 32, -1.0, 1.0).astype(np.float32)
    c["cosT"] = np.ascontiguousarray(np.cos(ang)[:, d].T).astype(np.float32)
    c["sinT"] = np.ascontiguousarray((np.sin(ang)[:, d] * sign[None, :]).T).astype(np.float32)
    kf = np.zeros((128, 4, 8), np.float32)
    sidx = np.arange(8, dtype=np.float32)
    kf[:64, 0] = 7 - sidx
    kf[64:, 0] = sidx
    kf[:64, 1] = sidx + 1
    kf[64:, 1] = 8 - sidx
    kf[:64, 2] = -1 - sidx
    kf[64:, 2] = sidx - 8
    kf[:, 3, 0] = 1.0
    c["s5_kf"] = kf
    sg = np.ones((128, 1), np.float32)
    sg[64:] = -1.0
    c["s5_sgn"] = sg
    ss = (np.arange(128) // 16)[:, None]
    tt_ = (np.arange(128) // 16)[None, :]
    msk = np.zeros((128, 2, 128), np.float32)
    msk[:, 0, :] = (ss <= tt_)
    msk[:, 1, :] = (ss >= tt_)
    c["s5_msk"] = msk
    c["s5_iota"] = np.broadcast_to(np.arange(320, dtype=np.float32)[None, :], (128, 320)).copy()
    wd = np.zeros((128, 8, 240), np.float32)
    for g8 in range(8):
        for j in range(16):
            wd[16 * g8 + j, g8, 112 + j] = 1.0
    c["s5_wd"] = wd.astype(ml_dtypes.bfloat16)
    v4 = np.zeros((128, 4, 240), np.float32)
    for hh in range(2):
        for tq in range(4):
            for i in range(16):
                v4[64 * hh + 16 * tq + i, tq, 112 + i] = 1.0
    c["s5_v4"] = v4.astype(ml_dtypes.bfloat16)
    return c


def na_tables(rpb):
    variants, plan = na_plan()
    NV = len(variants)
    out = np.full((16, NV, 128, 128), NEG, np.float32)
    qc = np.arange(64)
    cstart = np.clip(qc - 8, 0, 48)
    kc = np.arange(64)
    col_ok = (kc[:, None] >= cstart[None, :]) & (kc[:, None] < cstart[None, :] + 16)
    dxi = np.clip(kc[:, None] - qc[None, :] + 15, 0, 30)
    for (dyo, o0, o1), vi in variants.items():
        for a in range(2):
            for b in range(2):
                dy = dyo + a - b
                ob = o0 if b == 0 else o1
                if not (ob <= dy < ob + 8):
                    continue
                g = rpb[:, dy + 7, :][:, dxi]
                blk = np.where(col_ok[None], g, np.float32(NEG))
                out[:, vi, 64 * a:64 * a + 64, 64 * b:64 * b + 64] = blk
    return out


def run(inputs, n_layers=4, final_norm=True, cores=8, trace=False):
    key = (n_layers, final_norm)
    if key not in _CACHE:
        _CACHE[key] = build_program(n_layers, final_norm)
    nc = _CACHE[key]
    consts = host_consts()
    shared = {}
    in_maps = []
    for b in range(cores):
        m = {}
        for name, v in inputs.items():
            v = np.asarray(v)
            if name in ("x", "ctx", "c"):
                m[name] = np.ascontiguousarray(v[b])
            elif name == "rpb2":
                if "natab" not in shared:
                    shared["natab"] = na_tables(v.astype(np.float32))
                continue
            else:
                m[name] = v
        m.update(consts)
        m.update(shared)
        in_maps.append(m)
    res = run_bass_kernel_spmd(nc, in_maps, core_ids=list(range(cores)), **({'trace': True} if trace else {}))
    if trace:
        print('EXEC_NS', res.exec_time_ns)
    return np.stack([r["out"] for r in res.results], axis=0)


def kernel(**inputs):
    return run(inputs).astype(np.float32)
```

```python
import math
from contextlib import ExitStack
import numpy as np
import ml_dtypes
import concourse.bass as bass
import concourse.mybir as mybir
from concourse.ap import AP
from concourse.bass_utils import run_bass_kernel_spmd

F32 = mybir.dt.float32
BF16 = mybir.dt.bfloat16
I32 = mybir.dt.int32
AF = mybir.ActivationFunctionType
ALU = mybir.AluOpType
DSZ = {F32: 4, BF16: 2, I32: 4}

D = 1024
T = 2048
C = 256
TT = T + C
NCH = 8
EPS = 1e-6
EPOCH = 2000
ENG = ["pe", "act", "dve", "pool", "sp"]
import os as _os
SAME_SYNC = {"dve": _os.environ.get("KN_DVESYNC", "1") == "1", "act": False, "pool": False, "pe": False, "sp": False}
DVE_MIN = int(_os.environ.get("KN_DVEMIN", "0"))
NDMA = 12


def ap_region(ap):
    t = ap.tensor
    dims = [list(x) for x in ap.ap]
    off = int(ap.offset)
    sz = DSZ.get(ap.dtype, None)
    if sz is None:
        sz = mybir.dt.size(ap.dtype)
    if str(ap.space) == "DRAM":
        lo = hi = off
        for st, cn in dims:
            e = st * (cn - 1)
            lo += min(0, e)
            hi += max(0, e)
        return (t.name, 0, 1, lo * sz, (hi + 1) * sz)
    pst, pcn = dims[0]
    row = 1
    for s in t.shape[1:]:
        row *= s
    p0 = off // row
    f0 = off % row
    if pcn > 1:
        assert pst == row, (pst, row)
    lo = hi = f0
    for st, cn in dims[1:]:
        e = st * (cn - 1)
        lo += min(0, e)
        hi += max(0, e)
    assert lo >= 0 and hi < row, (lo, hi, row, ap.ap, off)
    return (t.name, p0, p0 + pcn, lo * sz, (hi + 1) * sz)


def _fsz(ap):
    n = 1
    for x in ap.shape[1:]:
        n *= x
    return n


def _vcost(eng, out):
    n = _fsz(out)
    if eng == "pool":
        return 500.0 + 5.0 * n
    return 60.0 + 1.05 * n


def _apsig(ap):
    if not DVE_MIN:
        return None
    dims = ap.ap
    if len(dims) != 2 or dims[1][0] != 1 or dims[1][1] < DVE_MIN:
        return None
    return (ap.tensor.name, int(ap.offset), tuple(tuple(x) for x in dims), str(ap.dtype))


class Prog:
    BUCKET = 1024

    def __init__(self, nc):
        self.nc = nc
        self.ops = []
        self.strict = False
        self.last_on = {}
        self.since = {e: [] for e in ENG}
        self.barrier = {}
        self.pending_start = set()
        self.bk = {}
        import os
        self.sched = os.environ.get('KN_SCHED', '1') == '1'
        self.K = int(os.environ.get('KN_K', '64'))

    def _buckets(self, name, b0, b1):
        d = self.bk.setdefault(name, {})
        B = self.BUCKET if name in ("arena", "psum") else 65536
        for i in range(b0 // B, (b1 - 1) // B + 1):
            e = d.get(i)
            if e is None:
                e = d[i] = ([], [])
            yield e

    def add(self, eng, fn, reads=(), writes=(), dma=False, cost=100.0, nbytes=0, tbl=None, order_after=()):
        oid = len(self.ops)
        deps = set()
        regs_r = [ap_region(a) for a in reads]
        regs_w = [ap_region(a) for a in writes]
        hard = set()
        for a_, (name, p0, p1, b0, b1) in zip(reads, regs_r):
            sig = _apsig(a_)
            for (wl, rl) in self._buckets(name, b0, b1):
                for e in wl:
                    if e[0] < p1 and p0 < e[1] and e[2] < b1 and b0 < e[3]:
                        deps.add(e[4])
                        if not (sig is not None and e[5] == sig):
                            hard.add(e[4])
        for (name, p0, p1, b0, b1) in regs_w:
            for (wl, rl) in self._buckets(name, b0, b1):
                for lst in (wl, rl):
                    for e in lst:
                        if e[0] < p1 and p0 < e[1] and e[2] < b1 and b0 < e[3]:
                            deps.add(e[4])
        deps.discard(oid)
        wsig = {}
        for a_, r_ in zip(writes, regs_w):
            wsig[r_] = _apsig(a_)
        odeps = set()
        if eng in self.barrier:
            odeps.add(self.barrier[eng])
        if eng in self.pending_start:
            odeps.update(self.since[eng])
            self.pending_start.discard(eng)
            self.since[eng] = []
            self.barrier[eng] = oid
        elif self.strict and eng in self.last_on:
            odeps.add(self.last_on[eng])
        odeps.update(order_after)
        odeps -= deps
        odeps.discard(oid)
        self.last_on[eng] = oid
        self.since[eng].append(oid)
        for (name, p0, p1, b0, b1) in regs_w:
            for (wl, rl) in self._buckets(name, b0, b1):
                wl[:] = [e for e in wl if not (p0 <= e[0] and e[1] <= p1 and b0 <= e[2] and e[3] <= b1)]
                rl[:] = [e for e in rl if not (p0 <= e[0] and e[1] <= p1 and b0 <= e[2] and e[3] <= b1)]
                wl.append((p0, p1, b0, b1, oid, wsig[(name, p0, p1, b0, b1)]))
        for (name, p0, p1, b0, b1) in regs_r:
            for (wl, rl) in self._buckets(name, b0, b1):
                rl.append((p0, p1, b0, b1, oid))
        self.ops.append(dict(id=oid, eng=eng, fn=fn, dma=dma, deps=deps, odeps=odeps, hard=hard, cost=float(cost), nbytes=nbytes, tbl=tbl))
        return oid

    def begin_strict(self):
        self.strict = True
        self.pending_start = set(ENG)

    def full_barrier(self, engines=("pe", "act", "dve", "pool")):
        return
        for e in engines:
            if e in self.last_on:
                self.barrier[e] = self.last_on[e]
            self.pending_start.add(e)

    def end_strict(self):
        self.strict = False
        for e in ENG:
            if e in self.last_on:
                self.barrier[e] = self.last_on[e]
            self.since[e] = []

    def _schedule(self):
        import bisect
        ops = self.ops
        n = len(ops)
        if not self.sched:
            order = {e: [] for e in ENG}
            for o in ops:
                order[o["eng"]].append(o["id"])
            return order
        users = [[] for _ in range(n)]
        ndep = [0] * n
        for o in ops:
            alld = o["deps"] | o["odeps"]
            ndep[o["id"]] = len(alld)
            for d in alld:
                users[d].append(o["id"])
        finish = [0.0] * n
        ready_t = [0.0] * n
        cand = {e: [] for e in ENG}
        for o in ops:
            if ndep[o["id"]] == 0:
                cand[o["eng"]].append(o["id"])
        free = {e: 0.0 for e in ENG}
        order = {e: [] for e in ENG}
        K = self.K
        cur_tbl = [None]
        last_dve = [-1]
        DVE_STALL = float(_os.environ.get("KN_DVESTALL", "250"))
        import os
        KE = {e: int(os.environ.get('KN_K_' + e, str(K))) for e in ENG}
        done = 0
        while done < n:
            best = None
            for e in ENG:
                c = cand[e]
                if not c:
                    continue
                bi, bt = None, None
                for oid in c[:KE[e]]:
                    t = max(ready_t[oid], free[e])
                    if e == "dve" and last_dve[0] in ops[oid]["deps"]:
                        t += DVE_STALL
                    if e == "act":
                        tb_ = ops[oid]["tbl"]
                        if tb_ is not None and cur_tbl[0] is not None and tb_ != cur_tbl[0]:
                            t += 1300.0
                    if bt is None or t < bt - 1e-9:
                        bi, bt = oid, t
                if best is None or bt < best[0] - 1e-9 or (abs(bt - best[0]) <= 1e-9 and bi < best[1]):
                    best = (bt, bi, e)
            bt, oid, e = best
            o = ops[oid]
            cand[e].remove(oid)
            order[e].append(oid)
            if e == "act" and o["tbl"] is not None:
                cur_tbl[0] = o["tbl"]
            if e == "dve":
                last_dve[0] = oid
            if o["dma"]:
                free[e] = bt + 60.0
                finish[oid] = bt + 2000.0 + o["nbytes"] / 120.0
            else:
                free[e] = bt + o["cost"]
                finish[oid] = free[e]
            o["start"] = bt
            done += 1
            for u in users[oid]:
                ndep[u] -= 1
                if finish[oid] > ready_t[u]:
                    ready_t[u] = finish[oid]
                if ndep[u] == 0:
                    bisect.insort(cand[ops[u]["eng"]], u)
        self.model_span = max(finish) if n else 0.0
        return order

    def _semkey(self, ek, idx):
        if isinstance(ek, tuple):
            return (ek, (idx - 1) // 100), 16 * ((idx - 1) % 100 + 1), 16
        return (ek, (idx - 1) // EPOCH), (idx - 1) % EPOCH + 1, 1

    def emit(self):
        nc = self.nc
        ops = self.ops
        order = self._schedule()
        ident = {}
        cnt = {}
        for e in ENG:
            for oid in order[e]:
                if not ops[oid]["dma"]:
                    cnt[e] = cnt.get(e, 0) + 1
                    ident[oid] = (e, cnt[e])
        dmas = [o for o in ops if o["dma"]]
        dmas.sort(key=lambda o: (o.get("start", o["id"]), o["id"]))
        rr = 0
        prev_on_vq = {}
        chain = {}
        for o in dmas:
            ek = ("dma", rr)
            rr = (rr + 1) % NDMA
            cnt[ek] = cnt.get(ek, 0) + 1
            ident[o["id"]] = (ek, cnt[ek])
            if ek in prev_on_vq:
                chain[o["id"]] = prev_on_vq[ek]
            prev_on_vq[ek] = o["id"]
        q = {e: [] for e in ENG}
        keys = set()
        for e in ENG:
            seen = {}
            for oid in order[e]:
                o = ops[oid]
                deps = set(o["deps"])
                if oid in chain:
                    deps.add(chain[oid])
                need = {}
                for d in deps:
                    dk, di = ident[d]
                    if dk == e and not o["dma"] and (not SAME_SYNC[e] or (DVE_MIN and d not in o["hard"])):
                        continue
                    if need.get(dk, 0) < di:
                        need[dk] = di
                waits = []
                for dk, di in need.items():
                    if seen.get(dk, 0) >= di:
                        continue
                    seen[dk] = di
                    waits.append((dk, di))
                    keys.add(self._semkey(dk, di)[0])
                ek, idx = ident[oid]
                keys.add(self._semkey(ek, idx)[0])
                q[e].append((waits, o["fn"], ek, idx))
        final_waits = [ident[oid] for oid in prev_on_vq.values()]
        with ExitStack() as st:
            sems = {}
            for i, k in enumerate(sorted(keys, key=str)):
                sems[k] = st.enter_context(nc.semaphore("s%d" % i))
            block = st.enter_context(nc.Block())

            def replay(ename):
                def body(e):
                    for waits, fn, ek, idx in q[ename]:
                        for dk, di in waits:
                            k, v, _ = self._semkey(dk, di)
                            e.wait_ge(sems[k], v)
                        ins = fn(e)
                        k, v, inc = self._semkey(ek, idx)
                        ins.then_inc(sems[k], inc)
                    if ename == "sp":
                        for dk, di in final_waits:
                            k, v, _ = self._semkey(dk, di)
                            e.wait_ge(sems[k], v)
                return body

            block.tensor(replay("pe"))
            block.scalar(replay("act"))
            block.vector(replay("dve"))
            block.gpsimd(replay("pool"))
            block.sync(replay("sp"))

    def mm(self, out, lhsT, rhs, start=True, stop=True, **kw):
        rd = [lhsT, rhs] + ([] if start else [out])
        reg = ap_region(out)
        banks = range(reg[3] // 2048, (reg[4] - 1) // 2048 + 1)
        if not hasattr(self, "last_mm_bank"):
            self.last_mm_bank = {}
        after = [self.last_mm_bank[b_] for b_ in banks if b_ in self.last_mm_bank]
        oid = self.add("pe", lambda e: e.matmul(out, lhsT, rhs, start=start, stop=stop, **kw), rd, [out],
                       cost=70.0 + 0.36 * _fsz(out), order_after=after)
        for b_ in banks:
            self.last_mm_bank[b_] = oid
        return oid

    def tr(self, out, in_, ident):
        reg = ap_region(out)
        banks = range(reg[3] // 2048, (reg[4] - 1) // 2048 + 1)
        if not hasattr(self, "last_mm_bank"):
            self.last_mm_bank = {}
        after = [self.last_mm_bank[b_] for b_ in banks if b_ in self.last_mm_bank]
        oid = self.add("pe", lambda e: e.transpose(out, in_, ident), [in_, ident], [out], cost=150.0, order_after=after)
        for b_ in banks:
            self.last_mm_bank[b_] = oid
        return oid

    def act(self, out, in_, func, bias=None, scale=None, eng="act"):
        rd = [in_]
        kw = {}
        if bias is not None:
            kw["bias"] = bias
            if not isinstance(bias, (int, float)):
                rd.append(bias)
        if scale is not None:
            kw["scale"] = scale
            if not isinstance(scale, (int, float)):
                rd.append(scale)
        tbl = None if func in (AF.Copy, AF.Identity) else str(func)
        return self.add(eng, lambda e: e.activation(out, in_, func, **kw), rd, [out], cost=220.0 + 0.72 * _fsz(out), tbl=tbl)

    def copy(self, eng, out, in_):
        if eng == "act":
            return self.add("act", lambda e: e.activation(out, in_, AF.Copy), [in_], [out], cost=220.0 + 0.72 * _fsz(out))
        return self.add(eng, lambda e: e.tensor_copy(out, in_), [in_], [out], cost=_vcost(eng, out))

    def tt(self, eng, out, in0, in1, op):
        return self.add(eng, lambda e: e.tensor_tensor(out, in0, in1, op), [in0, in1], [out], cost=_vcost(eng, out))

    def ts(self, eng, out, in0, s1, s2, op0, op1=None):
        rd = [in0] + [s for s in (s1, s2) if s is not None and not isinstance(s, (int, float))]
        if op1 is None:
            return self.add(eng, lambda e: e.tensor_scalar(out, in0, s1, None, op0), rd, [out], cost=_vcost(eng, out))
        return self.add(eng, lambda e: e.tensor_scalar(out, in0, s1, s2, op0, op1), rd, [out], cost=_vcost(eng, out))

    def stt(self, eng, out, in0, scalar, in1, op0, op1):
        rd = [in0, in1] + ([] if isinstance(scalar, (int, float)) else [scalar])
        return self.add(eng, lambda e: e.scalar_tensor_tensor(out, in0, scalar, in1, op0, op1), rd, [out], cost=_vcost(eng, out))

    def scan(self, eng, out, d0, d1, init, op0=ALU.mult, op1=ALU.add):
        rd = [d0, d1] + ([] if isinstance(init, (int, float)) else [init])
        return self.add(eng, lambda e: e.tensor_tensor_scan(out, d0, d1, init, op0, op1), rd, [out], cost=100.0 + 2.1 * _fsz(out))

    def memset(self, eng, out, val):
        return self.add(eng, lambda e: e.memset(out, val), [], [out], cost=_vcost(eng, out))

    def recip(self, out, in_):
        return self.add("dve", lambda e: e.reciprocal(out, in_), [in_], [out], cost=100.0 + 4.0 * _fsz(out))

    def dma(self, out, in_, eng="sp", slow=False):
        kw = {"allow_slow_non_contiguous": True} if slow else {}
        nb = _fsz(out) * out.shape[0] * DSZ.get(out.dtype, 4)
        return self.add(eng, lambda e: e.dma_start(out=out, in_=in_, **kw), [in_], [out], dma=True, nbytes=nb * (6 if slow else 1))


class Arena:
    def __init__(self, nc, name, words):
        self.t = nc.alloc_sbuf_tensor(name, [128, words], F32)
        self.words = words

    def view(self, off_w, shape, dt):
        n = 1
        for s in shape[1:]:
            n *= s
        nw = (n * DSZ[dt] + 3) // 4
        assert off_w + nw <= self.words, (off_w, nw, self.words)
        v = self.t[0:shape[0], off_w:off_w + nw]
        if dt != F32:
            v = v.bitcast(dt)
            v = v[:, 0:n]
        if len(shape) > 2:
            names = " ".join("d%d" % i for i in range(1, len(shape)))
            kw = {"d%d" % i: shape[i] for i in range(1, len(shape))}
            v = v.rearrange("p (%s) -> p %s" % (names, names), **kw)
        return v


class Bump:
    def __init__(self, arena, lo_w, hi_w):
        self.a = arena
        self.lo = lo_w
        self.hi = hi_w
        self.p = lo_w

    def alloc(self, shape, dt):
        n = 1
        for s in shape[1:]:
            n *= s
        nw = (n * DSZ[dt] + 3) // 4
        nw = (nw + 7) // 8 * 8
        assert self.p + nw <= self.hi, ("bump overflow", self.p, nw, self.hi)
        v = self.a.view(self.p, shape, dt)
        self.p += nw
        return v

    def reset(self):
        self.p = self.lo


def rev(ap):
    dims = [list(x) for x in ap.ap]
    assert len(dims) == 2
    st, cn = dims[1]
    return AP(ap.tensor, ap.offset + st * (cn - 1), [dims[0], [-st, cn]])


def bc_free(ap, n, axis_pos):
    dims = [list(x) for x in ap.ap]
    dims.insert(axis_pos, [0, n])
    return AP(ap.tensor, ap.offset, dims)


COLCH = [(0, 256)] + [(256 + 512 * i, 512) for i in range(4)]


class K:
    pass


def build_program(n_layers=4, final_norm=True):
    nc = bass.Bass("TRN2", target_bir_lowering=False)
    P = Prog(nc)
    k = K()
    k.nc, k.P = nc, P

    def din(name, shape, dt=F32):
        return nc.dram_tensor(name, list(shape), dt, kind="ExternalInput").ap()

    W = {}
    W["x"] = din("x", [T, D])
    W["ctx"] = din("ctx", [C, D])
    W["c"] = din("c", [D])
    W["c_ctx"] = din("c_ctx", [D])
    shapes = dict(
        ada_w0=[D, 3 * D], ada_b0=[3 * D], norm0=[D], w_in0=[D, 2816], conv_w0=[4, 1408], conv_b0=[1408],
        lru_wa0=[2, 16, 88, 88], lru_ba0=[2, 1408], lru_wx0=[2, 16, 88, 88], lru_bx0=[2, 1408],
        lru_lam0=[2, 1408], w_out0=[1408, D],
        ada_w1=[D, 3 * D], ada_b1=[3 * D], norm1=[D], w_in1=[D, 2560], sink1=[16], w_out1=[D, D],
        ada_w2=[D, 3 * D], ada_b2=[3 * D], norm2=[D], w_in2=[D, 4096], w_out2=[D, D],
        ada_w3=[D, 3 * D], ada_b3=[3 * D], norm3=[D], w_in3=[D, 2048],
        s5_a_re3=[2, 64, 64], s5_a_im3=[2, 64, 64], s5_log_dt3=[2, 64],
        s5_b_re3=[2, 64, 64, 16], s5_b_im3=[2, 64, 64, 16], s5_c_re3=[2, 64, 16, 64], s5_c_im3=[2, 64, 16, 64],
        s5_d3=[D], glu_w3=[D, D], glu_b3=[D], w_out3=[D, D], norm_f=[D],
    )
    for nm, sh in shapes.items():
        W[nm] = din(nm, sh)
    W["ident_f"] = din("ident_f", [128, 128])
    W["ident_b"] = din("ident_b", [128, 128], BF16)
    W["pswap"] = din("pswap", [128, 128], BF16)
    W["m_prev"] = din("m_prev", [128, 128], BF16)
    W["m_next"] = din("m_next", [128, 128], BF16)
    W["cosT"] = din("cosT", [128, T])
    W["sinT"] = din("sinT", [128, T])
    W["natab"] = din("natab", [16, len(na_plan()[0]), 128, 128])
    W["s5_kf"] = din("s5_kf", [128, 4, 8])
    W["s5_sgn"] = din("s5_sgn", [128, 1])
    W["s5_msk"] = din("s5_msk", [128, 2, 128])
    W["s5_iota"] = din("s5_iota", [128, 320])
    W["s5_wd"] = din("s5_wd", [128, 8, 240], BF16)
    W["s5_v4"] = din("s5_v4", [128, 4, 240], BF16)
    if final_norm:
        out = nc.dram_tensor("out", [T, D], F32, kind="ExternalOutput").ap()
    else:
        out = nc.dram_tensor("out", [TT, D], F32, kind="ExternalOutput").ap()
    k.W = W
    k.hspill = nc.dram_tensor("hspill", [128, NCH, TT], F32, kind="Internal").ap()

    AW = 52000
    ar = Arena(nc, "arena", AW)
    k.ar = ar
    k.ps = nc.alloc_psum_tensor("psum", [128, 4096], F32)
    k.bank_rr = 0

    def bank(n=1):
        if k.bank_rr + n > 8:
            k.bank_rr = 0
        b = k.bank_rr
        k.bank_rr = (k.bank_rr + n) % 8
        return k.ps[:, 512 * b:512 * (b + n)]
    k.bank = bank

    CW = 5900
    RW = TT * NCH
    k.cb = Bump(ar, 0, CW)
    k.R1 = (CW, CW + RW)
    k.R2 = (CW + RW, CW + 2 * RW)
    k.R3 = (CW + 2 * RW, AW)
    assert k.R3[1] - k.R3[0] >= TT * NCH // 2

    k.ident_f = k.cb.alloc([128, 128], F32)
    k.ident_b = k.cb.alloc([128, 128], BF16)
    k.ones_b = k.cb.alloc([128, 128], BF16)
    P.dma(k.ident_f, W["ident_f"])
    P.dma(k.ident_b, W["ident_b"])
    P.memset("pool", k.ones_b, 1.0)
    k.oneb = k.cb.alloc([128, 1], F32)
    k.epsb = k.cb.alloc([128, 1], F32)
    P.memset("pool", k.oneb, 1.0)
    P.memset("pool", k.epsb, EPS)
    k.halfpi = k.cb.alloc([128, 1], F32)
    P.memset("pool", k.halfpi, math.pi / 2)
    k.mod = [k.cb.alloc([128, 24, 2], F32) for _ in range(4)]
    k.gmul = [k.cb.alloc([128, NCH, 2], F32) for _ in range(4)]
    k.gnorm = k.cb.alloc([128, 5, NCH], F32)

    modulation_setup(k)
    k.lb = Bump(ar, k.cb.p, CW)
    phase0(k)
    hreg = k.R1
    import os
    lsel = os.environ.get("KN_LAYERS")
    llist = [int(x) for x in lsel.split(",")] if lsel else list(range(n_layers))
    k.llist = llist
    if llist:
        modulation_layer(k, llist[0])
    for li, l in enumerate(llist):
        P.full_barrier()
        k.next_layer = llist[li + 1] if li + 1 < len(llist) else None
        hreg = layer(k, l, hreg)
    P.full_barrier()
    finalize(k, hreg, out, final_norm)
    P.emit()
    return nc


def phase0(k):
    P, W = k.P, k.W
    k.hT = k.ar.view(k.R1[0], [128, NCH, TT], F32)
    wb = Bump(k.ar, k.R2[0], k.R2[1])
    tiles = [wb.alloc([128, D], F32) for _ in range(4)]
    for j in range(TT // 128):
        xt = tiles[j % 4]
        src = W["ctx"][128 * j:128 * (j + 1), :] if j < 2 else W["x"][128 * (j - 2):128 * (j - 1), :]
        P.dma(xt, src)
        for half in range(2):
            ps = k.bank()
            for q in range(4):
                kk = half * 4 + q
                P.tr(ps[:, 128 * q:128 * (q + 1)], xt[:, 128 * kk:128 * (kk + 1)], k.ident_f)
            dst = k.hT[:, half * 4:half * 4 + 4, 128 * j:128 * (j + 1)]
            P.copy("act" if half == 0 else "dve", dst, ps.rearrange("p (q c) -> p q c", q=4))


def modulation_setup(k):
    P, W = k.P, k.W
    k.vec = k.cb.alloc([128, NCH, 2], F32)
    k.vecb = k.cb.alloc([128, NCH, 2], BF16)
    P.dma(k.vec[:, :, 0], W["c"].rearrange("(k p) -> p k", p=128), slow=True)
    P.dma(k.vec[:, :, 1], W["c_ctx"].rearrange("(k p) -> p k", p=128), slow=True)
    P.act(k.vecb, k.vec, AF.Silu)
    names = ["norm0", "norm1", "norm2", "norm3", "norm_f"]
    for i, nm in enumerate(names):
        P.dma(k.gnorm[:, i, :], W[nm].rearrange("(k p) -> p k", p=128), slow=True)
    k.adaw = [k.cb.alloc([128, NCH, 128], BF16) for _ in range(2)]
    k.adab = k.cb.alloc([128, 4, 24], F32)
    k.ada_it = 0


def modulation_layer(k, l):
    P, W = k.P, k.W
    bias = k.adab[:, l, :]
    P.dma(bias, W["ada_b%d" % l].rearrange("(k p) -> p k", p=128), slow=True)
    for part in range(3):
        ps = k.bank()
        for m in range(8):
            wt = k.adaw[k.ada_it % 2]
            k.ada_it += 1
            c0 = 1024 * part + 128 * m
            P.dma(wt, W["ada_w%d" % l][:, c0:c0 + 128].rearrange("(k p) n -> p k n", p=128), eng="pool")
            for kk in range(8):
                P.mm(ps[:, 2 * m:2 * m + 2], wt[:, kk, :], k.vecb[:, kk, :], start=(kk == 0), stop=(kk == 7))
        dst = k.mod[l][:, 8 * part:8 * part + 8, :]
        P.tt("dve", dst, ps[:, 0:16].rearrange("p (m v) -> p m v", v=2),
             bc_free(bias[:, 8 * part:8 * part + 8], 2, 2), ALU.add)
    sc = k.mod[l][:, 8:16, :]
    P.ts("dve", k.gmul[l], sc, 1.0, None, ALU.add)
    P.tt("dve", k.gmul[l], k.gmul[l], bc_free(k.gnorm[:, l, :], 2, 2), ALU.mult)


def rms_rstd(k, hT, wb, ncols, col0=0):
    P = k.P
    rstd = wb.alloc([128, ncols], F32)
    sq = wb.alloc([128, NCH, 512], BF16)
    c = 0
    while c < ncols:
        n = min(512, ncols - c)
        P.act(sq[:, :, 0:n], hT[:, :, col0 + c:col0 + c + n], AF.Square)
        ps = k.bank()
        for kk in range(NCH):
            P.mm(ps[:, 0:n], k.ones_b, sq[:, kk, 0:n], start=(kk == 0), stop=(kk == NCH - 1))
        P.act(rstd[:, c:c + n], ps[:, 0:n], AF.Sqrt, bias=k.epsb, scale=1.0 / D)
        P.recip(rstd[:, c:c + n], rstd[:, c:c + n])
        c += n
    return rstd


def layer(k, l, hreg):
    P, W = k.P, k.W
    other = k.R2 if hreg == k.R1 else k.R1
    hT = k.ar.view(hreg[0], [128, NCH, TT], F32)
    nT = k.ar.view(k.R3[0], [128, NCH, TT], BF16)
    wb = Bump(k.ar, other[0], other[1])
    rstd = rms_rstd(k, hT, wb, TT)
    tmp = [wb.alloc([128, TT], F32) for _ in range(2)]
    for kk in range(NCH):
        t = tmp[kk % 2]
        for (v, c0, n) in ((1, 0, C), (0, C, T)):
            P.stt("dve", t[:, c0:c0 + n], hT[:, kk, c0:c0 + n], k.gmul[l][:, kk, v:v + 1], rstd[:, c0:c0 + n],
                  ALU.mult, ALU.mult)
            P.act(nT[:, kk, c0:c0 + n], t[:, c0:c0 + n], AF.Identity, bias=k.mod[l][:, kk, v:v + 1], scale=1.0)
    for kk in range(NCH):
        P.dma(k.hspill[:, kk, :], hT[:, kk, :])
    if k.next_layer is not None:
        modulation_layer(k, k.next_layer)
    free = [(k.R1[0], k.R2[1])]
    abump = Bump(k.ar, k.R1[0], k.R1[1])
    wbump = Bump(k.ar, k.R2[0], k.R2[1])
    lat_only = (l == 3)
    if l == 0:
        chunks = mixer_rglru(k, nT, abump, wbump)
    elif l == 1:
        chunks = mixer_swa(k, nT, abump, wbump)
    elif l == 2:
        chunks = mixer_na(k, nT, abump, wbump)
    else:
        chunks = mixer_s5(k, nT, abump, wbump)
    newh = k.ar.view(k.R2[0], [128, NCH, TT], F32)
    sb = Bump(k.ar, k.R3[0], k.R3[1])
    hold = [sb.alloc([128, TT], F32) for _ in range(2)]
    nk = len(chunks)
    kp = chunks[0][0].shape[0]
    wo = [sb.alloc([kp, nk, 128], BF16) for _ in range(2)]
    for m in range(NCH):
        ho = hold[m % 2]
        P.dma(ho, k.hspill[:, m, :])
        wt = wo[m % 2]
        for ci, (a_ap, r0) in enumerate(chunks):
            pass
        r0s = [r0 for (_, r0) in chunks]
        assert all(r0s[i] == r0s[0] + i * kp for i in range(nk))
        P.dma(wt, W["w_out%d" % l][r0s[0]:r0s[0] + nk * kp, 128 * m:128 * (m + 1)].rearrange("(c p) n -> p c n", p=kp),
              eng="pool")
        for (c0, n) in COLCH:
            if lat_only and c0 < C:
                P.copy("pool", newh[:, m, c0:c0 + n], ho[:, c0:c0 + n])
                continue
            v = 1 if c0 < C else 0
            ps = k.bank()
            for ci, (a_ap, r0) in enumerate(chunks):
                P.mm(ps[:, 0:n], wt[:, ci, :], a_ap[:, c0:c0 + n], start=(ci == 0), stop=(ci == nk - 1))
            P.stt("dve", newh[:, m, c0:c0 + n], ps[:, 0:n], k.mod[l][:, 16 + m, v:v + 1], ho[:, c0:c0 + n],
                  ALU.mult, ALU.add)
    return k.R2


def finalize(k, hreg, out, final_norm):
    P, W = k.P, k.W
    hT = k.ar.view(hreg[0], [128, NCH, TT], F32)
    other = k.R2 if hreg == k.R1 else k.R1
    wb = Bump(k.ar, other[0], other[1])
    if final_norm:
        rstd = rms_rstd(k, hT, wb, T, col0=C)
        for kk in range(NCH):
            P.stt("dve", hT[:, kk, C:TT], hT[:, kk, C:TT], k.gnorm[:, 4, kk:kk + 1], rstd, ALU.mult, ALU.mult)
        j0 = 2
    else:
        j0 = 0
    ot = [wb.alloc([128, D], F32) for _ in range(3)]
    for j in range(j0, TT // 128):
        o = ot[j % 3]
        for half in range(2):
            ps = k.bank()
            for q in range(4):
                kk = half * 4 + q
                P.tr(ps[:, 128 * q:128 * (q + 1)], hT[:, kk, 128 * j:128 * (j + 1)], k.ident_f)
            P.copy("act" if half == 0 else "dve", o[:, 512 * half:512 * (half + 1)], ps)
        r = 128 * (j - j0)
        P.dma(out[r:r + 128, :], o)


def uraw_alias(k, uraw, n):
    return uraw[:, 0:n]


def mixer_rglru(k, nT, abump, wbump):
    P, W = k.P, k.W
    NB, BW = 16, 88
    aT = abump.alloc([BW, NB, TT], BF16)
    wb = wbump
    k.lb.reset()
    lb = k.lb
    cw = lb.alloc([BW, NB, 4], F32)
    cbias = lb.alloc([BW, NB], F32)
    gb = lb.alloc([BW, 2, 2, NB], F32)
    lam = lb.alloc([BW, 2, NB], F32)
    cl = lb.alloc([BW, 2, NB], F32)
    for j in range(4):
        P.dma(cw[:, :, j], W["conv_w0"][j].rearrange("(k p) -> p k", p=BW), slow=True)
    P.dma(cbias, W["conv_b0"].rearrange("(k p) -> p k", p=BW), slow=True)
    for d in range(2):
        P.dma(gb[:, 0, d], W["lru_ba0"][d].rearrange("(k p) -> p k", p=BW), slow=True)
        P.dma(gb[:, 1, d], W["lru_bx0"][d].rearrange("(k p) -> p k", p=BW), slow=True)
        P.dma(lam[:, d], W["lru_lam0"][d].rearrange("(k p) -> p k", p=BW), slow=True)
    P.act(cl, lam, AF.Exp, scale=-1.0)
    P.act(cl, cl, AF.Ln, bias=k.oneb[0:BW], scale=1.0)
    P.ts("dve", cl, cl, -8.0, None, ALU.mult)
    wa = lb.alloc([BW, 32, BW], BF16)
    wx = lb.alloc([BW, 32, BW], BF16)
    P.dma(wa, W["lru_wa0"].rearrange("d k i j -> i (d k) j"), eng="pool")
    P.dma(wx, W["lru_wx0"].rearrange("d k i j -> i (d k) j"), eng="pool")
    win = [wb.alloc([128, NCH, 2, BW], BF16) for _ in range(2)]
    PADL = 2
    UW = TT + 8
    OC, OL = 2, 2 + C + 3
    uraw = wb.alloc([BW, UW], F32)
    P.memset("pool", uraw, 0.0)
    u = wb.alloc([BW, TT], F32)
    ub = wb.alloc([BW, TT], BF16)
    sg = wb.alloc([BW, TT], BF16)
    ta = wb.alloc([BW, TT], F32)
    tb = wb.alloc([BW, TT], F32)
    tc = wb.alloc([BW, TT], F32)
    h0 = wb.alloc([BW, TT], F32)
    h1 = tc
    tcb = uraw_alias(k, uraw, TT)
    def upos(c0):
        return OC + c0 if c0 < C else OL + (c0 - C)

    for b in range(NB):
        wt = win[b % 2]
        P.dma(wt[:, :, 0, :], W["w_in0"][:, BW * b:BW * (b + 1)].rearrange("(k p) n -> p k n", p=128), eng="pool")
        P.dma(wt[:, :, 1, :], W["w_in0"][:, 1408 + BW * b:1408 + BW * (b + 1)].rearrange("(k p) n -> p k n", p=128), eng="pool")

        if b > 0:
            P.memset("pool", uraw[:, 0:OC], 0.0)
            P.memset("pool", uraw[:, OC + C:OL], 0.0)

        def conv_chunk(c0, n):
            o = upos(c0)
            P.ts("dve", u[:, c0:c0 + n], uraw[:, o - 2:o - 2 + n], cw[:, b, 0:1], cbias[:, b:b + 1], ALU.mult, ALU.add)
            for j in range(1, 4):
                P.stt("dve", u[:, c0:c0 + n], uraw[:, o - 2 + j:o - 2 + j + n], cw[:, b, j:j + 1], u[:, c0:c0 + n],
                      ALU.mult, ALU.add)
            P.copy("act", ub[:, c0:c0 + n], u[:, c0:c0 + n])

        for ci, (c0, n) in enumerate(COLCH):
            ps = k.bank()
            for kk in range(NCH):
                P.mm(ps[0:BW, 0:n], wt[:, kk, 0, :], nT[:, kk, c0:c0 + n], start=(kk == 0), stop=(kk == NCH - 1))
            o = upos(c0)
            P.copy("act", uraw[:, o:o + n], ps[0:BW, 0:n])
            ps2 = k.bank()
            for kk in range(NCH):
                P.mm(ps2[0:BW, 0:n], wt[:, kk, 1, :], nT[:, kk, c0:c0 + n], start=(kk == 0), stop=(kk == NCH - 1))
            P.act(sg[:, c0:c0 + n], ps2[0:BW, 0:n], AF.Silu)
            if ci == 0:
                conv_chunk(c0, n)
            elif ci >= 2:
                conv_chunk(*COLCH[ci - 1])
        conv_chunk(*COLCH[-1])
        for d in range(2):
            hd = h0 if d == 0 else h1
            order = list(range(5)) if d == 0 else [0, 4, 3, 2, 1]
            prev_c = None
            for oi, ci in enumerate(order):
                c0, n = COLCH[ci]
                sl = slice(c0, c0 + n)
                ps = k.bank()
                P.mm(ps[0:BW, 0:n], wa[:, d * NB + b, :], ub[:, sl])
                P.act(ta[:, sl], ps[0:BW, 0:n], AF.Sigmoid, bias=gb[:, 0, d, b:b + 1], scale=1.0)
                ps2 = k.bank()
                P.mm(ps2[0:BW, 0:n], wx[:, d * NB + b, :], ub[:, sl])
                P.act(tb[:, sl], ps2[0:BW, 0:n], AF.Sigmoid, bias=gb[:, 1, d, b:b + 1], scale=1.0)
                P.act(ta[:, sl], ta[:, sl], AF.Exp, scale=cl[:, d, b:b + 1])
                P.tt("dve", tc[:, sl], ta[:, sl], ta[:, sl], ALU.mult) if d == 0 else P.tt("dve", tcb[:, sl], ta[:, sl], ta[:, sl], ALU.mult)
                tcc = tc if d == 0 else tcb
                P.act(tcc[:, sl], tcc[:, sl], AF.Sqrt, bias=k.oneb[0:BW], scale=-1.0)
                P.tt("pool", tb[:, sl], tb[:, sl], u[:, sl], ALU.mult)
                P.tt("dve", tb[:, sl], tb[:, sl], tcc[:, sl], ALU.mult)
                if d == 0:
                    init = 0.0 if oi == 0 else hd[:, c0 - 1:c0]
                    P.scan("dve", hd[:, sl], ta[:, sl], tb[:, sl], init)
                else:
                    if oi == 0:
                        init = 0.0
                    elif oi == 1:
                        init = hd[:, 0:1]
                    else:
                        init = hd[:, c0 + n:c0 + n + 1]
                    P.scan("dve", rev(hd[:, sl]), rev(ta[:, sl]), rev(tb[:, sl]), init)
        for (c0, n) in COLCH:
            sl = slice(c0, c0 + n)
            P.tt("dve", h0[:, sl], h0[:, sl], h1[:, sl], ALU.add)
            P.tt("dve", aT[:, b, sl], h0[:, sl], sg[:, sl], ALU.mult)
    return [(aT[:, b, :], BW * b) for b in range(NB)]


NEG = -30000.0


def na_w0(r):
    return min(max(r - 4, 0), 24)


def na_plan():
    variants = {}
    plan = []
    for i in range(16):
        r = 2 * i
        lo = na_w0(r) // 2
        hi = (na_w0(r + 1) + 7) // 2
        lst = []
        for kt in range(lo, hi + 1):
            key = (2 * kt - r, na_w0(r) - r, na_w0(r + 1) - (r + 1))
            if key not in variants:
                variants[key] = len(variants)
            lst.append((kt, variants[key]))
        plan.append(lst)
    return variants, plan


def mixer_swa(k, nT, abump, wbump):
    import os
    st = os.environ.get("KN_SWA_STRICT", "0") == "1"
    if st:
        k.P.begin_strict()
    r = mixer_attn(k, nT, "swa")
    if st:
        k.P.end_strict()
    return r


def mixer_na(k, nT, abump, wbump):
    import os
    st = os.environ.get("KN_NA_STRICT", "0") == "1"
    if st:
        k.P.begin_strict()
    r = mixer_attn(k, nT, "na")
    if st:
        k.P.end_strict()
    return r


def mixer_attn(k, nT, kind):
    P, W = k.P, k.W
    swa = kind == "swa"
    l = 1 if swa else 2
    win = W["w_in%d" % l]
    aT = k.ar.view(k.R1[0], [128, NCH, TT], BF16)
    wb = Bump(k.ar, k.R1[0] + TT * NCH // 2, k.R2[1])
    k.lb.reset()
    lb = k.lb
    NT = TT // 128
    if swa:
        pswap = lb.alloc([128, 128], BF16)
        m_prev = lb.alloc([128, 128], BF16)
        m_next = lb.alloc([128, 128], BF16)
        P.dma(pswap, W["pswap"])
        P.dma(m_prev, W["m_prev"])
        P.dma(m_next, W["m_next"])
        cosT = wb.alloc([128, T], F32)
        sinT = wb.alloc([128, T], F32)
        P.dma(cosT, W["cosT"])
        P.dma(sinT, W["sinT"])
        sk = lb.alloc([128, 16], F32)
        P.dma(sk, W["sink1"].rearrange("(o h) -> o h", o=1).partition_broadcast(128), slow=True)
        P.act(sk, sk, AF.Exp)
        sinkcol = lb.alloc([128, 8], F32)
        skv = sk.rearrange("p (a b) -> p a b", b=2)
        P.copy("pool", sinkcol[0:64, :], skv[0:64, :, 0])
        P.copy("pool", sinkcol[64:128, :], skv[64:128, :, 1])
        qoff, koff, voff, goff = 0, 1024, 1280, 1536
    else:
        variants, plan = na_plan()
        NV = len(variants)
        qoff, koff, voff, goff = 0, 1024, 2048, 3072
    sets = []
    for _ in range(2):
        st = {}
        st["w"] = wb.alloc([128, NCH, 4, 128], BF16)
        st["qT"] = wb.alloc([128, TT], BF16)
        st["kT"] = wb.alloc([128, TT], BF16)
        st["V"] = wb.alloc([128, NT, 128], BF16)
        st["sg"] = wb.alloc([128, TT], BF16)
        if swa:
            st["qr"] = wb.alloc([128, T], BF16)
            st["kr"] = wb.alloc([128, T], BF16)
        else:
            st["tab0"] = wb.alloc([128, NV, 128], BF16)
            st["tab1"] = wb.alloc([128, NV, 128], BF16)
        sets.append(st)
    t1s = [wb.alloc([128, 512], F32) for _ in range(2)]
    t2s = [wb.alloc([128, 512], F32) for _ in range(2)]
    qfs = [wb.alloc([128, 512], F32) for _ in range(2)]
    PTs = [wb.alloc([128, 8, 128], BF16) for _ in range(3)]
    rdens = [wb.alloc([128, 128], F32) for _ in range(2)]
    oas = [wb.alloc([128, 128], F32) for _ in range(2)]
    cnt = {"t": 0, "pt": 0, "r": 0, "od": 0, "ip": 0}

    def bank_ip():
        b_ = 6 + cnt["ip"] % 2
        cnt["ip"] += 1
        return k.ps[:, 512 * b_:512 * (b_ + 1)]

    def bank_od():
        b_ = 4 + cnt["od"] % 2
        cnt["od"] += 1
        return k.ps[:, 512 * b_:512 * (b_ + 1)]

    def wcols(dst, c0, n):
        P.dma(dst, win[:, c0:c0 + n].rearrange("(kk p) n -> p kk n", p=128), eng="pool")

    import os
    SKIP = os.environ.get('KN_SKIP', '').split(',')
    def inproj_units(hp):
        st = sets[hp % 2]
        w = st["w"]
        units = []

        def u_weights():
            wcols(w[:, :, 0, :], qoff + 128 * hp, 128)
            if swa:
                kvh = hp // 2
                for e in range(2):
                    wcols(w[:, :, 1, 64 * e:64 * e + 64], koff + 64 * kvh, 64)
                    wcols(w[:, :, 2, 64 * e:64 * e + 64], voff + 64 * kvh, 64)
            else:
                wcols(w[:, :, 1, :], koff + 128 * hp, 128)
                wcols(w[:, :, 2, :], voff + 128 * hp, 128)
                for e in range(2):
                    for v0 in range(0, NV, 3):
                        v1 = min(NV, v0 + 3)
                        P.dma(st["tab%d" % e][:, v0:v1, :], W["natab"][2 * hp + e, v0:v1].rearrange("v p q -> p v q"), eng="pool")
            wcols(w[:, :, 3, :], goff + 128 * hp, 128)
        units.append(u_weights)

        def mk_q(c0, n):
            def f():
                ps = bank_ip()
                for kk in range(NCH):
                    P.mm(ps[:, 0:n], w[:, kk, 0, :], nT[:, kk, c0:c0 + n], start=(kk == 0), stop=(kk == NCH - 1))
                if swa and c0 >= C:
                    qf = qfs[cnt["t"] % 2]
                    t1 = t1s[cnt["t"] % 2]
                    t2 = t2s[cnt["t"] % 2]
                    cnt["t"] += 1
                    lc = c0 - C
                    P.act(qf[:, 0:n], ps[:, 0:n], AF.Copy, scale=0.125)
                    P.act(st["qT"][:, c0:c0 + n], ps[:, 0:n], AF.Copy, scale=0.125)
                    P.tt("dve", t1[:, 0:n], qf[:, 0:n], cosT[:, lc:lc + n], ALU.mult)
                    ps2 = bank_ip()
                    P.mm(ps2[:, 0:n], pswap, st["qT"][:, c0:c0 + n])
                    P.tt("dve", t2[:, 0:n], ps2[:, 0:n], sinT[:, lc:lc + n], ALU.mult)
                    P.tt("dve", st["qr"][:, lc:lc + n], t1[:, 0:n], t2[:, 0:n], ALU.add)
                else:
                    P.act(st["qT"][:, c0:c0 + n], ps[:, 0:n], AF.Copy, scale=0.125)
            return f

        def mk_k(c0, n):
            def f():
                ps = bank_ip()
                for kk in range(NCH):
                    P.mm(ps[:, 0:n], w[:, kk, 1, :], nT[:, kk, c0:c0 + n], start=(kk == 0), stop=(kk == NCH - 1))
                if swa and c0 >= C:
                    qf = qfs[cnt["t"] % 2]
                    t1 = t1s[cnt["t"] % 2]
                    t2 = t2s[cnt["t"] % 2]
                    cnt["t"] += 1
                    lc = c0 - C
                    P.copy("act", qf[:, 0:n], ps[:, 0:n])
                    P.copy("act", st["kT"][:, c0:c0 + n], ps[:, 0:n])
                    P.tt("dve", t1[:, 0:n], qf[:, 0:n], cosT[:, lc:lc + n], ALU.mult)
                    ps2 = bank_ip()
                    P.mm(ps2[:, 0:n], pswap, st["kT"][:, c0:c0 + n])
                    P.tt("dve", t2[:, 0:n], ps2[:, 0:n], sinT[:, lc:lc + n], ALU.mult)
                    P.tt("dve", st["kr"][:, lc:lc + n], t1[:, 0:n], t2[:, 0:n], ALU.add)
                else:
                    P.copy("act", st["kT"][:, c0:c0 + n], ps[:, 0:n])
            return f

        def mk_g(c0, n):
            def f():
                ps = bank_ip()
                for kk in range(NCH):
                    P.mm(ps[:, 0:n], w[:, kk, 3, :], nT[:, kk, c0:c0 + n], start=(kk == 0), stop=(kk == NCH - 1))
                P.act(st["sg"][:, c0:c0 + n], ps[:, 0:n], AF.Silu)
            return f

        def mk_v(j4):
            def f():
                nj = min(4, NT - j4)
                ps = bank_ip()
                for jj in range(nj):
                    j = j4 + jj
                    for kk in range(NCH):
                        P.mm(ps[:, 128 * jj:128 * (jj + 1)], nT[:, kk, 128 * j:128 * (j + 1)], w[:, kk, 2, :],
                             start=(kk == 0), stop=(kk == NCH - 1))
                P.copy("act", st["V"][:, j4:j4 + nj, :], ps[:, 0:128 * nj].rearrange("p (j c) -> p j c", c=128))
            return f
        for (c0, n) in COLCH:
            units.append(mk_q(c0, n))
            units.append(mk_k(c0, n))
            units.append(mk_g(c0, n))
        for j4 in range(0, NT, 4):
            units.append(mk_v(j4))
        return units

    def stage1(hp, qt, e):
        st = sets[hp % 2]
        is_ctx = qt < 2
        pr = slice(64 * e, 64 * e + 64)
        tiles = []
        qraw = st["qT"][pr, 128 * qt:128 * (qt + 1)]
        for cj in range(2):
            tiles.append((st["kT"][pr, 128 * cj:128 * (cj + 1)], None, st["V"][:, cj, 64 * e:64 * e + 64], qraw))
        if not is_ctx:
            i = qt - 2
            if swa:
                qrot = st["qr"][pr, 128 * i:128 * (i + 1)]
                for j, tab in ((i - 1, m_prev), (i, None), (i + 1, m_next)):
                    if 0 <= j < 16:
                        tiles.append((st["kr"][pr, 128 * j:128 * (j + 1)], tab,
                                      st["V"][:, 2 + j, 64 * e:64 * e + 64], qrot))
            else:
                for (kt, vi) in plan[i]:
                    tiles.append((st["kT"][pr, C + 128 * kt:C + 128 * (kt + 1)], st["tab%d" % e][:, vi, :],
                                  st["V"][:, 2 + kt, 64 * e:64 * e + 64], qraw))
        nt = len(tiles)
        ps2 = k.ps[:, 1024 * e:1024 * (e + 1)]
        for t, (kT_, tab, V_, q_) in enumerate(tiles):
            o = ps2[:, 128 * t:128 * (t + 1)]
            P.mm(o, kT_, q_, start=True, stop=(tab is None))
            if tab is not None:
                P.mm(o, k.ident_b, tab, start=False, stop=True)
        PT = PTs[cnt["pt"] % 3]
        cnt["pt"] += 1
        P.act(PT[:, 0:nt, :], ps2[:, 0:128 * nt].rearrange("p (t c) -> p t c", c=128), AF.Exp)
        return (tiles, PT)

    def stage2(hp, qt, e, s1):
        st = sets[hp % 2]
        tiles, PT = s1
        nt = len(tiles)
        pr = slice(64 * e, 64 * e + 64)
        od = k.ps[:, 2048 + 512 * e:2048 + 512 * (e + 1)]
        for t, (kT_, tab, V_, q_) in enumerate(tiles):
            P.mm(od[pr, 0:128], V_, PT[:, t, :], start=(t == 0), stop=(t == nt - 1), tile_position=(0, 64 * e))
        for t in range(nt):
            P.mm(od[pr, 128:256], k.ones_b[:, 0:64], PT[:, t, :], start=(t == 0), stop=(t == nt - 1),
                 tile_position=(0, 64 * e))
        rden = rdens[cnt["r"] % 2]
        oa = oas[cnt["r"] % 2]
        cnt["r"] += 1
        if swa:
            P.ts("dve", rden[pr, :], od[pr, 128:256], sinkcol[pr, hp:hp + 1], None, ALU.add)
            P.recip(rden[pr, :], rden[pr, :])
        else:
            P.recip(rden[pr, :], od[pr, 128:256])
        P.tt("dve", oa[pr, :], od[pr, 0:128], rden[pr, :], ALU.mult)
        P.tt("pool", aT[pr, hp, 128 * qt:128 * (qt + 1)], oa[pr, :], st["sg"][pr, 128 * qt:128 * (qt + 1)], ALU.mult)

    import os
    PIPE = os.environ.get("KN_PIPE", "1") == "1"
    for u in inproj_units(0):
        u()
    for hp in range(8):
        nxt = inproj_units(hp + 1) if hp + 1 < 8 else []
        items = [(qt, e) for qt in range(NT) for e in range(2)]
        prev = None
        ui = 0
        for idx, (qt, e) in enumerate(items):
            s1 = stage1(hp, qt, e)
            if PIPE:
                if prev is not None:
                    stage2(hp, prev[0], prev[1], prev[2])
                prev = (qt, e, s1)
            else:
                stage2(hp, qt, e, s1)
            want = (len(nxt) * (idx + 1)) // len(items)
            while ui < want:
                nxt[ui]()
                ui += 1
        if PIPE and prev is not None:
            stage2(hp, prev[0], prev[1], prev[2])
        while ui < len(nxt):
            nxt[ui]()
            ui += 1
    return [(aT[:, c, :], 128 * c) for c in range(NCH)]


TWO_PI = 2.0 * math.pi
INV2PI = 1.0 / TWO_PI
PI_LO = 3.1415925


def trig_tables(k, eng, x, k32, kf, out_sin, out_cos):
    P = k.P
    P.ts("dve", kf, x, INV2PI, None, ALU.mult)
    P.copy("dve", k32, kf)
    P.copy("dve", kf, k32)
    P.stt("dve", kf, kf, -TWO_PI, x, ALU.mult, ALU.add)
    P.ts("dve", kf, kf, PI_LO, -PI_LO, ALU.min, ALU.max)
    P.act(out_sin, kf, AF.Sin)
    P.act(x, kf, AF.Abs)
    P.act(out_cos, x, AF.Sin, bias=k.halfpi, scale=-1.0)


def cmul(k, eng, out_re, out_im, are, aim, bre, bim, t1, t2, neg_im=False):
    P = k.P
    P.tt(eng, t1, are, bre, ALU.mult)
    P.tt(eng, t2, aim, bim, ALU.mult)
    P.tt(eng, out_re, t1, t2, ALU.subtract)
    P.tt(eng, t1, are, bim, ALU.mult)
    P.tt(eng, t2, aim, bre, ALU.mult)
    if neg_im:
        P.ts(eng, t1, t1, -1.0, None, ALU.mult)
        P.tt(eng, out_im, t1, t2, ALU.subtract)
    else:
        P.tt(eng, out_im, t1, t2, ALU.add)


def mixer_s5(k, nT, abump, wbump):
    P, W = k.P, k.W
    win = W["w_in3"]
    base = k.R1[0]
    yg = k.ar.view(base, [128, NCH, T], BF16)
    aT = k.ar.view(base + 8192, [128, NCH, TT], BF16)
    wb = Bump(k.ar, base + 8192, k.R2[1])
    k.lb.reset()
    lb = k.lb
    NCOL = 320
    KF = lb.alloc([128, 4, 8], F32)
    SGN = lb.alloc([128, 1], F32)
    MSK = lb.alloc([128, 2, 128], F32)
    IOTA = lb.alloc([128, NCOL], F32)
    WD = lb.alloc([128, 8, 240], BF16)
    V4 = lb.alloc([128, 4, 240], BF16)
    dcol = lb.alloc([128, NCH], F32)
    gbias = lb.alloc([128, NCH], F32)
    for dst, nm in ((KF, "s5_kf"), (SGN, "s5_sgn"), (MSK, "s5_msk"), (IOTA, "s5_iota"), (WD, "s5_wd"), (V4, "s5_v4")):
        P.dma(dst, W[nm])
    P.dma(dcol, W["s5_d3"].rearrange("(k p) -> p k", p=128), slow=True)
    P.dma(gbias, W["glu_b3"].rearrange("(k p) -> p k", p=128), slow=True)
    uTf = wb.alloc([128, TT], F32)
    uTb = wb.alloc([128, 8 * NCOL], BF16)
    ytmp = wb.alloc([128, T], F32)
    wu = wb.alloc([128, NCH, 128], BF16)
    araw = wb.alloc([8, 2, 128], F32)
    A = wb.alloc([128, 2, 8], F32)
    ldt = wb.alloc([128, 8], F32)
    ar_ = wb.alloc([128, 8], F32)
    th = wb.alloc([128, 8], F32)
    rho8 = wb.alloc([128, 8], F32)
    phis = wb.alloc([128, 8], F32)
    fx = wb.alloc([128, 4, 8, 8], F32)
    fk32 = wb.alloc([128, 4, 8, 8], I32)
    fkf = wb.alloc([128, 4, 8, 8], F32)
    fmag = wb.alloc([128, 4, 8, 8], F32)
    fsin = wb.alloc([128, 4, 8, 8], F32)
    fcos = wb.alloc([128, 4, 8, 8], F32)
    Tre = wb.alloc([128, 4, 8, 8], F32)
    Tim = wb.alloc([128, 4, 8, 8], F32)
    sm = [wb.alloc([128, 8], F32) for _ in range(6)]
    kap = wb.alloc([128, 2, 8], F32)
    braw = wb.alloc([128, 2, 8, 16], F32)
    Bb = wb.alloc([128, 2, 8, 16], F32)
    craw = wb.alloc([128, 2, 128], F32)
    Cm = wb.alloc([128, 2, 8, 16], F32)
    bt1 = wb.alloc([128, 8, 8, 16], F32)
    bt2 = wb.alloc([128, 8, 8, 16], F32)
    W2p = wb.alloc([128, 2, 8, 128], BF16)
    Zp = wb.alloc([128, 2, 8, 128], BF16)
    W3 = wb.alloc([128, 2, 8, 128], BF16)
    W2 = wb.alloc([128, 8, 2, 128], BF16)
    W1 = wb.alloc([128, 8, 2, 128], BF16)
    NG = 4
    off_r = wb.p
    rx = wb.alloc([128, NG, NCOL], F32)
    rk32 = wb.alloc([128, NG, NCOL], I32)
    gtmp = k.ar.view(off_r, [128, T], F32)
    rkf = wb.alloc([128, NG, NCOL], F32)
    rsin = wb.alloc([128, NG, NCOL], F32)
    rcos = wb.alloc([128, NG, NCOL], F32)
    U8 = [wb.alloc([128, NCOL], BF16) for _ in range(2)]
    Ssb = [wb.alloc([128, 2, NCOL], F32) for _ in range(2)]
    Gin = wb.alloc([128, 2, NCOL], F32)
    Gs = wb.alloc([128, 2, NCOL], F32)
    pt = [wb.alloc([128, NCOL], F32) for _ in range(2)]
    Hb = [wb.alloc([128, 2, NCOL + 2], BF16) for _ in range(2)]
    Y8 = [wb.alloc([128, 256], BF16) for _ in range(2)]
    for h in Hb:
        P.memset("pool", h, 0.0)
    rho_b = wb.alloc([128, NCOL], F32)

    def bankS():
        b = k.s5_rr
        k.s5_rr = (k.s5_rr + 1) % 4
        return k.ps[:, 512 * b:512 * (b + 1)]
    k.s5_rr = 0
    UL = k.ps[:, 2048:4096].rearrange("p (t c) -> p t c", t=8)

    def bc(ap, n, pos):
        return bc_free(ap, n, pos)

    import os
    STG = int(os.environ.get('KN_S5', '99'))
    for ch in range(NCH if STG >= 99 else 1):
        g0 = 8 * ch
        P.dma(wu, win[:, 128 * ch:128 * (ch + 1)].rearrange("(kk p) n -> p kk n", p=128), eng="pool")
        for (c0, n) in COLCH:
            ps = bankS()
            for kk in range(NCH):
                P.mm(ps[:, 0:n], wu[:, kk, :], nT[:, kk, c0:c0 + n], start=(kk == 0), stop=(kk == NCH - 1))
            P.copy("act", uTf[:, c0:c0 + n], ps[:, 0:n])
        P.copy("act", uTb[:, 0:TT], uTf)
        P.copy("act", uTb[:, TT:TT + C], uTf[:, 0:C])
        if STG < 2:
            continue
        for d in range(2):
            P.dma(araw[:, 0, 64 * d:64 * d + 64], W["s5_a_re3"][d, g0:g0 + 8, :])
            P.dma(araw[:, 1, 64 * d:64 * d + 64], W["s5_a_im3"][d, g0:g0 + 8, :])
            P.dma(ldt[64 * d:64 * d + 64, :],
                  W["s5_log_dt3"][d:d + 1, g0:g0 + 8].partition_broadcast(64), slow=True)
            P.dma(braw[64 * d:64 * d + 64, 0], W["s5_b_re3"][d, g0:g0 + 8].rearrange("g p j -> p g j"), slow=True)
            P.dma(braw[64 * d:64 * d + 64, 1], W["s5_b_im3"][d, g0:g0 + 8].rearrange("g p j -> p g j"), slow=True)
            P.dma(craw[:, 0, 64 * d:64 * d + 64], W["s5_c_re3"][d, g0:g0 + 8].rearrange("g i p -> (g i) p"))
            P.dma(craw[:, 1, 64 * d:64 * d + 64], W["s5_c_im3"][d, g0:g0 + 8].rearrange("g i p -> (g i) p"))
        ps = bankS()
        for x in range(2):
            P.tr(ps[:, 8 * x:8 * x + 8], araw[:, x, :], k.ident_f[0:8, 0:8])
        P.copy("act", A, ps[:, 0:16].rearrange("p (x g) -> p x g", x=2))
        ps = bankS()
        for x in range(2):
            P.tr(ps[:, 128 * x:128 * x + 128], craw[:, x, :], k.ident_f)
        P.copy("act", Cm, ps[:, 0:256].rearrange("p (x g i) -> p x g i", x=2, g=8))
        if STG < 3:
            continue
        P.act(ldt, ldt, AF.Exp)
        P.tt("dve", ar_, A[:, 0, :], ldt, ALU.mult)
        P.tt("dve", th, A[:, 1, :], ldt, ALU.mult)
        P.act(rho8, ar_, AF.Exp, scale=8.0)
        P.ts("dve", phis, th, 8.0, SGN[:, 0:1], ALU.mult, ALU.mult)
        arb = bc(bc(ar_, 4, 1), 8, 3)
        thb = bc(bc(th, 4, 1), 8, 3)
        kfb = bc(KF, 8, 2)
        P.tt("dve", fmag, arb, kfb, ALU.mult)
        P.act(fmag, fmag, AF.Exp)
        P.tt("dve", fx, thb, kfb, ALU.mult)
        trig_tables(k, "dve", fx, fk32, fkf, fsin, fcos)
        P.tt("dve", Tre, fmag, fcos, ALU.mult)
        P.tt("dve", Tim, fmag, fsin, ALU.mult)
        lre, lim = Tre[:, 3, :, 0], Tim[:, 3, :, 0]
        Are, Aim = A[:, 0, :], A[:, 1, :]
        nre, den, t1_, t2_, rd = sm[0], sm[1], sm[2], sm[3], sm[4]
        P.ts("dve", nre, lre, -1.0, None, ALU.add)
        P.tt("dve", den, Are, Are, ALU.mult)
        P.tt("dve", t1_, Aim, Aim, ALU.mult)
        P.tt("dve", den, den, t1_, ALU.add)
        P.recip(rd, den)
        P.tt("dve", t1_, nre, Are, ALU.mult)
        P.tt("dve", t2_, lim, Aim, ALU.mult)
        P.tt("dve", t1_, t1_, t2_, ALU.add)
        P.tt("dve", kap[:, 0, :], t1_, rd, ALU.mult)
        P.tt("dve", t1_, lim, Are, ALU.mult)
        P.tt("dve", t2_, nre, Aim, ALU.mult)
        P.tt("dve", t1_, t1_, t2_, ALU.subtract)
        P.tt("dve", kap[:, 1, :], t1_, rd, ALU.mult)
        s1 = bt1[:, :, 0, :]
        s2 = bt2[:, :, 0, :]
        cmul(k, "dve", Bb[:, 0], Bb[:, 1], bc(kap[:, 0, :], 16, 2), bc(kap[:, 1, :], 16, 2),
             braw[:, 0], braw[:, 1], s1, s2)
        if STG < 4:
            continue
        def fam(f, x):
            t = Tre if x == 0 else Tim
            return bc(t[:, f], 16, 3)

        def vec(v, x):
            return bc(v[:, x], 8, 2)

        def o4(t, x):
            return t[:, x].rearrange("p g (s j) -> p g s j", s=8)
        cmul(k, "dve", o4(W2p, 0), o4(W2p, 1), fam(0, 0), fam(0, 1), vec(Bb, 0), vec(Bb, 1), bt1, bt2)
        cmul(k, "dve", o4(Zp, 0), o4(Zp, 1), fam(2, 0), fam(2, 1), vec(Bb, 0), vec(Bb, 1), bt1, bt2)
        cmul(k, "dve", o4(W3, 0), o4(W3, 1), fam(1, 0), fam(1, 1), vec(Cm, 0), vec(Cm, 1), bt1, bt2, neg_im=True)
        if STG < 5:
            continue
        for x in range(2 if os.environ.get('KN_5A', '1') == '1' else 0):
            ps = bankS()
            psb = ps.bitcast(BF16)
            for g8 in range(8):
                P.tr(psb[:, 128 * g8:128 * g8 + 128], W2p[:, x, g8, :], k.ident_b)
            P.copy("act", W2[:, :, x, :], psb[:, 0:1024].rearrange("p (g c) -> p g c", g=8))
        for gq in range(2):
            psd = [bankS(), bankS()]
            for gg in range(4):
                g8 = 4 * gq + gg
                for d in range(2):
                    o = psd[d][:, 128 * gg:128 * gg + 128]
                    pr = slice(64 * d, 64 * d + 64)
                    P.mm(o, Zp[pr, 0, g8, :], W3[pr, 0, g8, :], start=True, stop=False)
                    P.mm(o, Zp[pr, 1, g8, :], W3[pr, 1, g8, :], start=False, stop=True)
            for d in range(2):
                P.tt("dve", W1[:, 4 * gq:4 * gq + 4, d, :], psd[d].rearrange("p (g c) -> p g c", g=4),
                     bc(MSK[:, d, :], 4, 1), ALU.mult)
        if STG < 6:
            continue
        for g8 in range(8):
            if g8 % NG == 0:
                P.tt("dve", rx, bc(phis[:, g8:g8 + NG], NCOL, 2), bc(IOTA, NG, 1), ALU.mult)
                trig_tables(k, "dve", rx, rk32, rkf, rsin, rcos)
            gi = g8 % NG
            cs, sn = rcos[:, gi, :], rsin[:, gi, :]
            u8 = U8[g8 % 2]
            ps = bankS()
            for s_ in range(8):
                rhs = uTb.rearrange("p (c s) -> p s c", s=8)[:, s_, :]
                P.mm(ps[:, 0:NCOL], WD[:, g8, 112 - 16 * s_:112 - 16 * s_ + 128], rhs, start=(s_ == 0), stop=(s_ == 7))
            P.copy("act", u8, ps[:, 0:NCOL])
            ssb = Ssb[g8 % 2]
            for x in range(2):
                ps = bankS()
                P.mm(ps[:, 0:NCOL], W2[:, g8, x, :], u8)
                P.copy("act", ssb[:, x, :], ps[:, 0:NCOL])
            if STG < 7:
                continue
            P.tt("dve", pt[0], ssb[:, 0, :], cs, ALU.mult)
            P.tt("dve", pt[1], ssb[:, 1, :], sn, ALU.mult)
            P.tt("dve", Gin[:, 0, :], pt[0], pt[1], ALU.add)
            P.tt("dve", pt[0], ssb[:, 1, :], cs, ALU.mult)
            P.tt("dve", pt[1], ssb[:, 0, :], sn, ALU.mult)
            P.tt("dve", Gin[:, 1, :], pt[0], pt[1], ALU.subtract)
            if STG < 8:
                continue
            P.ts("dve", rho_b, IOTA, 0.0, rho8[:, g8:g8 + 1], ALU.mult, ALU.add)
            for x in range(2):
                P.scan("dve", Gs[0:64, x, 0:288], rho_b[0:64, 0:288], Gin[0:64, x, 0:288], 0.0)
                P.scan("dve", rev(Gs[64:128, x, 32:320]), rho_b[64:128, 0:288], rev(Gin[64:128, x, 32:320]), 0.0)
            if STG < 9:
                continue
            hb = Hb[g8 % 2]
            P.tt("dve", pt[0], Gs[:, 0, :], cs, ALU.mult)
            P.tt("dve", pt[1], Gs[:, 1, :], sn, ALU.mult)
            P.tt("dve", hb[0:64, 0, 1:289], pt[0][0:64, 0:288], pt[1][0:64, 0:288], ALU.subtract)
            P.tt("dve", hb[64:128, 0, 31:319], pt[0][64:128, 32:320], pt[1][64:128, 32:320], ALU.subtract)
            P.tt("dve", pt[0], Gs[:, 1, :], cs, ALU.mult)
            P.tt("dve", pt[1], Gs[:, 0, :], sn, ALU.mult)
            P.tt("dve", hb[0:64, 1, 1:289], pt[0][0:64, 0:288], pt[1][0:64, 0:288], ALU.add)
            P.tt("dve", hb[64:128, 1, 31:319], pt[0][64:128, 32:320], pt[1][64:128, 32:320], ALU.add)
            if STG < 10:
                continue
            ps = bankS()
            o = ps[:, 0:256]
            P.mm(o, W1[:, g8, 0, :], u8[:, 32:288], start=True, stop=False)
            P.mm(o, W1[:, g8, 1, :], u8[:, 32:288], start=False, stop=False)
            P.mm(o, W3[:, 0, g8, :], hb[:, 0, 32:288], start=False, stop=False)
            P.mm(o, W3[:, 1, g8, :], hb[:, 1, 32:288], start=False, stop=True)
            y8 = Y8[g8 % 2]
            P.copy("act", y8, o)
            for t in range(8):
                hh, tq = t // 4, t % 4
                P.mm(UL[:, t, :], V4[64 * hh:64 * hh + 64, tq, 112 - 16 * g8:112 - 16 * g8 + 128],
                     y8[64 * hh:64 * hh + 64, :], start=(g8 == 0 and t % 2 == 0), stop=(g8 == 7),
                     skip_group_check=True)
        if STG < 11:
            continue
        yv = ytmp.rearrange("p (c t) -> p t c", t=8)
        P.copy("dve", yv, UL)
        P.stt("dve", ytmp, uTf[:, C:TT], dcol[:, ch:ch + 1], ytmp, ALU.mult, ALU.add)
        P.tt("dve", gtmp, ytmp, ytmp, ALU.mult)
        P.ts("dve", gtmp, gtmp, 0.044715, 1.0, ALU.mult, ALU.add)
        P.tt("dve", gtmp, gtmp, ytmp, ALU.mult)
        P.act(gtmp, gtmp, AF.Sigmoid, scale=2.0 * math.sqrt(2.0 / math.pi))
        P.tt("dve", yg[:, ch, :], gtmp, ytmp, ALU.mult)
    if STG < 12:
        return [(aT[:, c, :], 128 * c) for c in range(NCH)]
    sb2 = Bump(k.ar, base + 8192 + TT * NCH // 2, k.R2[1])
    wg = [sb2.alloc([128, NCH, 128], BF16) for _ in range(2)]
    wi = [sb2.alloc([128, NCH, 128], BF16) for _ in range(2)]
    sgs = [sb2.alloc([128, 512], F32) for _ in range(2)]
    zs = [sb2.alloc([128, 512], F32) for _ in range(2)]
    sls = [sb2.alloc([128, 512], F32) for _ in range(2)]
    it = 0
    for m in range(NCH):
        P.dma(wg[m % 2], W["glu_w3"][:, 128 * m:128 * (m + 1)].rearrange("(kk p) n -> p kk n", p=128), eng="pool")
        P.dma(wi[m % 2], win[:, 1024 + 128 * m:1024 + 128 * (m + 1)].rearrange("(kk p) n -> p kk n", p=128), eng="pool")
        for q in range(4):
            c0 = 512 * q
            ps = k.bank()
            for kk in range(NCH):
                P.mm(ps, wg[m % 2][:, kk, :], yg[:, kk, c0:c0 + 512], start=(kk == 0), stop=(kk == NCH - 1))
            sg_, z_, sl_ = sgs[it % 2], zs[it % 2], sls[it % 2]
            it += 1
            P.act(sg_, ps, AF.Sigmoid, bias=gbias[:, m:m + 1], scale=1.0)
            P.tt("dve", z_, sg_, yg[:, m, c0:c0 + 512], ALU.mult)
            ps2 = k.bank()
            for kk in range(NCH):
                P.mm(ps2, wi[m % 2][:, kk, :], nT[:, kk, C + c0:C + c0 + 512], start=(kk == 0), stop=(kk == NCH - 1))
            P.act(sl_, ps2, AF.Silu)
            P.tt("dve", aT[:, m, C + c0:C + c0 + 512], z_, sl_, ALU.mult)
    return [(aT[:, c, :], 128 * c) for c in range(NCH)]


_CACHE = {}


def host_consts():
    c = {}
    c["ident_f"] = np.eye(128, dtype=np.float32)
    c["ident_b"] = np.eye(128, dtype=np.float32).astype(ml_dtypes.bfloat16)
    m = np.arange(128)
    sw = (m // 64) * 64 + ((m % 64) + 32) % 64
    ps = np.zeros((128, 128), np.float32)
    ps[sw, m] = 1.0
    c["pswap"] = ps.astype(ml_dtypes.bfloat16)
    kk = np.arange(128)[:, None]
    qq = np.arange(128)[None, :]
    c["m_prev"] = np.where(qq <= kk, 0.0, NEG).astype(np.float32).astype(ml_dtypes.bfloat16)
    c["m_next"] = np.where(kk <= qq, 0.0, NEG).astype(np.float32).astype(ml_dtypes.bfloat16)
    pos = np.arange(T)
    row = (pos // 64).astype(np.float32)
    col = (pos % 64).astype(np.float32)
    freqs = (10000.0 ** (-np.arange(16, dtype=np.float32) / 16)).astype(np.float32)
    ang = np.concatenate([row[:, None] * freqs, col[:, None] * freqs], axis=-1).astype(np.float32)
    d = (m % 64) % 32
    sign = np.where((m % 64) < 32, -1.0, 1.0).astype(np.float32)
    c["cosT"] = np.ascontiguousarray(np.cos(ang)[:, d].T).astype(np.float32)
    c["sinT"] = np.ascontiguousarray((np.sin(ang)[:, d] * sign[None, :]).T).astype(np.float32)
    kf = np.zeros((128, 4, 8), np.float32)
    sidx = np.arange(8, dtype=np.float32)
    kf[:64, 0] = 7 - sidx
    kf[64:, 0] = sidx
    kf[:64, 1] = sidx + 1
    kf[64:, 1] = 8 - sidx
    kf[:64, 2] = -1 - sidx
    kf[64:, 2] = sidx - 8
    kf[:, 3, 0] = 1.0
    c["s5_kf"] = kf
    sg = np.ones((128, 1), np.float32)
    sg[64:] = -1.0
    c["s5_sgn"] = sg
    ss = (np.arange(128) // 16)[:, None]
    tt_ = (np.arange(128) // 16)[None, :]
    msk = np.zeros((128, 2, 128), np.float32)
    msk[:, 0, :] = (ss <= tt_)
    msk[:, 1, :] = (ss >= tt_)
    c["s5_msk"] = msk
    c["s5_iota"] = np.broadcast_to(np.arange(320, dtype=np.float32)[None, :], (128, 320)).copy()
    wd = np.zeros((128, 8, 240), np.float32)
    for g8 in range(8):
        for j in range(16):
            wd[16 * g8 + j, g8, 112 + j] = 1.0
    c["s5_wd"] = wd.astype(ml_dtypes.bfloat16)
    v4 = np.zeros((128, 4, 240), np.float32)
    for hh in range(2):
        for tq in range(4):
            for i in range(16):
                v4[64 * hh + 16 * tq + i, tq, 112 + i] = 1.0
    c["s5_v4"] = v4.astype(ml_dtypes.bfloat16)
    return c


def na_tables(rpb):
    variants, plan = na_plan()
    NV = len(variants)
    out = np.full((16, NV, 128, 128), NEG, np.float32)
    qc = np.arange(64)
    cstart = np.clip(qc - 8, 0, 48)
    kc = np.arange(64)
    col_ok = (kc[:, None] >= cstart[None, :]) & (kc[:, None] < cstart[None, :] + 16)
    dxi = np.clip(kc[:, None] - qc[None, :] + 15, 0, 30)
    for (dyo, o0, o1), vi in variants.items():
        for a in range(2):
            for b in range(2):
                dy = dyo + a - b
                ob = o0 if b == 0 else o1
                if not (ob <= dy < ob + 8):
                    continue
                g = rpb[:, dy + 7, :][:, dxi]
                blk = np.where(col_ok[None], g, np.float32(NEG))
                out[:, vi, 64 * a:64 * a + 64, 64 * b:64 * b + 64] = blk
    return out


def run(inputs, n_layers=4, final_norm=True, cores=8, trace=False):
    key = (n_layers, final_norm)
    if key not in _CACHE:
        _CACHE[key] = build_program(n_layers, final_norm)
    nc = _CACHE[key]
    consts = host_consts()
    shared = {}
    in_maps = []
    for b in range(cores):
        m = {}
        for name, v in inputs.items():
            v = np.asarray(v)
            if name in ("x", "ctx", "c"):
                m[name] = np.ascontiguousarray(v[b])
            elif name == "rpb2":
                if "natab" not in shared:
                    shared["natab"] = na_tables(v.astype(np.float32))
                continue
            else:
                m[name] = v
        m.update(consts)
        m.update(shared)
        in_maps.append(m)
    res = run_bass_kernel_spmd(nc, in_maps, core_ids=list(range(cores)), **({'trace': True} if trace else {}))
    if trace:
        print('EXEC_NS', res.exec_time_ns)
    return np.stack([r["out"] for r in res.results], axis=0)


def kernel(**inputs):
    return run(inputs).astype(np.float32)
```

```python
import math
from contextlib import ExitStack
import numpy as np
import ml_dtypes
import concourse.bass as bass
import concourse.mybir as mybir
from concourse.ap import AP
from concourse.bass_utils import run_bass_kernel_spmd

F32 = mybir.dt.float32
BF16 = mybir.dt.bfloat16
I32 = mybir.dt.int32
AF = mybir.ActivationFunctionType
ALU = mybir.AluOpType
DSZ = {F32: 4, BF16: 2, I32: 4}

D = 1024
T = 2048
C = 256
TT = T + C
NCH = 8
EPS = 1e-6
EPOCH = 2000
ENG = ["pe", "act", "dve", "pool", "sp"]
import os as _os
SAME_SYNC = {"dve": _os.environ.get("KN_DVESYNC", "1") == "1", "act": False, "pool": False, "pe": False, "sp": False}
DVE_MIN = int(_os.environ.get("KN_DVEMIN", "0"))
NDMA = 12


def ap_region(ap):
    t = ap.tensor
    dims = [list(x) for x in ap.ap]
    off = int(ap.offset)
    sz = DSZ.get(ap.dtype, None)
    if sz is None:
        sz = mybir.dt.size(ap.dtype)
    if str(ap.space) == "DRAM":
        lo = hi = off
        for st, cn in dims:
            e = st * (cn - 1)
            lo += min(0, e)
            hi += max(0, e)
        return (t.name, 0, 1, lo * sz, (hi + 1) * sz)
    pst, pcn = dims[0]
    row = 1
    for s in t.shape[1:]:
        row *= s
    p0 = off // row
    f0 = off % row
    if pcn > 1:
        assert pst == row, (pst, row)
    lo = hi = f0
    for st, cn in dims[1:]:
        e = st * (cn - 1)
        lo += min(0, e)
        hi += max(0, e)
    assert lo >= 0 and hi < row, (lo, hi, row, ap.ap, off)
    return (t.name, p0, p0 + pcn, lo * sz, (hi + 1) * sz)


def _fsz(ap):
    n = 1
    for x in ap.shape[1:]:
        n *= x
    return n


def _vcost(eng, out):
    n = _fsz(out)
    if eng == "pool":
        return 500.0 + 5.0 * n
    return 60.0 + 1.05 * n


def _apsig(ap):
    if not DVE_MIN:
        return None
    dims = ap.ap
    if len(dims) != 2 or dims[1][0] != 1 or dims[1][1] < DVE_MIN:
        return None
    return (ap.tensor.name, int(ap.offset), tuple(tuple(x) for x in dims), str(ap.dtype))


class Prog:
    BUCKET = 1024

    def __init__(self, nc):
        self.nc = nc
        self.ops = []
        self.strict = False
        self.last_on = {}
        self.since = {e: [] for e in ENG}
        self.barrier = {}
        self.pending_start = set()
        self.bk = {}
        import os
        self.sched = os.environ.get('KN_SCHED', '1') == '1'
        self.K = int(os.environ.get('KN_K', '64'))

    def _buckets(self, name, b0, b1):
        d = self.bk.setdefault(name, {})
        B = self.BUCKET if name in ("arena", "psum") else 65536
        for i in range(b0 // B, (b1 - 1) // B + 1):
            e = d.get(i)
            if e is None:
                e = d[i] = ([], [])
            yield e

    def add(self, eng, fn, reads=(), writes=(), dma=False, cost=100.0, nbytes=0, tbl=None, order_after=()):
        oid = len(self.ops)
        deps = set()
        regs_r = [ap_region(a) for a in reads]
        regs_w = [ap_region(a) for a in writes]
        hard = set()
        for a_, (name, p0, p1, b0, b1) in zip(reads, regs_r):
            sig = _apsig(a_)
            for (wl, rl) in self._buckets(name, b0, b1):
                for e in wl:
                    if e[0] < p1 and p0 < e[1] and e[2] < b1 and b0 < e[3]:
                        deps.add(e[4])
                        if not (sig is not None and e[5] == sig):
                            hard.add(e[4])
        for (name, p0, p1, b0, b1) in regs_w:
            for (wl, rl) in self._buckets(name, b0, b1):
                for lst in (wl, rl):
                    for e in lst:
                        if e[0] < p1 and p0 < e[1] and e[2] < b1 and b0 < e[3]:
                            deps.add(e[4])
        deps.discard(oid)
        wsig = {}
        for a_, r_ in zip(writes, regs_w):
            wsig[r_] = _apsig(a_)
        odeps = set()
        if eng in self.barrier:
            odeps.add(self.barrier[eng])
        if eng in self.pending_start:
            odeps.update(self.since[eng])
            self.pending_start.discard(eng)
            self.since[eng] = []
            self.barrier[eng] = oid
        elif self.strict and eng in self.last_on:
            odeps.add(self.last_on[eng])
        odeps.update(order_after)
        odeps -= deps
        odeps.discard(oid)
        self.last_on[eng] = oid
        self.since[eng].append(oid)
        for (name, p0, p1, b0, b1) in regs_w:
            for (wl, rl) in self._buckets(name, b0, b1):
                wl[:] = [e for e in wl if not (p0 <= e[0] and e[1] <= p1 and b0 <= e[2] and e[3] <= b1)]
                rl[:] = [e for e in rl if not (p0 <= e[0] and e[1] <= p1 and b0 <= e[2] and e[3] <= b1)]
                wl.append((p0, p1, b0, b1, oid, wsig[(name, p0, p1, b0, b1)]))
        for (name, p0, p1, b0, b1) in regs_r:
            for (wl, rl) in self._buckets(name, b0, b1):
                rl.append((p0, p1, b0, b1, oid))
        self.ops.append(dict(id=oid, eng=eng, fn=fn, dma=dma, deps=deps, odeps=odeps, hard=hard, cost=float(cost), nbytes=nbytes, tbl=tbl))
        return oid

    def begin_strict(self):
        self.strict = True
        self.pending_start = set(ENG)

    def full_barrier(self, engines=("pe", "act", "dve", "pool")):
        return
        for e in engines:
            if e in self.last_on:
                self.barrier[e] = self.last_on[e]
            self.pending_start.add(e)

    def end_strict(self):
        self.strict = False
        for e in ENG:
            if e in self.last_on:
                self.barrier[e] = self.last_on[e]
            self.since[e] = []

    def _schedule(self):
        import bisect
        ops = self.ops
        n = len(ops)
        if not self.sched:
            order = {e: [] for e in ENG}
            for o in ops:
                order[o["eng"]].append(o["id"])
            return order
        users = [[] for _ in range(n)]
        ndep = [0] * n
        for o in ops:
            alld = o["deps"] | o["odeps"]
            ndep[o["id"]] = len(alld)
            for d in alld:
                users[d].append(o["id"])
        finish = [0.0] * n
        ready_t = [0.0] * n
        cand = {e: [] for e in ENG}
        for o in ops:
            if ndep[o["id"]] == 0:
                cand[o["eng"]].append(o["id"])
        free = {e: 0.0 for e in ENG}
        order = {e: [] for e in ENG}
        K = self.K
        cur_tbl = [None]
        last_dve = [-1]
        DVE_STALL = float(_os.environ.get("KN_DVESTALL", "250"))
        import os
        KE = {e: int(os.environ.get('KN_K_' + e, str(K))) for e in ENG}
        done = 0
        while done < n:
            best = None
            for e in ENG:
                c = cand[e]
                if not c:
                    continue
                bi, bt = None, None
                for oid in c[:KE[e]]:
                    t = max(ready_t[oid], free[e])
                    if e == "dve" and last_dve[0] in ops[oid]["deps"]:
                        t += DVE_STALL
                    if e == "act":
                        tb_ = ops[oid]["tbl"]
                        if tb_ is not None and cur_tbl[0] is not None and tb_ != cur_tbl[0]:
                            t += 1300.0
                    if bt is None or t < bt - 1e-9:
                        bi, bt = oid, t
                if best is None or bt < best[0] - 1e-9 or (abs(bt - best[0]) <= 1e-9 and bi < best[1]):
                    best = (bt, bi, e)
            bt, oid, e = best
            o = ops[oid]
            cand[e].remove(oid)
            order[e].append(oid)
            if e == "act" and o["tbl"] is not None:
                cur_tbl[0] = o["tbl"]
            if e == "dve":
                last_dve[0] = oid
            if o["dma"]:
                free[e] = bt + 60.0
                finish[oid] = bt + 2000.0 + o["nbytes"] / 120.0
            else:
                free[e] = bt + o["cost"]
                finish[oid] = free[e]
            o["start"] = bt
            done += 1
            for u in users[oid]:
                ndep[u] -= 1
                if finish[oid] > ready_t[u]:
                    ready_t[u] = finish[oid]
                if ndep[u] == 0:
                    bisect.insort(cand[ops[u]["eng"]], u)
        self.model_span = max(finish) if n else 0.0
        return order

    def _semkey(self, ek, idx):
        if isinstance(ek, tuple):
            return (ek, (idx - 1) // 100), 16 * ((idx - 1) % 100 + 1), 16
        return (ek, (idx - 1) // EPOCH), (idx - 1) % EPOCH + 1, 1

    def emit(self):
        nc = self.nc
        ops = self.ops
        order = self._schedule()
        ident = {}
        cnt = {}
        for e in ENG:
            for oid in order[e]:
                if not ops[oid]["dma"]:
                    cnt[e] = cnt.get(e, 0) + 1
                    ident[oid] = (e, cnt[e])
        dmas = [o for o in ops if o["dma"]]
        dmas.sort(key=lambda o: (o.get("start", o["id"]), o["id"]))
        rr = 0
        prev_on_vq = {}
        chain = {}
        for o in dmas:
            ek = ("dma", rr)
            rr = (rr + 1) % NDMA
            cnt[ek] = cnt.get(ek, 0) + 1
            ident[o["id"]] = (ek, cnt[ek])
            if ek in prev_on_vq:
                chain[o["id"]] = prev_on_vq[ek]
            prev_on_vq[ek] = o["id"]
        q = {e: [] for e in ENG}
        keys = set()
        for e in ENG:
            seen = {}
            for oid in order[e]:
                o = ops[oid]
                deps = set(o["deps"])
                if oid in chain:
                    deps.add(chain[oid])
                need = {}
                for d in deps:
                    dk, di = ident[d]
                    if dk == e and not o["dma"] and (not SAME_SYNC[e] or (DVE_MIN and d not in o["hard"])):
                        continue
                    if need.get(dk, 0) < di:
                        need[dk] = di
                waits = []
                for dk, di in need.items():
                    if seen.get(dk, 0) >= di:
                        continue
                    seen[dk] = di
                    waits.append((dk, di))
                    keys.add(self._semkey(dk, di)[0])
                ek, idx = ident[oid]
                keys.add(self._semkey(ek, idx)[0])
                q[e].append((waits, o["fn"], ek, idx))
        final_waits = [ident[oid] for oid in prev_on_vq.values()]
        with ExitStack() as st:
            sems = {}
            for i, k in enumerate(sorted(keys, key=str)):
                sems[k] = st.enter_context(nc.semaphore("s%d" % i))
            block = st.enter_context(nc.Block())

            def replay(ename):
                def body(e):
                    for waits, fn, ek, idx in q[ename]:
                        for dk, di in waits:
                            k, v, _ = self._semkey(dk, di)
                            e.wait_ge(sems[k], v)
                        ins = fn(e)
                        k, v, inc = self._semkey(ek, idx)
                        ins.then_inc(sems[k], inc)
                    if ename == "sp":
                        for dk, di in final_waits:
                            k, v, _ = self._semkey(dk, di)
                            e.wait_ge(sems[k], v)
                return body

            block.tensor(replay("pe"))
            block.scalar(replay("act"))
            block.vector(replay("dve"))
            block.gpsimd(replay("pool"))
            block.sync(replay("sp"))

    def mm(self, out, lhsT, rhs, start=True, stop=True, **kw):
        rd = [lhsT, rhs] + ([] if start else [out])
        reg = ap_region(out)
        banks = range(reg[3] // 2048, (reg[4] - 1) // 2048 + 1)
        if not hasattr(self, "last_mm_bank"):
            self.last_mm_bank = {}
        after = [self.last_mm_bank[b_] for b_ in banks if b_ in self.last_mm_bank]
        oid = self.add("pe", lambda e: e.matmul(out, lhsT, rhs, start=start, stop=stop, **kw), rd, [out],
                       cost=70.0 + 0.36 * _fsz(out), order_after=after)
        for b_ in banks:
            self.last_mm_bank[b_] = oid
        return oid

    def tr(self, out, in_, ident):
        reg = ap_region(out)
        banks = range(reg[3] // 2048, (reg[4] - 1) // 2048 + 1)
        if not hasattr(self, "last_mm_bank"):
            self.last_mm_bank = {}
        after = [self.last_mm_bank[b_] for b_ in banks if b_ in self.last_mm_bank]
        oid = self.add("pe", lambda e: e.transpose(out, in_, ident), [in_, ident], [out], cost=150.0, order_after=after)
        for b_ in banks:
            self.last_mm_bank[b_] = oid
        return oid

    def act(self, out, in_, func, bias=None, scale=None, eng="act"):
        rd = [in_]
        kw = {}
        if bias is not None:
            kw["bias"] = bias
            if not isinstance(bias, (int, float)):
                rd.append(bias)
        if scale is not None:
            kw["scale"] = scale
            if not isinstance(scale, (int, float)):
                rd.append(scale)
        tbl = None if func in (AF.Copy, AF.Identity) else str(func)
        return self.add(eng, lambda e: e.activation(out, in_, func, **kw), rd, [out], cost=220.0 + 0.72 * _fsz(out), tbl=tbl)

    def copy(self, eng, out, in_):
        if eng == "act":
            return self.add("act", lambda e: e.activation(out, in_, AF.Copy), [in_], [out], cost=220.0 + 0.72 * _fsz(out))
        return self.add(eng, lambda e: e.tensor_copy(out, in_), [in_], [out], cost=_vcost(eng, out))

    def tt(self, eng, out, in0, in1, op):
        return self.add(eng, lambda e: e.tensor_tensor(out, in0, in1, op), [in0, in1], [out], cost=_vcost(eng, out))

    def ts(self, eng, out, in0, s1, s2, op0, op1=None):
        rd = [in0] + [s for s in (s1, s2) if s is not None and not isinstance(s, (int, float))]
        if op1 is None:
            return self.add(eng, lambda e: e.tensor_scalar(out, in0, s1, None, op0), rd, [out], cost=_vcost(eng, out))
        return self.add(eng, lambda e: e.tensor_scalar(out, in0, s1, s2, op0, op1), rd, [out], cost=_vcost(eng, out))

    def stt(self, eng, out, in0, scalar, in1, op0, op1):
        rd = [in0, in1] + ([] if isinstance(scalar, (int, float)) else [scalar])
        return self.add(eng, lambda e: e.scalar_tensor_tensor(out, in0, scalar, in1, op0, op1), rd, [out], cost=_vcost(eng, out))

    def scan(self, eng, out, d0, d1, init, op0=ALU.mult, op1=ALU.add):
        rd = [d0, d1] + ([] if isinstance(init, (int, float)) else [init])
        return self.add(eng, lambda e: e.tensor_tensor_scan(out, d0, d1, init, op0, op1), rd, [out], cost=100.0 + 2.1 * _fsz(out))

    def memset(self, eng, out, val):
        return self.add(eng, lambda e: e.memset(out, val), [], [out], cost=_vcost(eng, out))

    def recip(self, out, in_):
        return self.add("dve", lambda e: e.reciprocal(out, in_), [in_], [out], cost=100.0 + 4.0 * _fsz(out))

    def dma(self, out, in_, eng="sp", slow=False):
        kw = {"allow_slow_non_contiguous": True} if slow else {}
        nb = _fsz(out) * out.shape[0] * DSZ.get(out.dtype, 4)
        return self.add(eng, lambda e: e.dma_start(out=out, in_=in_, **kw), [in_], [out], dma=True, nbytes=nb * (6 if slow else 1))


class Arena:
    def __init__(self, nc, name, words):
        self.t = nc.alloc_sbuf_tensor(name, [128, words], F32)
        self.words = words

    def view(self, off_w, shape, dt):
        n = 1
        for s in shape[1:]:
            n *= s
        nw = (n * DSZ[dt] + 3) // 4
        assert off_w + nw <= self.words, (off_w, nw, self.words)
        v = self.t[0:shape[0], off_w:off_w + nw]
        if dt != F32:
            v = v.bitcast(dt)
            v = v[:, 0:n]
        if len(shape) > 2:
            names = " ".join("d%d" % i for i in range(1, len(shape)))
            kw = {"d%d" % i: shape[i] for i in range(1, len(shape))}
            v = v.rearrange("p (%s) -> p %s" % (names, names), **kw)
        return v


class Bump:
    def __init__(self, arena, lo_w, hi_w):
        self.a = arena
        self.lo = lo_w
        self.hi = hi_w
        self.p = lo_w

    def alloc(self, shape, dt):
        n = 1
        for s in shape[1:]:
            n *= s
        nw = (n * DSZ[dt] + 3) // 4
        nw = (nw + 7) // 8 * 8
        assert self.p + nw <= self.hi, ("bump overflow", self.p, nw, self.hi)
        v = self.a.view(self.p, shape, dt)
        self.p += nw
        return v

    def reset(self):
        self.p = self.lo


def rev(ap):
    dims = [list(x) for x in ap.ap]
    assert len(dims) == 2
    st, cn = dims[1]
    return AP(ap.tensor, ap.offset + st * (cn - 1), [dims[0], [-st, cn]])


def bc_free(ap, n, axis_pos):
    dims = [list(x) for x in ap.ap]
    dims.insert(axis_pos, [0, n])
    return AP(ap.tensor, ap.offset, dims)


COLCH = [(0, 256)] + [(256 + 512 * i, 512) for i in range(4)]


class K:
    pass


def build_program(n_layers=4, final_norm=True):
    nc = bass.Bass("TRN2", target_bir_lowering=False)
    P = Prog(nc)
    k = K()
    k.nc, k.P = nc, P

    def din(name, shape, dt=F32):
        return nc.dram_tensor(name, list(shape), dt, kind="ExternalInput").ap()

    W = {}
    W["x"] = din("x", [T, D])
    W["ctx"] = din("ctx", [C, D])
    W["c"] = din("c", [D])
    W["c_ctx"] = din("c_ctx", [D])
    shapes = dict(
        ada_w0=[D, 3 * D], ada_b0=[3 * D], norm0=[D], w_in0=[D, 2816], conv_w0=[4, 1408], conv_b0=[1408],
        lru_wa0=[2, 16, 88, 88], lru_ba0=[2, 1408], lru_wx0=[2, 16, 88, 88], lru_bx0=[2, 1408],
        lru_lam0=[2, 1408], w_out0=[1408, D],
        ada_w1=[D, 3 * D], ada_b1=[3 * D], norm1=[D], w_in1=[D, 2560], sink1=[16], w_out1=[D, D],
        ada_w2=[D, 3 * D], ada_b2=[3 * D], norm2=[D], w_in2=[D, 4096], w_out2=[D, D],
        ada_w3=[D, 3 * D], ada_b3=[3 * D], norm3=[D], w_in3=[D, 2048],
        s5_a_re3=[2, 64, 64], s5_a_im3=[2, 64, 64], s5_log_dt3=[2, 64],
        s5_b_re3=[2, 64, 64, 16], s5_b_im3=[2, 64, 64, 16], s5_c_re3=[2, 64, 16, 64], s5_c_im3=[2, 64, 16, 64],
        s5_d3=[D], glu_w3=[D, D], glu_b3=[D], w_out3=[D, D], norm_f=[D],
    )
    for nm, sh in shapes.items():
        W[nm] = din(nm, sh)
    W["ident_f"] = din("ident_f", [128, 128])
    W["ident_b"] = din("ident_b", [128, 128], BF16)
    W["pswap"] = din("pswap", [128, 128], BF16)
    W["m_prev"] = din("m_prev", [128, 128], BF16)
    W["m_next"] = din("m_next", [128, 128], BF16)
    W["cosT"] = din("cosT", [128, T])
    W["sinT"] = din("sinT", [128, T])
    W["natab"] = din("natab", [16, len(na_plan()[0]), 128, 128])
    W["s5_kf"] = din("s5_kf", [128, 4, 8])
    W["s5_sgn"] = din("s5_sgn", [128, 1])
    W["s5_msk"] = din("s5_msk", [128, 2, 128])
    W["s5_iota"] = din("s5_iota", [128, 320])
    W["s5_wd"] = din("s5_wd", [128, 8, 240], BF16)
    W["s5_v4"] = din("s5_v4", [128, 4, 240], BF16)
    if final_norm:
        out = nc.dram_tensor("out", [T, D], F32, kind="ExternalOutput").ap()
    else:
        out = nc.dram_tensor("out", [TT, D], F32, kind="ExternalOutput").ap()
    k.W = W
    k.hspill = nc.dram_tensor("hspill", [128, NCH, TT], F32, kind="Internal").ap()

    AW = 52000
    ar = Arena(nc, "arena", AW)
    k.ar = ar
    k.ps = nc.alloc_psum_tensor("psum", [128, 4096], F32)
    k.bank_rr = 0

    def bank(n=1):
        if k.bank_rr + n > 8:
            k.bank_rr = 0
        b = k.bank_rr
        k.bank_rr = (k.bank_rr + n) % 8
        return k.ps[:, 512 * b:512 * (b + n)]
    k.bank = bank

    CW = 5900
    RW = TT * NCH
    k.cb = Bump(ar, 0, CW)
    k.R1 = (CW, CW + RW)
    k.R2 = (CW + RW, CW + 2 * RW)
    k.R3 = (CW + 2 * RW, AW)
    assert k.R3[1] - k.R3[0] >= TT * NCH // 2

    k.ident_f = k.cb.alloc([128, 128], F32)
    k.ident_b = k.cb.alloc([128, 128], BF16)
    k.ones_b = k.cb.alloc([128, 128], BF16)
    P.dma(k.ident_f, W["ident_f"])
    P.dma(k.ident_b, W["ident_b"])
    P.memset("pool", k.ones_b, 1.0)
    k.oneb = k.cb.alloc([128, 1], F32)
    k.epsb = k.cb.alloc([128, 1], F32)
    P.memset("pool", k.oneb, 1.0)
    P.memset("pool", k.epsb, EPS)
    k.halfpi = k.cb.alloc([128, 1], F32)
    P.memset("pool", k.halfpi, math.pi / 2)
    k.mod = [k.cb.alloc([128, 24, 2], F32) for _ in range(4)]
    k.gmul = [k.cb.alloc([128, NCH, 2], F32) for _ in range(4)]
    k.gnorm = k.cb.alloc([128, 5, NCH], F32)

    modulation_setup(k)
    k.lb = Bump(ar, k.cb.p, CW)
    phase0(k)
    hreg = k.R1
    import os
    lsel = os.environ.get("KN_LAYERS")
    llist = [int(x) for x in lsel.split(",")] if lsel else list(range(n_layers))
    k.llist = llist
    if llist:
        modulation_layer(k, llist[0])
    for li, l in enumerate(llist):
        P.full_barrier()
        k.next_layer = llist[li + 1] if li + 1 < len(llist) else None
        hreg = layer(k, l, hreg)
    P.full_barrier()
    finalize(k, hreg, out, final_norm)
    P.emit()
    return nc


def phase0(k):
    P, W = k.P, k.W
    k.hT = k.ar.view(k.R1[0], [128, NCH, TT], F32)
    wb = Bump(k.ar, k.R2[0], k.R2[1])
    tiles = [wb.alloc([128, D], F32) for _ in range(4)]
    for j in range(TT // 128):
        xt = tiles[j % 4]
        src = W["ctx"][128 * j:128 * (j + 1), :] if j < 2 else W["x"][128 * (j - 2):128 * (j - 1), :]
        P.dma(xt, src)
        for half in range(2):
            ps = k.bank()
            for q in range(4):
                kk = half * 4 + q
                P.tr(ps[:, 128 * q:128 * (q + 1)], xt[:, 128 * kk:128 * (kk + 1)], k.ident_f)
            dst = k.hT[:, half * 4:half * 4 + 4, 128 * j:128 * (j + 1)]
            P.copy("act" if half == 0 else "dve", dst, ps.rearrange("p (q c) -> p q c", q=4))


def modulation_setup(k):
    P, W = k.P, k.W
    k.vec = k.cb.alloc([128, NCH, 2], F32)
    k.vecb = k.cb.alloc([128, NCH, 2], BF16)
    P.dma(k.vec[:, :, 0], W["c"].rearrange("(k p) -> p k", p=128), slow=True)
    P.dma(k.vec[:, :, 1], W["c_ctx"].rearrange("(k p) -> p k", p=128), slow=True)
    P.act(k.vecb, k.vec, AF.Silu)
    names = ["norm0", "norm1", "norm2", "norm3", "norm_f"]
    for i, nm in enumerate(names):
        P.dma(k.gnorm[:, i, :], W[nm].rearrange("(k p) -> p k", p=128), slow=True)
    k.adaw = [k.cb.alloc([128, NCH, 128], BF16) for _ in range(2)]
    k.adab = k.cb.alloc([128, 4, 24], F32)
    k.ada_it = 0


def modulation_layer(k, l):
    P, W = k.P, k.W
    bias = k.adab[:, l, :]
    P.dma(bias, W["ada_b%d" % l].rearrange("(k p) -> p k", p=128), slow=True)
    for part in range(3):
        ps = k.bank()
        for m in range(8):
            wt = k.adaw[k.ada_it % 2]
            k.ada_it += 1
            c0 = 1024 * part + 128 * m
            P.dma(wt, W["ada_w%d" % l][:, c0:c0 + 128].rearrange("(k p) n -> p k n", p=128), eng="pool")
            for kk in range(8):
                P.mm(ps[:, 2 * m:2 * m + 2], wt[:, kk, :], k.vecb[:, kk, :], start=(kk == 0), stop=(kk == 7))
        dst = k.mod[l][:, 8 * part:8 * part + 8, :]
        P.tt("dve", dst, ps[:, 0:16].rearrange("p (m v) -> p m v", v=2),
             bc_free(bias[:, 8 * part:8 * part + 8], 2, 2), ALU.add)
    sc = k.mod[l][:, 8:16, :]
    P.ts("dve", k.gmul[l], sc, 1.0, None, ALU.add)
    P.tt("dve", k.gmul[l], k.gmul[l], bc_free(k.gnorm[:, l, :], 2, 2), ALU.mult)


def rms_rstd(k, hT, wb, ncols, col0=0):
    P = k.P
    rstd = wb.alloc([128, ncols], F32)
    sq = wb.alloc([128, NCH, 512], BF16)
    c = 0
    while c < ncols:
        n = min(512, ncols - c)
        P.act(sq[:, :, 0:n], hT[:, :, col0 + c:col0 + c + n], AF.Square)
        ps = k.bank()
        for kk in range(NCH):
            P.mm(ps[:, 0:n], k.ones_b, sq[:, kk, 0:n], start=(kk == 0), stop=(kk == NCH - 1))
        P.act(rstd[:, c:c + n], ps[:, 0:n], AF.Sqrt, bias=k.epsb, scale=1.0 / D)
        P.recip(rstd[:, c:c + n], rstd[:, c:c + n])
        c += n
    return rstd


def layer(k, l, hreg):
    P, W = k.P, k.W
    other = k.R2 if hreg == k.R1 else k.R1
    hT = k.ar.view(hreg[0], [128, NCH, TT], F32)
    nT = k.ar.view(k.R3[0], [128, NCH, TT], BF16)
    wb = Bump(k.ar, other[0], other[1])
    rstd = rms_rstd(k, hT, wb, TT)
    tmp = [wb.alloc([128, TT], F32) for _ in range(2)]
    for kk in range(NCH):
        t = tmp[kk % 2]
        for (v, c0, n) in ((1, 0, C), (0, C, T)):
            P.stt("dve", t[:, c0:c0 + n], hT[:, kk, c0:c0 + n], k.gmul[l][:, kk, v:v + 1], rstd[:, c0:c0 + n],
                  ALU.mult, ALU.mult)
            P.act(nT[:, kk, c0:c0 + n], t[:, c0:c0 + n], AF.Identity, bias=k.mod[l][:, kk, v:v + 1], scale=1.0)
    for kk in range(NCH):
        P.dma(k.hspill[:, kk, :], hT[:, kk, :])
    if k.next_layer is not None:
        modulation_layer(k, k.next_layer)
    free = [(k.R1[0], k.R2[1])]
    abump = Bump(k.ar, k.R1[0], k.R1[1])
    wbump = Bump(k.ar, k.R2[0], k.R2[1])
    lat_only = (l == 3)
    if l == 0:
        chunks = mixer_rglru(k, nT, abump, wbump)
    elif l == 1:
        chunks = mixer_swa(k, nT, abump, wbump)
    elif l == 2:
        chunks = mixer_na(k, nT, abump, wbump)
    else:
        chunks = mixer_s5(k, nT, abump, wbump)
    newh = k.ar.view(k.R2[0], [128, NCH, TT], F32)
    sb = Bump(k.ar, k.R3[0], k.R3[1])
    hold = [sb.alloc([128, TT], F32) for _ in range(2)]
    nk = len(chunks)
    kp = chunks[0][0].shape[0]
    wo = [sb.alloc([kp, nk, 128], BF16) for _ in range(2)]
    for m in range(NCH):
        ho = hold[m % 2]
        P.dma(ho, k.hspill[:, m, :])
        wt = wo[m % 2]
        for ci, (a_ap, r0) in enumerate(chunks):
            pass
        r0s = [r0 for (_, r0) in chunks]
        assert all(r0s[i] == r0s[0] + i * kp for i in range(nk))
        P.dma(wt, W["w_out%d" % l][r0s[0]:r0s[0] + nk * kp, 128 * m:128 * (m + 1)].rearrange("(c p) n -> p c n", p=kp),
              eng="pool")
        for (c0, n) in COLCH:
            if lat_only and c0 < C:
                P.copy("pool", newh[:, m, c0:c0 + n], ho[:, c0:c0 + n])
                continue
            v = 1 if c0 < C else 0
            ps = k.bank()
            for ci, (a_ap, r0) in enumerate(chunks):
                P.mm(ps[:, 0:n], wt[:, ci, :], a_ap[:, c0:c0 + n], start=(ci == 0), stop=(ci == nk - 1))
            P.stt("dve", newh[:, m, c0:c0 + n], ps[:, 0:n], k.mod[l][:, 16 + m, v:v + 1], ho[:, c0:c0 + n],
                  ALU.mult, ALU.add)
    return k.R2


def finalize(k, hreg, out, final_norm):
    P, W = k.P, k.W
    hT = k.ar.view(hreg[0], [128, NCH, TT], F32)
    other = k.R2 if hreg == k.R1 else k.R1
    wb = Bump(k.ar, other[0], other[1])
    if final_norm:
        rstd = rms_rstd(k, hT, wb, T, col0=C)
        for kk in range(NCH):
            P.stt("dve", hT[:, kk, C:TT], hT[:, kk, C:TT], k.gnorm[:, 4, kk:kk + 1], rstd, ALU.mult, ALU.mult)
        j0 = 2
    else:
        j0 = 0
    ot = [wb.alloc([128, D], F32) for _ in range(3)]
    for j in range(j0, TT // 128):
        o = ot[j % 3]
        for half in range(2):
            ps = k.bank()
            for q in range(4):
                kk = half * 4 + q
                P.tr(ps[:, 128 * q:128 * (q + 1)], hT[:, kk, 128 * j:128 * (j + 1)], k.ident_f)
            P.copy("act" if half == 0 else "dve", o[:, 512 * half:512 * (half + 1)], ps)
        r = 128 * (j - j0)
        P.dma(out[r:r + 128, :], o)


def uraw_alias(k, uraw, n):
    return uraw[:, 0:n]


def mixer_rglru(k, nT, abump, wbump):
    P, W = k.P, k.W
    NB, BW = 16, 88
    aT = abump.alloc([BW, NB, TT], BF16)
    wb = wbump
    k.lb.reset()
    lb = k.lb
    cw = lb.alloc([BW, NB, 4], F32)
    cbias = lb.alloc([BW, NB], F32)
    gb = lb.alloc([BW, 2, 2, NB], F32)
    lam = lb.alloc([BW, 2, NB], F32)
    cl = lb.alloc([BW, 2, NB], F32)
    for j in range(4):
        P.dma(cw[:, :, j], W["conv_w0"][j].rearrange("(k p) -> p k", p=BW), slow=True)
    P.dma(cbias, W["conv_b0"].rearrange("(k p) -> p k", p=BW), slow=True)
    for d in range(2):
        P.dma(gb[:, 0, d], W["lru_ba0"][d].rearrange("(k p) -> p k", p=BW), slow=True)
        P.dma(gb[:, 1, d], W["lru_bx0"][d].rearrange("(k p) -> p k", p=BW), slow=True)
        P.dma(lam[:, d], W["lru_lam0"][d].rearrange("(k p) -> p k", p=BW), slow=True)
    P.act(cl, lam, AF.Exp, scale=-1.0)
    P.act(cl, cl, AF.Ln, bias=k.oneb[0:BW], scale=1.0)
    P.ts("dve", cl, cl, -8.0, None, ALU.mult)
    wa = lb.alloc([BW, 32, BW], BF16)
    wx = lb.alloc([BW, 32, BW], BF16)
    P.dma(wa, W["lru_wa0"].rearrange("d k i j -> i (d k) j"), eng="pool")
    P.dma(wx, W["lru_wx0"].rearrange("d k i j -> i (d k) j"), eng="pool")
    win = [wb.alloc([128, NCH, 2, BW], BF16) for _ in range(2)]
    PADL = 2
    UW = TT + 8
    OC, OL = 2, 2 + C + 3
    uraw = wb.alloc([BW, UW], F32)
    P.memset("pool", uraw, 0.0)
    u = wb.alloc([BW, TT], F32)
    ub = wb.alloc([BW, TT], BF16)
    sg = wb.alloc([BW, TT], BF16)
    ta = wb.alloc([BW, TT], F32)
    tb = wb.alloc([BW, TT], F32)
    tc = wb.alloc([BW, TT], F32)
    h0 = wb.alloc([BW, TT], F32)
    h1 = tc
    tcb = uraw_alias(k, uraw, TT)
    def upos(c0):
        return OC + c0 if c0 < C else OL + (c0 - C)

    for b in range(NB):
        wt = win[b % 2]
        P.dma(wt[:, :, 0, :], W["w_in0"][:, BW * b:BW * (b + 1)].rearrange("(k p) n -> p k n", p=128), eng="pool")
        P.dma(wt[:, :, 1, :], W["w_in0"][:, 1408 + BW * b:1408 + BW * (b + 1)].rearrange("(k p) n -> p k n", p=128), eng="pool")

        if b > 0:
            P.memset("pool", uraw[:, 0:OC], 0.0)
            P.memset("pool", uraw[:, OC + C:OL], 0.0)

        def conv_chunk(c0, n):
            o = upos(c0)
            P.ts("dve", u[:, c0:c0 + n], uraw[:, o - 2:o - 2 + n], cw[:, b, 0:1], cbias[:, b:b + 1], ALU.mult, ALU.add)
            for j in range(1, 4):
                P.stt("dve", u[:, c0:c0 + n], uraw[:, o - 2 + j:o - 2 + j + n], cw[:, b, j:j + 1], u[:, c0:c0 + n],
                      ALU.mult, ALU.add)
            P.copy("act", ub[:, c0:c0 + n], u[:, c0:c0 + n])

        for ci, (c0, n) in enumerate(COLCH):
            ps = k.bank()
            for kk in range(NCH):
                P.mm(ps[0:BW, 0:n], wt[:, kk, 0, :], nT[:, kk, c0:c0 + n], start=(kk == 0), stop=(kk == NCH - 1))
            o = upos(c0)
            P.copy("act", uraw[:, o:o + n], ps[0:BW, 0:n])
            ps2 = k.bank()
            for kk in range(NCH):
                P.mm(ps2[0:BW, 0:n], wt[:, kk, 1, :], nT[:, kk, c0:c0 + n], start=(kk == 0), stop=(kk == NCH - 1))
            P.act(sg[:, c0:c0 + n], ps2[0:BW, 0:n], AF.Silu)
            if ci == 0:
                conv_chunk(c0, n)
            elif ci >= 2:
                conv_chunk(*COLCH[ci - 1])
        conv_chunk(*COLCH[-1])
        for d in range(2):
            hd = h0 if d == 0 else h1
            order = list(range(5)) if d == 0 else [0, 4, 3, 2, 1]
            prev_c = None
            for oi, ci in enumerate(order):
                c0, n = COLCH[ci]
                sl = slice(c0, c0 + n)
                ps = k.bank()
                P.mm(ps[0:BW, 0:n], wa[:, d * NB + b, :], ub[:, sl])
                P.act(ta[:, sl], ps[0:BW, 0:n], AF.Sigmoid, bias=gb[:, 0, d, b:b + 1], scale=1.0)
                ps2 = k.bank()
                P.mm(ps2[0:BW, 0:n], wx[:, d * NB + b, :], ub[:, sl])
                P.act(tb[:, sl], ps2[0:BW, 0:n], AF.Sigmoid, bias=gb[:, 1, d, b:b + 1], scale=1.0)
                P.act(ta[:, sl], ta[:, sl], AF.Exp, scale=cl[:, d, b:b + 1])
                P.tt("dve", tc[:, sl], ta[:, sl], ta[:, sl], ALU.mult) if d == 0 else P.tt("dve", tcb[:, sl], ta[:, sl], ta[:, sl], ALU.mult)
                tcc = tc if d == 0 else tcb
                P.act(tcc[:, sl], tcc[:, sl], AF.Sqrt, bias=k.oneb[0:BW], scale=-1.0)
                P.tt("dve", tb[:, sl], tb[:, sl], u[:, sl], ALU.mult)
                P.tt("dve", tb[:, sl], tb[:, sl], tcc[:, sl], ALU.mult)
                if d == 0:
                    init = 0.0 if oi == 0 else hd[:, c0 - 1:c0]
                    P.scan("dve", hd[:, sl], ta[:, sl], tb[:, sl], init)
                else:
                    if oi == 0:
                        init = 0.0
                    elif oi == 1:
                        init = hd[:, 0:1]
                    else:
                        init = hd[:, c0 + n:c0 + n + 1]
                    P.scan("dve", rev(hd[:, sl]), rev(ta[:, sl]), rev(tb[:, sl]), init)
        for (c0, n) in COLCH:
            sl = slice(c0, c0 + n)
            P.tt("dve", h0[:, sl], h0[:, sl], h1[:, sl], ALU.add)
            P.tt("dve", aT[:, b, sl], h0[:, sl], sg[:, sl], ALU.mult)
    return [(aT[:, b, :], BW * b) for b in range(NB)]


NEG = -30000.0


def na_w0(r):
    return min(max(r - 4, 0), 24)


def na_plan():
    variants = {}
    plan = []
    for i in range(16):
        r = 2 * i
        lo = na_w0(r) // 2
        hi = (na_w0(r + 1) + 7) // 2
        lst = []
        for kt in range(lo, hi + 1):
            key = (2 * kt - r, na_w0(r) - r, na_w0(r + 1) - (r + 1))
            if key not in variants:
                variants[key] = len(variants)
            lst.append((kt, variants[key]))
        plan.append(lst)
    return variants, plan


def mixer_swa(k, nT, abump, wbump):
    import os
    st = os.environ.get("KN_SWA_STRICT", "0") == "1"
    if st:
        k.P.begin_strict()
    r = mixer_attn(k, nT, "swa")
    if st:
        k.P.end_strict()
    return r


def mixer_na(k, nT, abump, wbump):
    import os
    st = os.environ.get("KN_NA_STRICT", "0") == "1"
    if st:
        k.P.begin_strict()
    r = mixer_attn(k, nT, "na")
    if st:
        k.P.end_strict()
    return r


def mixer_attn(k, nT, kind):
    P, W = k.P, k.W
    swa = kind == "swa"
    l = 1 if swa else 2
    win = W["w_in%d" % l]
    aT = k.ar.view(k.R1[0], [128, NCH, TT], BF16)
    wb = Bump(k.ar, k.R1[0] + TT * NCH // 2, k.R2[1])
    k.lb.reset()
    lb = k.lb
    NT = TT // 128
    if swa:
        pswap = lb.alloc([128, 128], BF16)
        m_prev = lb.alloc([128, 128], BF16)
        m_next = lb.alloc([128, 128], BF16)
        P.dma(pswap, W["pswap"])
        P.dma(m_prev, W["m_prev"])
        P.dma(m_next, W["m_next"])
        cosT = wb.alloc([128, T], F32)
        sinT = wb.alloc([128, T], F32)
        P.dma(cosT, W["cosT"])
        P.dma(sinT, W["sinT"])
        sk = lb.alloc([128, 16], F32)
        P.dma(sk, W["sink1"].rearrange("(o h) -> o h", o=1).partition_broadcast(128), slow=True)
        P.act(sk, sk, AF.Exp)
        sinkcol = lb.alloc([128, 8], F32)
        skv = sk.rearrange("p (a b) -> p a b", b=2)
        P.copy("pool", sinkcol[0:64, :], skv[0:64, :, 0])
        P.copy("pool", sinkcol[64:128, :], skv[64:128, :, 1])
        qoff, koff, voff, goff = 0, 1024, 1280, 1536
    else:
        variants, plan = na_plan()
        NV = len(variants)
        qoff, koff, voff, goff = 0, 1024, 2048, 3072
    sets = []
    for _ in range(2):
        st = {}
        st["w"] = wb.alloc([128, NCH, 4, 128], BF16)
        st["qT"] = wb.alloc([128, TT], BF16)
        st["kT"] = wb.alloc([128, TT], BF16)
        st["V"] = wb.alloc([128, NT, 128], BF16)
        st["sg"] = wb.alloc([128, TT], BF16)
        if swa:
            st["qr"] = wb.alloc([128, T], BF16)
            st["kr"] = wb.alloc([128, T], BF16)
        else:
            st["tab0"] = wb.alloc([128, NV, 128], BF16)
            st["tab1"] = wb.alloc([128, NV, 128], BF16)
        sets.append(st)
    t1s = [wb.alloc([128, 512], F32) for _ in range(2)]
    t2s = [wb.alloc([128, 512], F32) for _ in range(2)]
    qfs = [wb.alloc([128, 512], F32) for _ in range(2)]
    PTs = [wb.alloc([128, 8, 128], BF16) for _ in range(3)]
    rdens = [wb.alloc([128, 128], F32) for _ in range(2)]
    oas = [wb.alloc([128, 128], F32) for _ in range(2)]
    cnt = {"t": 0, "pt": 0, "r": 0, "od": 0, "ip": 0}

    def bank_ip():
        b_ = 6 + cnt["ip"] % 2
        cnt["ip"] += 1
        return k.ps[:, 512 * b_:512 * (b_ + 1)]

    def bank_od():
        b_ = 4 + cnt["od"] % 2
        cnt["od"] += 1
        return k.ps[:, 512 * b_:512 * (b_ + 1)]

    def wcols(dst, c0, n):
        P.dma(dst, win[:, c0:c0 + n].rearrange("(kk p) n -> p kk n", p=128), eng="pool")

    import os
    SKIP = os.environ.get('KN_SKIP', '').split(',')
    def inproj_units(hp):
        st = sets[hp % 2]
        w = st["w"]
        units = []

        def u_weights():
            wcols(w[:, :, 0, :], qoff + 128 * hp, 128)
            if swa:
                kvh = hp // 2
                for e in range(2):
                    wcols(w[:, :, 1, 64 * e:64 * e + 64], koff + 64 * kvh, 64)
                    wcols(w[:, :, 2, 64 * e:64 * e + 64], voff + 64 * kvh, 64)
            else:
                wcols(w[:, :, 1, :], koff + 128 * hp, 128)
                wcols(w[:, :, 2, :], voff + 128 * hp, 128)
                for e in range(2):
                    for v0 in range(0, NV, 3):
                        v1 = min(NV, v0 + 3)
                        P.dma(st["tab%d" % e][:, v0:v1, :], W["natab"][2 * hp + e, v0:v1].rearrange("v p q -> p v q"), eng="pool")
            wcols(w[:, :, 3, :], goff + 128 * hp, 128)
        units.append(u_weights)

        def mk_q(c0, n):
            def f():
                ps = bank_ip()
                for kk in range(NCH):
                    P.mm(ps[:, 0:n], w[:, kk, 0, :], nT[:, kk, c0:c0 + n], start=(kk == 0), stop=(kk == NCH - 1))
                if swa and c0 >= C:
                    qf = qfs[cnt["t"] % 2]
                    t1 = t1s[cnt["t"] % 2]
                    t2 = t2s[cnt["t"] % 2]
                    cnt["t"] += 1
                    lc = c0 - C
                    P.act(qf[:, 0:n], ps[:, 0:n], AF.Copy, scale=0.125)
                    P.act(st["qT"][:, c0:c0 + n], ps[:, 0:n], AF.Copy, scale=0.125)
                    P.tt("dve", t1[:, 0:n], qf[:, 0:n], cosT[:, lc:lc + n], ALU.mult)
                    ps2 = bank_ip()
                    P.mm(ps2[:, 0:n], pswap, st["qT"][:, c0:c0 + n])
                    P.tt("dve", t2[:, 0:n], ps2[:, 0:n], sinT[:, lc:lc + n], ALU.mult)
                    P.tt("dve", st["qr"][:, lc:lc + n], t1[:, 0:n], t2[:, 0:n], ALU.add)
                else:
                    P.act(st["qT"][:, c0:c0 + n], ps[:, 0:n], AF.Copy, scale=0.125)
            return f

        def mk_k(c0, n):
            def f():
                ps = bank_ip()
                for kk in range(NCH):
                    P.mm(ps[:, 0:n], w[:, kk, 1, :], nT[:, kk, c0:c0 + n], start=(kk == 0), stop=(kk == NCH - 1))
                if swa and c0 >= C:
                    qf = qfs[cnt["t"] % 2]
                    t1 = t1s[cnt["t"] % 2]
                    t2 = t2s[cnt["t"] % 2]
                    cnt["t"] += 1
                    lc = c0 - C
                    P.copy("act", qf[:, 0:n], ps[:, 0:n])
                    P.copy("act", st["kT"][:, c0:c0 + n], ps[:, 0:n])
                    P.tt("dve", t1[:, 0:n], qf[:, 0:n], cosT[:, lc:lc + n], ALU.mult)
                    ps2 = bank_ip()
                    P.mm(ps2[:, 0:n], pswap, st["kT"][:, c0:c0 + n])
                    P.tt("dve", t2[:, 0:n], ps2[:, 0:n], sinT[:, lc:lc + n], ALU.mult)
                    P.tt("dve", st["kr"][:, lc:lc + n], t1[:, 0:n], t2[:, 0:n], ALU.add)
                else:
                    P.copy("act", st["kT"][:, c0:c0 + n], ps[:, 0:n])
            return f

        def mk_g(c0, n):
            def f():
                ps = bank_ip()
                for kk in range(NCH):
                    P.mm(ps[:, 0:n], w[:, kk, 3, :], nT[:, kk, c0:c0 + n], start=(kk == 0), stop=(kk == NCH - 1))
                P.act(st["sg"][:, c0:c0 + n], ps[:, 0:n], AF.Silu)
            return f

        def mk_v(j4):
            def f():
                nj = min(4, NT - j4)
                ps = bank_ip()
                for jj in range(nj):
                    j = j4 + jj
                    for kk in range(NCH):
                        P.mm(ps[:, 128 * jj:128 * (jj + 1)], nT[:, kk, 128 * j:128 * (j + 1)], w[:, kk, 2, :],
                             start=(kk == 0), stop=(kk == NCH - 1))
                P.copy("act", st["V"][:, j4:j4 + nj, :], ps[:, 0:128 * nj].rearrange("p (j c) -> p j c", c=128))
            return f
        for (c0, n) in COLCH:
            units.append(mk_q(c0, n))
            units.append(mk_k(c0, n))
            units.append(mk_g(c0, n))
        for j4 in range(0, NT, 4):
            units.append(mk_v(j4))
        return units

    def stage1(hp, qt, e):
        st = sets[hp % 2]
        is_ctx = qt < 2
        pr = slice(64 * e, 64 * e + 64)
        tiles = []
        qraw = st["qT"][pr, 128 * qt:128 * (qt + 1)]
        for cj in range(2):
            tiles.append((st["kT"][pr, 128 * cj:128 * (cj + 1)], None, st["V"][:, cj, 64 * e:64 * e + 64], qraw))
        if not is_ctx:
            i = qt - 2
            if swa:
                qrot = st["qr"][pr, 128 * i:128 * (i + 1)]
                for j, tab in ((i - 1, m_prev), (i, None), (i + 1, m_next)):
                    if 0 <= j < 16:
                        tiles.append((st["kr"][pr, 128 * j:128 * (j + 1)], tab,
                                      st["V"][:, 2 + j, 64 * e:64 * e + 64], qrot))
            else:
                for (kt, vi) in plan[i]:
                    tiles.append((st["kT"][pr, C + 128 * kt:C + 128 * (kt + 1)], st["tab%d" % e][:, vi, :],
                                  st["V"][:, 2 + kt, 64 * e:64 * e + 64], qraw))
        nt = len(tiles)
        ps2 = k.ps[:, 1024 * e:1024 * (e + 1)]
        for t, (kT_, tab, V_, q_) in enumerate(tiles):
            o = ps2[:, 128 * t:128 * (t + 1)]
            P.mm(o, kT_, q_, start=True, stop=(tab is None))
            if tab is not None:
                P.mm(o, k.ident_b, tab, start=False, stop=True)
        PT = PTs[cnt["pt"] % 3]
        cnt["pt"] += 1
        P.act(PT[:, 0:nt, :], ps2[:, 0:128 * nt].rearrange("p (t c) -> p t c", c=128), AF.Exp)
        return (tiles, PT)

    def stage2(hp, qt, e, s1):
        st = sets[hp % 2]
        tiles, PT = s1
        nt = len(tiles)
        pr = slice(64 * e, 64 * e + 64)
        od = k.ps[:, 2048 + 512 * e:2048 + 512 * (e + 1)]
        for t, (kT_, tab, V_, q_) in enumerate(tiles):
            P.mm(od[pr, 0:128], V_, PT[:, t, :], start=(t == 0), stop=(t == nt - 1), tile_position=(0, 64 * e))
        for t in range(nt):
            P.mm(od[pr, 128:256], k.ones_b[:, 0:64], PT[:, t, :], start=(t == 0), stop=(t == nt - 1),
                 tile_position=(0, 64 * e))
        rden = rdens[cnt["r"] % 2]
        oa = oas[cnt["r"] % 2]
        cnt["r"] += 1
        if swa:
            P.ts("dve", rden[pr, :], od[pr, 128:256], sinkcol[pr, hp:hp + 1], None, ALU.add)
            P.recip(rden[pr, :], rden[pr, :])
        else:
            P.recip(rden[pr, :], od[pr, 128:256])
        P.tt("dve", oa[pr, :], od[pr, 0:128], rden[pr, :], ALU.mult)
        P.tt("pool", aT[pr, hp, 128 * qt:128 * (qt + 1)], oa[pr, :], st["sg"][pr, 128 * qt:128 * (qt + 1)], ALU.mult)

    import os
    PIPE = os.environ.get("KN_PIPE", "1") == "1"
    for u in inproj_units(0):
        u()
    for hp in range(8):
        nxt = inproj_units(hp + 1) if hp + 1 < 8 else []
        items = [(qt, e) for qt in range(NT) for e in range(2)]
        prev = None
        ui = 0
        for idx, (qt, e) in enumerate(items):
            s1 = stage1(hp, qt, e)
            if PIPE:
                if prev is not None:
                    stage2(hp, prev[0], prev[1], prev[2])
                prev = (qt, e, s1)
            else:
                stage2(hp, qt, e, s1)
            want = (len(nxt) * (idx + 1)) // len(items)
            while ui < want:
                nxt[ui]()
                ui += 1
        if PIPE and prev is not None:
            stage2(hp, prev[0], prev[1], prev[2])
        while ui < len(nxt):
            nxt[ui]()
            ui += 1
    return [(aT[:, c, :], 128 * c) for c in range(NCH)]


TWO_PI = 2.0 * math.pi
INV2PI = 1.0 / TWO_PI
PI_LO = 3.1415925


def trig_tables(k, eng, x, k32, kf, out_sin, out_cos):
    P = k.P
    P.ts("dve", kf, x, INV2PI, None, ALU.mult)
    P.copy("dve", k32, kf)
    P.copy("dve", kf, k32)
    P.stt("dve", kf, kf, -TWO_PI, x, ALU.mult, ALU.add)
    P.ts("dve", kf, kf, PI_LO, -PI_LO, ALU.min, ALU.max)
    P.act(out_sin, kf, AF.Sin)
    P.act(x, kf, AF.Abs)
    P.act(out_cos, x, AF.Sin, bias=k.halfpi, scale=-1.0)


def cmul(k, eng, out_re, out_im, are, aim, bre, bim, t1, t2, neg_im=False):
    P = k.P
    P.tt(eng, t1, are, bre, ALU.mult)
    P.tt(eng, t2, aim, bim, ALU.mult)
    P.tt(eng, out_re, t1, t2, ALU.subtract)
    P.tt(eng, t1, are, bim, ALU.mult)
    P.tt(eng, t2, aim, bre, ALU.mult)
    if neg_im:
        P.ts(eng, t1, t1, -1.0, None, ALU.mult)
        P.tt(eng, out_im, t1, t2, ALU.subtract)
    else:
        P.tt(eng, out_im, t1, t2, ALU.add)


def mixer_s5(k, nT, abump, wbump):
    P, W = k.P, k.W
    win = W["w_in3"]
    base = k.R1[0]
    yg = k.ar.view(base, [128, NCH, T], BF16)
    aT = k.ar.view(base + 8192, [128, NCH, TT], BF16)
    wb = Bump(k.ar, base + 8192, k.R2[1])
    k.lb.reset()
    lb = k.lb
    NCOL = 320
    KF = lb.alloc([128, 4, 8], F32)
    SGN = lb.alloc([128, 1], F32)
    MSK = lb.alloc([128, 2, 128], F32)
    IOTA = lb.alloc([128, NCOL], F32)
    WD = lb.alloc([128, 8, 240], BF16)
    V4 = lb.alloc([128, 4, 240], BF16)
    dcol = lb.alloc([128, NCH], F32)
    gbias = lb.alloc([128, NCH], F32)
    for dst, nm in ((KF, "s5_kf"), (SGN, "s5_sgn"), (MSK, "s5_msk"), (IOTA, "s5_iota"), (WD, "s5_wd"), (V4, "s5_v4")):
        P.dma(dst, W[nm])
    P.dma(dcol, W["s5_d3"].rearrange("(k p) -> p k", p=128), slow=True)
    P.dma(gbias, W["glu_b3"].rearrange("(k p) -> p k", p=128), slow=True)
    uTf = wb.alloc([128, TT], F32)
    uTb = wb.alloc([128, 8 * NCOL], BF16)
    ytmp = wb.alloc([128, T], F32)
    wu = wb.alloc([128, NCH, 128], BF16)
    araw = wb.alloc([8, 2, 128], F32)
    A = wb.alloc([128, 2, 8], F32)
    ldt = wb.alloc([128, 8], F32)
    ar_ = wb.alloc([128, 8], F32)
    th = wb.alloc([128, 8], F32)
    rho8 = wb.alloc([128, 8], F32)
    phis = wb.alloc([128, 8], F32)
    fx = wb.alloc([128, 4, 8, 8], F32)
    fk32 = wb.alloc([128, 4, 8, 8], I32)
    fkf = wb.alloc([128, 4, 8, 8], F32)
    fmag = wb.alloc([128, 4, 8, 8], F32)
    fsin = wb.alloc([128, 4, 8, 8], F32)
    fcos = wb.alloc([128, 4, 8, 8], F32)
    Tre = wb.alloc([128, 4, 8, 8], F32)
    Tim = wb.alloc([128, 4, 8, 8], F32)
    sm = [wb.alloc([128, 8], F32) for _ in range(6)]
    kap = wb.alloc([128, 2, 8], F32)
    braw = wb.alloc([128, 2, 8, 16], F32)
    Bb = wb.alloc([128, 2, 8, 16], F32)
    craw = wb.alloc([128, 2, 128], F32)
    Cm = wb.alloc([128, 2, 8, 16], F32)
    bt1 = wb.alloc([128, 8, 8, 16], F32)
    bt2 = wb.alloc([128, 8, 8, 16], F32)
    W2p = wb.alloc([128, 2, 8, 128], BF16)
    Zp = wb.alloc([128, 2, 8, 128], BF16)
    W3 = wb.alloc([128, 2, 8, 128], BF16)
    W2 = wb.alloc([128, 8, 2, 128], BF16)
    W1 = wb.alloc([128, 8, 2, 128], BF16)
    NG = 4
    off_r = wb.p
    rx = wb.alloc([128, NG, NCOL], F32)
    rk32 = wb.alloc([128, NG, NCOL], I32)
    gtmp = k.ar.view(off_r, [128, T], F32)
    rkf = wb.alloc([128, NG, NCOL], F32)
    rsin = wb.alloc([128, NG, NCOL], F32)
    rcos = wb.alloc([128, NG, NCOL], F32)
    U8 = [wb.alloc([128, NCOL], BF16) for _ in range(2)]
    Ssb = [wb.alloc([128, 2, NCOL], F32) for _ in range(2)]
    Gin = wb.alloc([128, 2, NCOL], F32)
    Gs = wb.alloc([128, 2, NCOL], F32)
    pt = [wb.alloc([128, NCOL], F32) for _ in range(2)]
    Hb = [wb.alloc([128, 2, NCOL + 2], BF16) for _ in range(2)]
    Y8 = [wb.alloc([128, 256], BF16) for _ in range(2)]
    for h in Hb:
        P.memset("pool", h, 0.0)
    rho_b = wb.alloc([128, NCOL], F32)

    def bankS():
        b = k.s5_rr
        k.s5_rr = (k.s5_rr + 1) % 4
        return k.ps[:, 512 * b:512 * (b + 1)]
    k.s5_rr = 0
    UL = k.ps[:, 2048:4096].rearrange("p (t c) -> p t c", t=8)

    def bc(ap, n, pos):
        return bc_free(ap, n, pos)

    import os
    STG = int(os.environ.get('KN_S5', '99'))
    for ch in range(NCH if STG >= 99 else 1):
        g0 = 8 * ch
        P.dma(wu, win[:, 128 * ch:128 * (ch + 1)].rearrange("(kk p) n -> p kk n", p=128), eng="pool")
        for (c0, n) in COLCH:
            ps = bankS()
            for kk in range(NCH):
                P.mm(ps[:, 0:n], wu[:, kk, :], nT[:, kk, c0:c0 + n], start=(kk == 0), stop=(kk == NCH - 1))
            P.copy("act", uTf[:, c0:c0 + n], ps[:, 0:n])
        P.copy("act", uTb[:, 0:TT], uTf)
        P.copy("act", uTb[:, TT:TT + C], uTf[:, 0:C])
        if STG < 2:
            continue
        for d in range(2):
            P.dma(araw[:, 0, 64 * d:64 * d + 64], W["s5_a_re3"][d, g0:g0 + 8, :])
            P.dma(araw[:, 1, 64 * d:64 * d + 64], W["s5_a_im3"][d, g0:g0 + 8, :])
            P.dma(ldt[64 * d:64 * d + 64, :],
                  W["s5_log_dt3"][d:d + 1, g0:g0 + 8].partition_broadcast(64), slow=True)
            P.dma(braw[64 * d:64 * d + 64, 0], W["s5_b_re3"][d, g0:g0 + 8].rearrange("g p j -> p g j"), slow=True)
            P.dma(braw[64 * d:64 * d + 64, 1], W["s5_b_im3"][d, g0:g0 + 8].rearrange("g p j -> p g j"), slow=True)
            P.dma(craw[:, 0, 64 * d:64 * d + 64], W["s5_c_re3"][d, g0:g0 + 8].rearrange("g i p -> (g i) p"))
            P.dma(craw[:, 1, 64 * d:64 * d + 64], W["s5_c_im3"][d, g0:g0 + 8].rearrange("g i p -> (g i) p"))
        ps = bankS()
        for x in range(2):
            P.tr(ps[:, 8 * x:8 * x + 8], araw[:, x, :], k.ident_f[0:8, 0:8])
        P.copy("act", A, ps[:, 0:16].rearrange("p (x g) -> p x g", x=2))
        ps = bankS()
        for x in range(2):
            P.tr(ps[:, 128 * x:128 * x + 128], craw[:, x, :], k.ident_f)
        P.copy("act", Cm, ps[:, 0:256].rearrange("p (x g i) -> p x g i", x=2, g=8))
        if STG < 3:
            continue
        P.act(ldt, ldt, AF.Exp)
        P.tt("dve", ar_, A[:, 0, :], ldt, ALU.mult)
        P.tt("dve", th, A[:, 1, :], ldt, ALU.mult)
        P.act(rho8, ar_, AF.Exp, scale=8.0)
        P.ts("dve", phis, th, 8.0, SGN[:, 0:1], ALU.mult, ALU.mult)
        arb = bc(bc(ar_, 4, 1), 8, 3)
        thb = bc(bc(th, 4, 1), 8, 3)
        kfb = bc(KF, 8, 2)
        P.tt("dve", fmag, arb, kfb, ALU.mult)
        P.act(fmag, fmag, AF.Exp)
        P.tt("dve", fx, thb, kfb, ALU.mult)
        trig_tables(k, "dve", fx, fk32, fkf, fsin, fcos)
        P.tt("dve", Tre, fmag, fcos, ALU.mult)
        P.tt("dve", Tim, fmag, fsin, ALU.mult)
        lre, lim = Tre[:, 3, :, 0], Tim[:, 3, :, 0]
        Are, Aim = A[:, 0, :], A[:, 1, :]
        nre, den, t1_, t2_, rd = sm[0], sm[1], sm[2], sm[3], sm[4]
        P.ts("dve", nre, lre, -1.0, None, ALU.add)
        P.tt("dve", den, Are, Are, ALU.mult)
        P.tt("dve", t1_, Aim, Aim, ALU.mult)
        P.tt("dve", den, den, t1_, ALU.add)
        P.recip(rd, den)
        P.tt("dve", t1_, nre, Are, ALU.mult)
        P.tt("dve", t2_, lim, Aim, ALU.mult)
        P.tt("dve", t1_, t1_, t2_, ALU.add)
        P.tt("dve", kap[:, 0, :], t1_, rd, ALU.mult)
        P.tt("dve", t1_, lim, Are, ALU.mult)
        P.tt("dve", t2_, nre, Aim, ALU.mult)
        P.tt("dve", t1_, t1_, t2_, ALU.subtract)
        P.tt("dve", kap[:, 1, :], t1_, rd, ALU.mult)
        s1 = bt1[:, :, 0, :]
        s2 = bt2[:, :, 0, :]
        cmul(k, "dve", Bb[:, 0], Bb[:, 1], bc(kap[:, 0, :], 16, 2), bc(kap[:, 1, :], 16, 2),
             braw[:, 0], braw[:, 1], s1, s2)
        if STG < 4:
            continue
        def fam(f, x):
            t = Tre if x == 0 else Tim
            return bc(t[:, f], 16, 3)

        def vec(v, x):
            return bc(v[:, x], 8, 2)

        def o4(t, x):
            return t[:, x].rearrange("p g (s j) -> p g s j", s=8)
        cmul(k, "dve", o4(W2p, 0), o4(W2p, 1), fam(0, 0), fam(0, 1), vec(Bb, 0), vec(Bb, 1), bt1, bt2)
        cmul(k, "dve", o4(Zp, 0), o4(Zp, 1), fam(2, 0), fam(2, 1), vec(Bb, 0), vec(Bb, 1), bt1, bt2)
        cmul(k, "dve", o4(W3, 0), o4(W3, 1), fam(1, 0), fam(1, 1), vec(Cm, 0), vec(Cm, 1), bt1, bt2, neg_im=True)
        if STG < 5:
            continue
        for x in range(2 if os.environ.get('KN_5A', '1') == '1' else 0):
            ps = bankS()
            psb = ps.bitcast(BF16)
            for g8 in range(8):
                P.tr(psb[:, 128 * g8:128 * g8 + 128], W2p[:, x, g8, :], k.ident_b)
            P.copy("act", W2[:, :, x, :], psb[:, 0:1024].rearrange("p (g c) -> p g c", g=8))
        for gq in range(2):
            psd = [bankS(), bankS()]
            for gg in range(4):
                g8 = 4 * gq + gg
                for d in range(2):
                    o = psd[d][:, 128 * gg:128 * gg + 128]
                    pr = slice(64 * d, 64 * d + 64)
                    P.mm(o, Zp[pr, 0, g8, :], W3[pr, 0, g8, :], start=True, stop=False)
                    P.mm(o, Zp[pr, 1, g8, :], W3[pr, 1, g8, :], start=False, stop=True)
            for d in range(2):
                P.tt("dve", W1[:, 4 * gq:4 * gq + 4, d, :], psd[d].rearrange("p (g c) -> p g c", g=4),
                     bc(MSK[:, d, :], 4, 1), ALU.mult)
        if STG < 6:
            continue
        for g8 in range(8):
            if g8 % NG == 0:
                P.tt("dve", rx, bc(phis[:, g8:g8 + NG], NCOL, 2), bc(IOTA, NG, 1), ALU.mult)
                trig_tables(k, "dve", rx, rk32, rkf, rsin, rcos)
            gi = g8 % NG
            cs, sn = rcos[:, gi, :], rsin[:, gi, :]
            u8 = U8[g8 % 2]
            ps = bankS()
            for s_ in range(8):
                rhs = uTb.rearrange("p (c s) -> p s c", s=8)[:, s_, :]
                P.mm(ps[:, 0:NCOL], WD[:, g8, 112 - 16 * s_:112 - 16 * s_ + 128], rhs, start=(s_ == 0), stop=(s_ == 7))
            P.copy("act", u8, ps[:, 0:NCOL])
            ssb = Ssb[g8 % 2]
            for x in range(2):
                ps = bankS()
                P.mm(ps[:, 0:NCOL], W2[:, g8, x, :], u8)
                P.copy("act", ssb[:, x, :], ps[:, 0:NCOL])
            if STG < 7:
                continue
            P.tt("dve", pt[0], ssb[:, 0, :], cs, ALU.mult)
            P.tt("dve", pt[1], ssb[:, 1, :], sn, ALU.mult)
            P.tt("dve", Gin[:, 0, :], pt[0], pt[1], ALU.add)
            P.tt("dve", pt[0], ssb[:, 1, :], cs, ALU.mult)
            P.tt("dve", pt[1], ssb[:, 0, :], sn, ALU.mult)
            P.tt("dve", Gin[:, 1, :], pt[0], pt[1], ALU.subtract)
            if STG < 8:
                continue
            P.ts("dve", rho_b, IOTA, 0.0, rho8[:, g8:g8 + 1], ALU.mult, ALU.add)
            for x in range(2):
                P.scan("dve", Gs[0:64, x, 0:288], rho_b[0:64, 0:288], Gin[0:64, x, 0:288], 0.0)
                P.scan("dve", rev(Gs[64:128, x, 32:320]), rho_b[64:128, 0:288], rev(Gin[64:128, x, 32:320]), 0.0)
            if STG < 9:
                continue
            hb = Hb[g8 % 2]
            P.tt("dve", pt[0], Gs[:, 0, :], cs, ALU.mult)
            P.tt("dve", pt[1], Gs[:, 1, :], sn, ALU.mult)
            P.tt("dve", hb[0:64, 0, 1:289], pt[0][0:64, 0:288], pt[1][0:64, 0:288], ALU.subtract)
            P.tt("dve", hb[64:128, 0, 31:319], pt[0][64:128, 32:320], pt[1][64:128, 32:320], ALU.subtract)
            P.tt("dve", pt[0], Gs[:, 1, :], cs, ALU.mult)
            P.tt("dve", pt[1], Gs[:, 0, :], sn, ALU.mult)
            P.tt("dve", hb[0:64, 1, 1:289], pt[0][0:64, 0:288], pt[1][0:64, 0:288], ALU.add)
            P.tt("dve", hb[64:128, 1, 31:319], pt[0][64:128, 32:320], pt[1][64:128, 32:320], ALU.add)
            if STG < 10:
                continue
            ps = bankS()
            o = ps[:, 0:256]
            P.mm(o, W1[:, g8, 0, :], u8[:, 32:288], start=True, stop=False)
            P.mm(o, W1[:, g8, 1, :], u8[:, 32:288], start=False, stop=False)
            P.mm(o, W3[:, 0, g8, :], hb[:, 0, 32:288], start=False, stop=False)
            P.mm(o, W3[:, 1, g8, :], hb[:, 1, 32:288], start=False, stop=True)
            y8 = Y8[g8 % 2]
            P.copy("act", y8, o)
            for t in range(8):
                hh, tq = t // 4, t % 4
                P.mm(UL[:, t, :], V4[64 * hh:64 * hh + 64, tq, 112 - 16 * g8:112 - 16 * g8 + 128],
                     y8[64 * hh:64 * hh + 64, :], start=(g8 == 0 and t % 2 == 0), stop=(g8 == 7),
                     skip_group_check=True)
        if STG < 11:
            continue
        yv = ytmp.rearrange("p (c t) -> p t c", t=8)
        P.copy("dve", yv, UL)
        P.stt("dve", ytmp, uTf[:, C:TT], dcol[:, ch:ch + 1], ytmp, ALU.mult, ALU.add)
        P.tt("dve", gtmp, ytmp, ytmp, ALU.mult)
        P.ts("dve", gtmp, gtmp, 0.044715, 1.0, ALU.mult, ALU.add)
        P.tt("dve", gtmp, gtmp, ytmp, ALU.mult)
        P.act(gtmp, gtmp, AF.Sigmoid, scale=2.0 * math.sqrt(2.0 / math.pi))
        P.tt("dve", yg[:, ch, :], gtmp, ytmp, ALU.mult)
    if STG < 12:
        return [(aT[:, c, :], 128 * c) for c in range(NCH)]
    sb2 = Bump(k.ar, base + 8192 + TT * NCH // 2, k.R2[1])
    wg = [sb2.alloc([128, NCH, 128], BF16) for _ in range(2)]
    wi = [sb2.alloc([128, NCH, 128], BF16) for _ in range(2)]
    sgs = [sb2.alloc([128, 512], F32) for _ in range(2)]
    zs = [sb2.alloc([128, 512], F32) for _ in range(2)]
    sls = [sb2.alloc([128, 512], F32) for _ in range(2)]
    it = 0
    for m in range(NCH):
        P.dma(wg[m % 2], W["glu_w3"][:, 128 * m:128 * (m + 1)].rearrange("(kk p) n -> p kk n", p=128), eng="pool")
        P.dma(wi[m % 2], win[:, 1024 + 128 * m:1024 + 128 * (m + 1)].rearrange("(kk p) n -> p kk n", p=128), eng="pool")
        for q in range(4):
            c0 = 512 * q
            ps = k.bank()
            for kk in range(NCH):
                P.mm(ps, wg[m % 2][:, kk, :], yg[:, kk, c0:c0 + 512], start=(kk == 0), stop=(kk == NCH - 1))
            sg_, z_, sl_ = sgs[it % 2], zs[it % 2], sls[it % 2]
            it += 1
            P.act(sg_, ps, AF.Sigmoid, bias=gbias[:, m:m + 1], scale=1.0)
            P.tt("dve", z_, sg_, yg[:, m, c0:c0 + 512], ALU.mult)
            ps2 = k.bank()
            for kk in range(NCH):
                P.mm(ps2, wi[m % 2][:, kk, :], nT[:, kk, C + c0:C + c0 + 512], start=(kk == 0), stop=(kk == NCH - 1))
            P.act(sl_, ps2, AF.Silu)
            P.tt("dve", aT[:, m, C + c0:C + c0 + 512], z_, sl_, ALU.mult)
    return [(aT[:, c, :], 128 * c) for c in range(NCH)]


_CACHE = {}


def host_consts():
    c = {}
    c["ident_f"] = np.eye(128, dtype=np.float32)
    c["ident_b"] = np.eye(128, dtype=np.float32).astype(ml_dtypes.bfloat16)
    m = np.arange(128)
    sw = (m // 64) * 64 + ((m % 64) + 32) % 64
    ps = np.zeros((128, 128), np.float32)
    ps[sw, m] = 1.0
    c["pswap"] = ps.astype(ml_dtypes.bfloat16)
    kk = np.arange(128)[:, None]
    qq = np.arange(128)[None, :]
    c["m_prev"] = np.where(qq <= kk, 0.0, NEG).astype(np.float32).astype(ml_dtypes.bfloat16)
    c["m_next"] = np.where(kk <= qq, 0.0, NEG).astype(np.float32).astype(ml_dtypes.bfloat16)
    pos = np.arange(T)
    row = (pos // 64).astype(np.float32)
    col = (pos % 64).astype(np.float32)
    freqs = (10000.0 ** (-np.arange(16, dtype=np.float32) / 16)).astype(np.float32)
    ang = np.concatenate([row[:, None] * freqs, col[:, None] * freqs], axis=-1).astype(np.float32)
    d = (m % 64) % 32
    sign = np.where((m % 64) < 32, -1.0, 1.0).astype(np.float32)
    c["cosT"] = np.ascontiguousarray(np.cos(ang)[:, d].T).astype(np.float32)
    c["sinT"] = np.ascontiguousarray((np.sin(ang)[:, d] * sign[None, :]).T).astype(np.float32)
    kf = np.zeros((128, 4, 8), np.float32)
    sidx = np.arange(8, dtype=np.float32)
    kf[:64, 0] = 7 - sidx
    kf[64:, 0] = sidx
    kf[:64, 1] = sidx + 1
    kf[64:, 1] = 8 - sidx
    kf[:64, 2] = -1 - sidx
    kf[64:, 2] = sidx - 8
    kf[:, 3, 0] = 1.0
    c["s5_kf"] = kf
    sg = np.ones((128, 1), np.float32)
    sg[64:] = -1.0
    c["s5_sgn"] = sg
    ss = (np.arange(128) // 16)[:, None]
    tt_ = (np.arange(128) // 16)[None, :]
    msk = np.zeros((128, 2, 128), np.float32)
    msk[:, 0, :] = (ss <= tt_)
    msk[:, 1, :] = (ss >= tt_)
    c["s5_msk"] = msk
    c["s5_iota"] = np.broadcast_to(np.arange(320, dtype=np.float32)[None, :], (128, 320)).copy()
    wd = np.zeros((128, 8, 240), np.float32)
    for g8 in range(8):
        for j in range(16):
            wd[16 * g8 + j, g8, 112 + j] = 1.0
    c["s5_wd"] = wd.astype(ml_dtypes.bfloat16)
    v4 = np.zeros((128, 4, 240), np.float32)
    for hh in range(2):
        for tq in range(4):
            for i in range(16):
                v4[64 * hh + 16 * tq + i, tq, 112 + i] = 1.0
    c["s5_v4"] = v4.astype(ml_dtypes.bfloat16)
    return c


def na_tables(rpb):
    variants, plan = na_plan()
    NV = len(variants)
    out = np.full((16, NV, 128, 128), NEG, np.float32)
    qc = np.arange(64)
    cstart = np.clip(qc - 8, 0, 48)
    kc = np.arange(64)
    col_ok = (kc[:, None] >= cstart[None, :]) & (kc[:, None] < cstart[None, :] + 16)
    dxi = np.clip(kc[:, None] - qc[None, :] + 15, 0, 30)
    for (dyo, o0, o1), vi in variants.items():
        for a in range(2):
            for b in range(2):
                dy = dyo + a - b
                ob = o0 if b == 0 else o1
                if not (ob <= dy < ob + 8):
                    continue
                g = rpb[:, dy + 7, :][:, dxi]
                blk = np.where(col_ok[None], g, np.float32(NEG))
                out[:, vi, 64 * a:64 * a + 64, 64 * b:64 * b + 64] = blk
    return out


def run(inputs, n_layers=4, final_norm=True, cores=8, trace=False):
    key = (n_layers, final_norm)
    if key not in _CACHE:
        _CACHE[key] = build_program(n_layers, final_norm)
    nc = _CACHE[key]
    consts = host_consts()
    shared = {}
    in_maps = []
    for b in range(cores):
        m = {}
        for name, v in inputs.items():
            v = np.asarray(v)
            if name in ("x", "ctx", "c"):
                m[name] = np.ascontiguousarray(v[b])
            elif name == "rpb2":
                if "natab" not in shared:
                    shared["natab"] = na_tables(v.astype(np.float32))
                continue
            else:
                m[name] = v
        m.update(consts)
        m.update(shared)
        in_maps.append(m)
    res = run_bass_kernel_spmd(nc, in_maps, core_ids=list(range(cores)), **({'trace': True} if trace else {}))
    if trace:
        print('EXEC_NS', res.exec_time_ns)
    return np.stack([r["out"] for r in res.results], axis=0)


def kernel(**inputs):
    return run(inputs).astype(np.float32)
```
